# Optimizing a Trainium2 kernel written in Bass

```python
import math
import jax
import jax.numpy as jnp
from jax import lax
import numpy as np

D_MODEL = 2048
BATCH = 8
SEQ = 2048
DEPTH = 1
DEC_BATCH = 16
DEC_SEQ = 32
PAST_LEN = 2048

CHUNK = 64
WINDOW = 128
WINDOW_CHUNKS = WINDOW // CHUNK
HEAD_DIM = 64
D_MIX = D_MODEL
D_ATTN = D_MIX // 2
D_LRU = D_MIX - D_ATTN
N_Q_HEADS = D_ATTN // HEAD_DIM
N_KV_HEADS = 4
GQA_GROUP = N_Q_HEADS // N_KV_HEADS
D_KV = N_KV_HEADS * HEAD_DIM
N_LRU_BLOCKS = 16
LRU_BLOCK = D_LRU // N_LRU_BLOCKS
CONV_WIDTH = 4
LRU_C = 8.0
D_IN = 2 * D_ATTN + 2 * D_KV + 2 * D_LRU
SPLIT_POINTS = (D_ATTN, D_ATTN + D_KV, D_ATTN + 2 * D_KV, 2 * D_ATTN + 2 * D_KV,
                2 * D_ATTN + 2 * D_KV + D_LRU)
DEEPNORM_ALPHA = (2.0 * DEPTH) ** 0.25
DEEPNORM_BETA = (8.0 * DEPTH) ** -0.25
LN_EPS = 1e-5
NEG_INF = -1e30

kernel_name = "hymba_swa_sink_alibi_rglru_deepnorm_step"


def _alibi_slopes():
    h = jnp.arange(1, N_Q_HEADS + 1, dtype=jnp.float32)
    return jnp.exp2(-8.0 * h / N_Q_HEADS).reshape(N_KV_HEADS, GQA_GROUP)


def _band_attention(q, k, v, qpos, kpos, sinks):
    qc = qpos // CHUNK
    kc = kpos // CHUNK
    valid = ((kpos[:, None, :] >= 0)
             & (kc[:, None, :] <= qc[:, :, None])
             & (kc[:, None, :] >= qc[:, :, None] - WINDOW_CHUNKS))
    dist = jnp.abs(qpos[:, :, None] - kpos[:, None, :]).astype(jnp.float32)
    slopes = _alibi_slopes()
    s = jnp.einsum('bnqhgd,bnkhd->bnhgqk', q.astype(jnp.float32), k.astype(jnp.float32))
    s = s * (HEAD_DIM ** -0.5) - slopes[None, None, :, :, None, None] * dist[None, :, None, None]
    s = jnp.where(valid[None, :, None, None], s, NEG_INF)
    sink = sinks.astype(jnp.float32).reshape(N_KV_HEADS, GQA_GROUP)[None, None, :, :, None, None]
    m = jnp.maximum(jnp.max(s, axis=-1, keepdims=True), sink)
    e = jnp.exp(s - m)
    p = e / (jnp.sum(e, axis=-1, keepdims=True) + jnp.exp(sink - m))
    return jnp.einsum('bnhgqk,bnkhd->bnqhgd', p, v.astype(jnp.float32))


def _attn_prompt(q, k, v, sinks):
    B, S = q.shape[0], q.shape[1]
    nC = S // CHUNK
    pad = WINDOW_CHUNKS * CHUNK
    qb = q.reshape(B, nC, CHUNK, N_KV_HEADS, GQA_GROUP, HEAD_DIM)

    def band(t):
        tp = jnp.pad(t, ((0, 0), (pad, 0), (0, 0), (0, 0))).reshape(B, nC + WINDOW_CHUNKS, CHUNK, N_KV_HEADS, HEAD_DIM)
        return jnp.concatenate([tp[:, j:j + nC] for j in range(WINDOW_CHUNKS + 1)], axis=2)

    kb, vb = band(k), band(v)
    qpos = jnp.arange(S, dtype=jnp.int32).reshape(nC, CHUNK)
    kpos = (jnp.arange(nC, dtype=jnp.int32)[:, None] * CHUNK - pad
            + jnp.arange((WINDOW_CHUNKS + 1) * CHUNK, dtype=jnp.int32)[None, :])
    o = _band_attention(qb, kb, vb, qpos, kpos, sinks)
    return o.reshape(B, S, D_ATTN)


def _attn_sample(q, k_all, v_all, sinks):
    B, T = q.shape[0], q.shape[1]
    n_keys = k_all.shape[1]
    qpos = (PAST_LEN + jnp.arange(T, dtype=jnp.int32))[None]
    kpos = (PAST_LEN + T - n_keys + jnp.arange(n_keys, dtype=jnp.int32))[None]
    o = _band_attention(q[:, None], k_all[:, None], v_all[:, None], qpos, kpos, sinks)
    return o.reshape(B, T, D_ATTN)


def _block_diag(x, w, b):
    B, T = x.shape[0], x.shape[1]
    xb = x.reshape(B, T, N_LRU_BLOCKS, LRU_BLOCK)
    return (jnp.einsum('btnd,nde->btne', xb, w) + b).reshape(B, T, D_LRU)


def _rglru(x, h0, w_ga, b_ga, w_gx, b_gx, lam):
    xf = x.astype(jnp.float32)
    r = jax.nn.sigmoid(_block_diag(xf, w_ga.astype(jnp.float32), b_ga.astype(jnp.float32)))
    i = jax.nn.sigmoid(_block_diag(xf, w_gx.astype(jnp.float32), b_gx.astype(jnp.float32)))
    log_a = -LRU_C * r * jax.nn.softplus(-lam.astype(jnp.float32))
    a = jnp.exp(log_a)
    mult = jnp.sqrt(jnp.maximum(-jnp.expm1(2.0 * log_a), 0.0))
    bterm = mult * (i * xf)
    bterm = bterm.at[:, 0].add(a[:, 0] * h0.astype(jnp.float32))

    def combine(lhs, rhs):
        a1, b1 = lhs
        a2, b2 = rhs
        return a1 * a2, a2 * b1 + b2

    _, h = lax.associative_scan(combine, (a, bterm), axis=1)
    return h


def _layer_norm(y, g, b):
    yf = y.astype(jnp.float32)
    mu = jnp.mean(yf, axis=-1, keepdims=True)
    var = jnp.mean(jnp.square(yf - mu), axis=-1, keepdims=True)
    return ((yf - mu) * lax.rsqrt(var + LN_EPS) * g.astype(jnp.float32) + b.astype(jnp.float32)).astype(y.dtype)


def _layer(x, k_cache, v_cache, conv_ctx, h0, w_in, b_in, conv_w, conv_b, w_ga, b_ga, w_gx, b_gx,
           lam, sinks, w_out, ln_g, ln_b):
    B, T = x.shape[0], x.shape[1]
    z = jnp.einsum('btd,de->bte', x, w_in) + b_in
    q, k, v, g_attn, x_lru, g_lru = jnp.split(z, SPLIT_POINTS, axis=-1)
    q = q.reshape(B, T, N_KV_HEADS, GQA_GROUP, HEAD_DIM)
    k = k.reshape(B, T, N_KV_HEADS, HEAD_DIM)
    v = v.reshape(B, T, N_KV_HEADS, HEAD_DIM)

    if k_cache is None:
        o_attn = _attn_prompt(q, k, v, sinks)
        k_all, v_all = k, v
    else:
        k_all = jnp.concatenate([k_cache.astype(k.dtype), k], axis=1)
        v_all = jnp.concatenate([v_cache.astype(v.dtype), v], axis=1)
        o_attn = _attn_sample(q, k_all, v_all, sinks)
    k_win = k_all[:, -WINDOW:]
    v_win = v_all[:, -WINDOW:]

    conv_in = jnp.concatenate([conv_ctx.astype(x_lru.dtype), x_lru], axis=1)
    xc = conv_b + sum(conv_in[:, j:j + T] * conv_w[j] for j in range(CONV_WIDTH))
    h = _rglru(xc, h0, w_ga, b_ga, w_gx, b_gx, lam)
    conv_new = conv_in[:, -(CONV_WIDTH - 1):]
    h_new = h[:, -1].astype(x.dtype)

    u = jnp.concatenate([o_attn.astype(x.dtype) * jax.nn.silu(g_attn),
                         h.astype(x.dtype) * jax.nn.silu(g_lru)], axis=-1)
    y = _layer_norm(DEEPNORM_ALPHA * x + jnp.einsum('bte,ed->btd', u, w_out), ln_g, ln_b)
    return y, k_win, v_win, conv_new, h_new


def setup_inputs(seed: int = 0) -> dict:
    key = jax.random.key(seed)
    ks = jax.random.split(key, 20)
    kv_buf = min(WINDOW, PAST_LEN)
    f32 = jnp.float32
    a0 = jax.random.uniform(ks[13], (DEPTH, D_LRU), f32, 0.9, 0.999)
    return {
        "x_prompt": jax.random.normal(ks[0], (BATCH, SEQ, D_MODEL), f32),
        "x_sample": jax.random.normal(ks[1], (DEC_BATCH, DEC_SEQ, D_MODEL), f32),
        "cache_k": jax.random.normal(ks[2], (DEPTH, DEC_BATCH, kv_buf, N_KV_HEADS, HEAD_DIM), f32),
        "cache_v": jax.random.normal(ks[3], (DEPTH, DEC_BATCH, kv_buf, N_KV_HEADS, HEAD_DIM), f32),
        "state_conv": jax.random.normal(ks[4], (DEPTH, DEC_BATCH, CONV_WIDTH - 1, D_LRU), f32),
        "state_h": jax.random.normal(ks[5], (DEPTH, DEC_BATCH, D_LRU), f32),
        "w_in": jax.random.normal(ks[6], (DEPTH, D_MODEL, D_IN), f32) * D_MODEL ** -0.5,
        "b_in": jax.random.normal(ks[7], (DEPTH, D_IN), f32) * 0.02,
        "conv_w": jax.random.normal(ks[8], (DEPTH, CONV_WIDTH, D_LRU), f32) * CONV_WIDTH ** -0.5,
        "conv_b": jax.random.normal(ks[9], (DEPTH, D_LRU), f32) * 0.02,
        "w_gate_a": jax.random.normal(ks[10], (DEPTH, N_LRU_BLOCKS, LRU_BLOCK, LRU_BLOCK), f32) * LRU_BLOCK ** -0.5,
        "b_gate_a": jax.random.normal(ks[11], (DEPTH, N_LRU_BLOCKS, LRU_BLOCK), f32) * 0.02,
        "w_gate_x": jax.random.normal(ks[12], (DEPTH, N_LRU_BLOCKS, LRU_BLOCK, LRU_BLOCK), f32) * LRU_BLOCK ** -0.5,
        "b_gate_x": jax.random.normal(ks[14], (DEPTH, N_LRU_BLOCKS, LRU_BLOCK), f32) * 0.02,
        "lru_lambda": jnp.log(a0) - jnp.log1p(-a0),
        "attn_sinks": jax.random.normal(ks[15], (DEPTH, N_Q_HEADS), f32),
        "w_out": jax.random.normal(ks[16], (DEPTH, D_MIX, D_MODEL), f32) * (D_MIX ** -0.5) * DEEPNORM_BETA,
        "ln_g": 1.0 + 0.02 * jax.random.normal(ks[17], (DEPTH, D_MODEL), f32),
        "ln_b": 0.02 * jax.random.normal(ks[18], (DEPTH, D_MODEL), f32),
    }


def reference(x_prompt, x_sample, cache_k, cache_v, state_conv, state_h, w_in, b_in, conv_w, conv_b,
              w_gate_a, b_gate_a, w_gate_x, b_gate_x, lru_lambda, attn_sinks, w_out, ln_g, ln_b):
    yp, ys = x_prompt, x_sample
    pk, pv, pc, ph = [], [], [], []
    sk, sv, sc, sh = [], [], [], []
    for l in range(DEPTH):
        w = (w_in[l], b_in[l], conv_w[l], conv_b[l], w_gate_a[l], b_gate_a[l], w_gate_x[l], b_gate_x[l],
             lru_lambda[l], attn_sinks[l], w_out[l], ln_g[l], ln_b[l])
        conv0 = jnp.zeros((yp.shape[0], CONV_WIDTH - 1, D_LRU), yp.dtype)
        h0 = jnp.zeros((yp.shape[0], D_LRU), jnp.float32)
        yp, k1, v1, c1, h1 = _layer(yp, None, None, conv0, h0, *w)
        ys, k2, v2, c2, h2 = _layer(ys, cache_k[l], cache_v[l], state_conv[l], state_h[l], *w)
        pk.append(k1); pv.append(v1); pc.append(c1); ph.append(h1)
        sk.append(k2); sv.append(v2); sc.append(c2); sh.append(h2)
    return (yp, ys, jnp.stack(pk), jnp.stack(pv), jnp.stack(pc), jnp.stack(ph),
            jnp.stack(sk), jnp.stack(sv), jnp.stack(sc), jnp.stack(sh))
```

```python
import numpy as np
from contextlib import ExitStack
import concourse.bass as bass
import concourse.mybir as mybir
from concourse.bass_utils import run_bass_kernel_spmd

F32 = mybir.dt.float32
BF16 = mybir.dt.bfloat16
AF = mybir.ActivationFunctionType
ALU = mybir.AluOpType

NCORES = 8
D = 2048
SEQ = 2048
NPP = 1024
NS = 32
NCOL = NPP + NS
NCH = 34
LW = NPP + 3 + NS
ALPHA = 2.0 ** 0.25
LN_EPS = 1e-5
GROUPS = [(0, 512), (512, 1024), (1024, 1056)]
LGROUPS = [(0, 512), (512, 1024), (1027, 1059)]


class Buf:
    __slots__ = ("w", "r", "name", "reg")

    def __init__(self, name=""):
        self.w = {}
        self.r = {}
        self.name = name
        self.reg = None


class Eng:
    def __init__(self, nc, h, name, in_order=False):
        self.h = h
        self.name = name
        self.sem = nc.alloc_semaphore(name="s_" + name)
        self.cnt = 0
        self.seen = {}
        self.in_order = in_order


class Fw:
    def __init__(self, nc, ndma=24):
        self.nc = nc
        self.pe = Eng(nc, nc.tensor, "pe", in_order=True)
        self.act = Eng(nc, nc.scalar, "act")
        self.dve = Eng(nc, nc.vector, "dve")
        self.pool = Eng(nc, nc.gpsimd, "pool")
        self.sp = Eng(nc, nc.sync, "sp")
        self.engs = [self.pe, self.act, self.dve, self.pool, self.sp]
        self.dsem = [nc.alloc_semaphore(name="d%d" % i) for i in range(ndma)]
        self.dcnt = [0] * ndma
        self.dnext = {"hw": 0, "sw": 0}
        self.half = ndma // 2
        self.limit = {}
        self.semobj = {}
        for e in self.engs:
            self.semobj[id(e.sem)] = e.sem
            self.limit[id(e.sem)] = 0
        for s in self.dsem:
            self.semobj[id(s)] = s
            self.limit[id(s)] = 0

    def _wait(self, eng, toks):
        for k, val in toks.items():
            if eng.in_order and k == id(eng.sem):
                continue
            if eng.seen.get(k, 0) >= val:
                continue
            assert val <= self.limit[k], "wait on a value never signalled (%s)" % eng.name
            eng.h.wait_ge(self.semobj[k], val)
            eng.seen[k] = val

    @staticmethod
    def _merge(dst, src):
        for k, v in src.items():
            if dst.get(k, 0) < v:
                dst[k] = v

    def _deps(self, rd, wr):
        toks = {}
        for b in rd:
            self._merge(toks, b.w)
        for b in wr:
            self._merge(toks, b.w)
            self._merge(toks, b.r)
        return toks

    def op(self, eng, fn, rd=(), wr=(), inc=True, note_r=()):
        self._wait(eng, self._deps(rd, wr))
        ins = fn(eng.h)
        k = id(eng.sem)
        if inc:
            eng.cnt += 1
            ins.then_inc(eng.sem, 1)
            self.limit[k] = eng.cnt
            tok = {k: eng.cnt}
        else:
            tok = {k: eng.cnt + 1}
        for b in rd:
            self._merge(b.r, tok)
        for b in note_r:
            self._merge(b.r, tok)
        for b in wr:
            b.w = dict(tok)
            b.r = {}
        return ins

    def dma(self, q, out, in_, rd=(), wr=()):
        self._wait(q, self._deps(rd, wr))
        kind = "sw" if q is self.pool else "hw"
        j = self.dnext[kind] + (self.half if kind == "sw" else 0)
        self.dnext[kind] = (self.dnext[kind] + 1) % self.half
        sem = self.dsem[j]
        k = id(sem)
        if self.dcnt[j] > 0:
            self._wait(q, {k: 16 * self.dcnt[j]})
        q.h.dma_start(out=out, in_=in_).then_inc(sem, 16)
        self.dcnt[j] += 1
        self.limit[k] = 16 * self.dcnt[j]
        tok = {k: 16 * self.dcnt[j]}
        for b in rd:
            self._merge(b.r, tok)
        for b in wr:
            b.w = dict(tok)
            b.r = {}

    def barrier(self):
        toks = {}
        for e in self.engs:
            if e.cnt > 0:
                toks[id(e.sem)] = e.cnt
        for j, s in enumerate(self.dsem):
            if self.dcnt[j] > 0:
                toks[id(s)] = 16 * self.dcnt[j]
        for e in self.engs:
            self._wait(e, toks)


def build():
    nc = bass.Bass("TRN2", target_bir_lowering=False)
    fw = Fw(nc)
    PE, ACT, DVE, POOL, SP = fw.pe, fw.act, fw.dve, fw.pool, fw.sp

    def din(name, shape):
        return nc.dram_tensor(name, shape, F32, kind="ExternalInput").ap()

    def dout(name, shape):
        return nc.dram_tensor(name, shape, F32, kind="ExternalOutput").ap()

    xp = din("xp", [SEQ, D]); xs = din("xs", [2 * NS, D])
    ck = din("ck", [2, 128, 256]); cv = din("cv", [2, 128, 256])
    sconv_d = din("sconv", [128, 48]); sh_d = din("sh", [128, 16])
    win = din("win", [NCH, 128, 2048]); wkv_d = din("wkv", [128, 8192]); wout_d = din("wout", [128, 32768])
    bt_d = din("bt", [128, NCH]); bkv_d = din("bkv", [128, 512])
    cw_d = din("cw", [128, 32]); cb_d = din("cb", [128, 8]); lam_d = din("lam", [128, 8])
    bga_d = din("bga", [128, 8]); bgx_d = din("bgx", [128, 8]); wg_d = din("wg", [128, 2048])
    lng_d = din("lng", [128, D]); lnb_d = din("lnb", [128, D]); sinks_d = din("sinks", [128, 16])
    abp_d = din("abp", [128, 4096]); abs_d = din("abs", [128, 1024]); ident_d = din("ident", [128, 128])

    yp = dout("yp", [SEQ, D]); ys = dout("ys", [2 * NS, D])
    kwp = dout("kwp", [128, 256]); vwp = dout("vwp", [128, 256])
    cvp = dout("cvp", [3, 1024]); hp = dout("hp", [1, 1024])
    kws = dout("kws", [2, 128, 256]); vws = dout("vws", [2, 128, 256])
    cvs = dout("cvs", [2, 3, 1024]); hs = dout("hs", [2, 1024])

    uniq = {"n": 0}

    def sb(name, shape, dt, side=None):
        uniq["n"] += 1
        return nc.sbuf_tensor("s%d_%s" % (uniq["n"], name), shape, dt, side=side)

    def ps(name, shape, dt):
        uniq["n"] += 1
        return nc.psum_tensor("p%d_%s" % (uniq["n"], name), shape, dt)

    with ExitStack() as _st1:
        uT = _st1.enter_context(sb("uT", [128, 16, NCOL + NS], BF16))
        identb = _st1.enter_context(sb("identb", [128, 128], BF16))
        identf = _st1.enter_context(sb("identf", [128, 128], F32))
        bt = _st1.enter_context(sb("bt", [128, NCH], F32))
        bkv = _st1.enter_context(sb("bkv", [128, 512], F32))
        cw = _st1.enter_context(sb("cw", [128, 8, 4], F32))
        cb = _st1.enter_context(sb("cb", [128, 8], F32))
        lamc = _st1.enter_context(sb("lamc", [128, 8], F32))
        bga = _st1.enter_context(sb("bga", [128, 8], F32))
        bgx = _st1.enter_context(sb("bgx", [128, 8], F32))
        wg = _st1.enter_context(sb("wg", [128, 2, 8, 128], BF16))
        Ep = _st1.enter_context(sb("Ep", [128, 4, 2, 4, 128], BF16))
        Es = _st1.enter_context(sb("Es", [128, 4, 2, 4, 32], BF16))
        esink = _st1.enter_context(sb("esink", [128, 16], F32))
        KTh = _st1.enter_context(sb("KTh", [128, 2, 128], BF16))
        V1h = _st1.enter_context(sb("V1h", [128, 4, 65], BF16))
        convst = _st1.enter_context(sb("convst", [128, 8, 3], F32))
        hst = _st1.enter_context(sb("hst", [128, 8], F32))
        outst = _st1.enter_context(sb("outst", [128, 8, 12], F32))
        sconv = _st1.enter_context(sb("sconv", [128, 2, 8, 3], F32))
        sh_sb = _st1.enter_context(sb("sh", [128, 2, 8], F32))
        epsb = _st1.enter_context(sb("epsb", [128, 1], F32))
        qtr = _st1.enter_context(sb("qtr", [128, 1], F32))
        scr = _st1.enter_context(sb("scr", [128, 2], F32))
        bth = _st1.enter_context(sb("bth", [128, NCH], F32))
        wb = _st1.enter_context(sb("wb", [128, 3, 2048], BF16))
        XW = _st1.enter_context(sb("XW", [128, 16 * NCOL], BF16))
        xT = XW[:, :].rearrange("p (k n) -> p k n", k=16)
        woA = XW[:, 0:8 * D].rearrange("p (k n) -> p k n", k=8)

        PS = _st1.enter_context(ps("PS", [128, 8, 512], F32))
        xst = _st1.enter_context(sb("xst", [128, 3, 2048], BF16))
        po = PS[:, 0:2, :]
        ptr = PS[:, 2, :].bitcast(BF16)
        pacc = PS[:, 3:6, :]
        pst = PS[:, 6:8, :]
        ptx = [PS[:, 6, :].bitcast(BF16), PS[:, 7, :].bitcast(BF16)]
        py = PS

        reg = {"all": []}

        def PB():
            return Buf()

        def B(t):
            b_ = Buf()
            ml = nc.lookup_mloc(t)
            b_.reg = (int(ml.addr), int(ml.addr) + int(ml.dims[1]))
            for o in reg["all"]:
                if o.reg is None or (o.reg[0] < b_.reg[1] and b_.reg[0] < o.reg[1]):
                    fw._merge(b_.w, o.w)
                    fw._merge(b_.w, o.r)
            reg["all"].append(b_)
            return b_

        b_ps = [PB() for _ in range(8)]
        b_po, b_ptr, b_pacc, b_ptx = b_ps[0:2], b_ps[2], b_ps[3:6], b_ps[6:8]
        b_pst2 = b_ps[6:8]
        b_py = b_ps
        b_xT = [PB() for _ in range(9)]
        b_xst = [PB(), PB(), PB()]
        b_wb = [PB() for _ in range(3)]
        b_uT = [[PB() for _ in range(9)] for _ in range(16)]
        b_const = PB()
        b_ident = PB()
        b_bias = PB()
        b_scr = PB()
        b_KTh, b_V1h = PB(), PB()
        b_convst = [PB() for _ in range(8)]
        b_hst = [PB() for _ in range(8)]
        b_outst = [PB() for _ in range(8)]

        init_bufs = []

        def IB():
            b_ = Buf()
            init_bufs.append(b_)
            return b_

        st_init = ExitStack()
        stg = st_init.enter_context(sb("stg", [128, 4096], F32, side="right"))
        stg2 = st_init.enter_context(sb("stg2", [128, 1024], F32, side="right"))
        stg3 = st_init.enter_context(sb("stg3", [128, 16], F32, side="right"))
        lamt = st_init.enter_context(sb("lamt", [128, 16], F32, side="right"))
        b_bga, b_bgx = IB(), IB()
        for dst, src in ((identf, ident_d), (bkv, bkv_d), (cb, cb_d)):
            fw.dma(SP, dst[:], src, wr=[IB()])
        fw.dma(SP, bt[:], bt_d, wr=[b_bias])
        fw.op(DVE, lambda e: e.tensor_scalar(out=bth[:], in0=bt[:], scalar1=0.5, scalar2=None, op0=ALU.mult),
              rd=[b_bias], wr=[IB()])
        fw.dma(SP, bga[:], bga_d, wr=[b_bga])
        fw.dma(SP, bgx[:], bgx_d, wr=[b_bgx])
        fw.dma(SP, cw[:].rearrange("p c t -> p (c t)"), cw_d, wr=[IB()])
        fw.dma(SP, sconv[:].rearrange("p s c t -> p (s c t)"), sconv_d, wr=[IB()])
        fw.dma(SP, sh_sb[:].rearrange("p s c -> p (s c)"), sh_d, wr=[IB()])
        fw.dma(POOL, identb[:], ident_d, wr=[b_ident])
        fw.op(DVE, lambda e: e.tensor_scalar(out=bga[:], in0=bga[:], scalar1=0.5, scalar2=None, op0=ALU.mult),
              rd=[b_bga], wr=[b_bga])
        fw.op(DVE, lambda e: e.tensor_scalar(out=bgx[:], in0=bgx[:], scalar1=0.5, scalar2=None, op0=ALU.mult),
              rd=[b_bgx], wr=[b_bgx])
        fw.op(DVE, lambda e: e.memset(qtr[:], 0.25), wr=[IB()])
        fw.op(DVE, lambda e: e.memset(scr[:], 0.0), wr=[IB()])
        fw.op(DVE, lambda e: e.memset(convst[:], 0.0), wr=b_convst)
        fw.op(DVE, lambda e: e.memset(epsb[:], LN_EPS), wr=[IB()])
        fw.op(DVE, lambda e: e.memset(hst[:], 0.0), wr=b_hst)
        fw.op(DVE, lambda e: e.memset(V1h[:], 1.0), wr=[b_V1h])
        fw.op(DVE, lambda e: e.memset(KTh[:], 0.0), wr=[b_KTh])

        def late_init():
            b_stg, b_stg2, b_stg3, b_lam = IB(), IB(), IB(), IB()
            fw.dma(POOL, wg[:].rearrange("p a c e -> p (a c e)"), wg_d, wr=[IB()])
            fw.dma(SP, stg[:], abp_d, wr=[b_stg])
            fw.dma(SP, stg2[:], abs_d, wr=[b_stg2])
            fw.dma(SP, stg3[:], sinks_d, wr=[b_stg3])
            fw.dma(SP, lamt[:, 0:8], lam_d, wr=[b_lam])
            fw.op(ACT, lambda e: e.activation(out=Ep[:].rearrange("p a b c d -> p (a b c d)"), in_=stg[:], func=AF.Exp),
                  rd=[b_stg], wr=[IB()])
            fw.op(ACT, lambda e: e.activation(out=Es[:].rearrange("p a b c d -> p (a b c d)"), in_=stg2[:], func=AF.Exp),
                  rd=[b_stg2], wr=[IB()])
            fw.op(ACT, lambda e: e.activation(out=esink[:], in_=stg3[:], func=AF.Exp), rd=[b_stg3], wr=[IB()])
            fw.op(ACT, lambda e: e.activation(out=lamt[:, 8:16], in_=lamt[:, 0:8], func=AF.Exp, scale=-1.0),
                  rd=[b_lam], wr=[b_lam])
            fw.op(ACT, lambda e: e.activation(out=lamt[:, 0:8], in_=lamt[:, 8:16], func=AF.Ln, bias=1.0),
                  rd=[b_lam], wr=[b_lam])
            fw.op(DVE, lambda e: e.tensor_scalar(out=lamc[:], in0=lamt[:, 0:8], scalar1=-4.0, scalar2=None,
                                                 op0=ALU.mult), rd=[b_lam], wr=[IB()])
            for b_ in init_bufs:
                fw._merge(b_const.w, b_.w)
                fw._merge(b_const.w, b_.r)
            reg["all"].extend(init_bufs)
            st_init.close()

        morder = list(range(10))
        for c in range(8):
            morder += [26 + c, 18 + c]
        morder += list(range(10, 18))
        wstate = [{"next": 0, "pre": 0}, {"next": 0, "pre": 0}]
        xdone = set()

        def next_wload(pas_, pre=False):
            st_ = wstate[pas_]
            if not pre and st_["pre"] > 0:
                st_["pre"] -= 1
                return
            i = st_["next"]
            if i < NCH:
                fw.dma(POOL, wb[:, i % 3, :], win[morder[i]], wr=[b_wb[i % 3]])
                st_["next"] = i + 1
                if pre:
                    st_["pre"] += 1

        def xload(pas_, t):
            if (pas_, "l", t) in xdone:
                return
            xdone.add((pas_, "l", t))
            nr = 128 if t < 8 else NS
            src = xp[pas_ * NPP + t * 128: pas_ * NPP + (t + 1) * 128, :] if t < 8 else xs[pas_ * NS:(pas_ + 1) * NS, :]
            fw.dma(POOL, xst[0:nr, t % 3, :], src, wr=[b_xst[t % 3]])

        def xT_tile(pas_, t, pbanks=(6, 7), act_only=False):
            if (pas_, "t", t) in xdone:
                return
            xdone.add((pas_, "t", t))
            nr = 128 if t < 8 else NS
            c0 = t * 128
            for half in range(2):
                bank = b_ps[pbanks[half]]
                pv = PS[:, pbanks[half], :].bitcast(BF16)
                for kk in range(8):
                    k = half * 8 + kk
                    fw.op(PE, lambda e, k=k, kk=kk, pv=pv: e.transpose(
                        pv[:, kk * 128: kk * 128 + nr], xst[0:nr, t % 3, k * 128:(k + 1) * 128],
                        identb[0:nr, 0:nr]),
                        rd=[b_xst[t % 3], b_ident], wr=[bank], inc=(kk == 7))
                src_ap = pv.rearrange("p (k c) -> p k c", k=8)[:, :, 0:nr]
                dst_ap = xT[:, half * 8:(half + 1) * 8, c0:c0 + nr]
                if half == 0 or act_only:
                    fw.op(ACT, lambda e, s=src_ap, d=dst_ap: e.copy(out=d, in_=s), rd=[bank], wr=[b_xT[t]])
                else:
                    fw.op(DVE, lambda e, s=src_ap, d=dst_ap: e.tensor_copy(out=d, in_=s), rd=[bank],
                          wr=[b_xT[t]])
            if t + 3 < 9:
                xload(pas_, t + 3)
            if pas_ == 0 and t < 3:
                next_wload(0)

        for pas in range(2):
            tok0 = pas * NPP
            st_wob = ExitStack()
            b_woB = [None] * 8
            wob_state = {"next": 0}
            b_woA = [PB() for _ in range(4)]
            with ExitStack() as _st3:
                QT = _st3.enter_context(sb("QT", [128, 8, NCOL], BF16))
                KT = _st3.enter_context(sb("KT", [128, 2, NPP + 128 + NS], BF16))
                V1 = _st3.enter_context(sb("V1", [128, 10, 4, 65], BF16))
                b_QT = [[B(QT) for _ in range(3)] for _ in range(8)]
                b_KT = [[B(KT) for _ in range(10)] for _ in range(2)]
                b_V1 = [B(V1) for _ in range(10)]
                fw.op(DVE, lambda e: e.memset(V1[:], 1.0), wr=b_V1)

                if True:
                    def kv_tokmajor(t):
                        nr = 128 if t < 8 else NS
                        c0 = t * 128
                        need_k = (t == 8) or (pas == 1 and t == 7)
                        lo = 0 if need_k else 256
                        bank = t % 2
                        for k in range(16):
                            fw.op(PE, lambda e, k=k: e.matmul(po[0:nr, bank, lo:512], lhsT=xT[:, k, c0:c0 + nr],
                                                               rhs=wkv[:, k, lo:512], start=(k == 0), stop=(k == 15)),
                                  rd=[b_xT[t], b_wkv], wr=[b_po[bank]], inc=(k == 15))
                        vt = t if t < 8 else 9
                        if need_k:
                            fw.op(DVE, lambda e: e.tensor_tensor(out=kvst[0:nr, bank, :], in0=po[0:nr, bank, :],
                                                                 in1=bkv[0:nr, :], op=ALU.add),
                                  rd=[b_po[bank], b_const], wr=[b_kvst[bank]])
                            fw.op(ACT, lambda e: e.copy(out=V1[0:nr, vt, :, 0:64],
                                                        in_=kvst[0:nr, bank, 256:512].rearrange("p (h d) -> p h d", h=4)),
                                  rd=[b_kvst[bank]], wr=[b_V1[vt]])
                            if t == 8:
                                fw.dma(SP, kws[pas, 96:128, :], kvst[0:NS, bank, 0:256], rd=[b_kvst[bank]])
                                fw.dma(SP, vws[pas, 96:128, :], kvst[0:NS, bank, 256:512], rd=[b_kvst[bank]])
                            else:
                                fw.dma(SP, kwp, kvst[:, bank, 0:256], rd=[b_kvst[bank]])
                                fw.dma(SP, vwp, kvst[:, bank, 256:512], rd=[b_kvst[bank]])
                        else:
                            fw.op(DVE, lambda e: e.tensor_tensor(
                                out=V1[0:nr, vt, :, 0:64],
                                in0=po[0:nr, bank, 256:512].rearrange("p (h d) -> p h d", h=4),
                                in1=bkv[0:nr, 256:512].rearrange("p (h d) -> p h d", h=4), op=ALU.add),
                                rd=[b_po[bank], b_const], wr=[b_V1[vt]])

                    for t_ in range(3):
                        xload(pas, t_)
                    if pas == 1:
                        for _ in range(3):
                            next_wload(pas)

                    def kv_phase():
                        for t in range(9):
                            kv_tokmajor(t)
                        for c in range(2):
                            fw.op(PE, lambda e, c=c: e.transpose(ptr[:, c * 128:(c + 1) * 128],
                                                                 ckst[:, c * 128:(c + 1) * 128], identb[:]),
                                  rd=[b_ck, b_const], wr=[b_ptr], inc=(c == 1))
                        fw.op(DVE, lambda e: e.tensor_copy(out=KT[:, :, NPP:NPP + 128],
                                                           in_=ptr[:, 0:256].rearrange("p (c t) -> p c t", c=2)),
                              rd=[b_ptr], wr=[b_KT[0][8], b_KT[1][8]])


                with ExitStack() as _st6:
                    tA = _st6.enter_context(sb("tA", [128, 2, 512], F32))
                    b_tA = [B(tA), B(tA)]
                    st_1a = ExitStack()
                    wkv = st_1a.enter_context(sb("wkv", [128, 16, 512], BF16))
                    kvst = st_1a.enter_context(sb("kvst", [128, 2, 512], F32))
                    ckst = st_1a.enter_context(sb("ckst", [128, 256], BF16))
                    b_wkv, b_kvst, b_ck = B(wkv), [B(kvst), B(kvst)], B(ckst)
                    st_lru = ExitStack()
                    accn = {"i": 0, "e": 0, "banks": list(range(8)), "last": [0] * 8}

                    def inproj_group(slot, gi, evac, mid=None):
                        lo, hi = GROUPS[gi]
                        bi = accn["banks"][accn["i"] % len(accn["banks"])]
                        accn["i"] += 1
                        accn["last"][bi] = accn["i"]
                        t_lo, t_hi = lo // 128, (hi + 127) // 128
                        for k in range(16):
                            fw.op(PE, lambda e, k=k: e.matmul(PS[:, bi, 0:hi - lo], lhsT=wb[:, slot, k * 128:(k + 1) * 128],
                                                               rhs=xT[:, k, lo:hi], start=(k == 0), stop=(k == 15)),
                                  rd=[b_wb[slot]] + b_xT[t_lo:t_hi], wr=[b_ps[bi]], inc=(k == 15))
                            if k == 7 and mid is not None:
                                mid()
                        evac(PS[:, bi, 0:hi - lo], b_ps[bi], lo, hi, gi)

                    def tiles_of(lo, hi):
                        return range(lo // 128, (hi + 127) // 128)

                    def attn_unit(t, kv):
                        nq = 128 if t < 8 else NS
                        q0 = t * 128
                        j, half = kv // 2, kv % 2
                        pb = slice(half * 64, half * 64 + 64)
                        xb = kv % 2
                        p3 = (t * 4 + kv) % 3
                        pk = kv % 2
                        tiles = []
                        if t < 8:
                            if t == 0:
                                if pas == 1:
                                    tiles.append((KTh[pb, j, :], b_KTh, V1h[:, kv, :], b_V1h, 128, Ep[:, kv, 0]))
                            else:
                                tiles.append((KT[pb, j, (t - 1) * 128:t * 128], b_KT[j][t - 1],
                                              V1[:, t - 1, kv, :], b_V1[t - 1], 128, Ep[:, kv, 0]))
                            tiles.append((KT[pb, j, t * 128:(t + 1) * 128], b_KT[j][t], V1[:, t, kv, :], b_V1[t],
                                          128, Ep[:, kv, 1]))
                        else:
                            tiles.append((KT[pb, j, NPP:NPP + 128], b_KT[j][8], V1[:, 8, kv, :], b_V1[8], 128,
                                          Es[:, kv, 0]))
                            tiles.append((KT[pb, j, NPP + 128:NPP + 128 + NS], b_KT[j][9], V1[0:NS, 9, kv, :],
                                          b_V1[9], NS, Es[0:NS, kv, 1]))
                        nt = len(tiles)

                        def st():
                            for ti, (kt_ap, kt_buf, v_ap, v_buf, nk, e_ap) in enumerate(tiles):
                                fw.op(PE, lambda e: e.matmul(
                                    pst[0:nk, ti, 0:4 * nq].rearrange("p (g q) -> p g q", g=4),
                                    lhsT=kt_ap, rhs=QT[pb, j * 4:(j + 1) * 4, q0:q0 + nq], start=True, stop=True),
                                    rd=[kt_buf] + [b_QT[j * 4 + g][min(t // 4, 2)] for g in range(4)],
                                    wr=[b_pst2[ti]], inc=(ti == nt - 1))
                            for ti, (kt_ap, kt_buf, v_ap, v_buf, nk, e_ap) in enumerate(tiles):
                                fw.op(ACT, lambda e: e.activation(
                                    out=ex[0:nk, xb, ti, 0:4 * nq], in_=pst[0:nk, ti, 0:4 * nq], func=AF.Exp, scale=0.125),
                                    rd=[b_pst2[ti]], wr=[b_ex[xb][ti]])
                                fw.op(DVE, lambda e: e.tensor_tensor(
                                    out=PT[0:nk, p3, ti, 0:4 * nq], in0=ex[0:nk, xb, ti, 0:4 * nq],
                                    in1=e_ap.rearrange("p g q -> p (g q)"), op=ALU.mult),
                                    rd=[b_ex[xb][ti], b_const], wr=[b_PT[p3][ti]])

                        def pvn():
                            for g in range(4):
                                for ti, (kt_ap, kt_buf, v_ap, v_buf, nk, e_ap) in enumerate(tiles):
                                    last = (g == 3 and ti == nt - 1)
                                    fw.op(PE, lambda e: e.matmul(
                                        po[0:nq, pk, g * 65:(g + 1) * 65], lhsT=PT[0:nk, p3, ti, g * nq:(g + 1) * nq],
                                        rhs=v_ap, start=(ti == 0), stop=(ti == nt - 1)),
                                        rd=[b_PT[p3][ti], v_buf], wr=[b_po[pk]], inc=last)
                            pov = po[0:nq, pk, 0:260].rearrange("p (g d) -> p g d", g=4)
                            fw.op(DVE, lambda e: e.tensor_tensor(
                                out=den[0:nq, pk, 0:4], in0=pov[:, :, 64], in1=esink[0:nq, kv * 4:(kv + 1) * 4], op=ALU.add),
                                rd=[b_po[pk], b_const], wr=[b_den[pk]])
                            fw.op(DVE, lambda e: e.reciprocal(out=den[0:nq, pk, 4:8], in_=den[0:nq, pk, 0:4]),
                                  rd=[b_den[pk]], wr=[b_den[pk]])
                            fw.op(DVE, lambda e: e.tensor_tensor(
                                out=osb[0:nq, t, kv * 256:(kv + 1) * 256].rearrange("p (g d) -> p g d", g=4),
                                in0=pov[:, :, 0:64], in1=den[0:nq, pk, 4:8].unsqueeze(2).to_broadcast([nq, 4, 64]),
                                op=ALU.mult), rd=[b_po[pk], b_den[pk]], wr=[b_osb[t]])
                        return st, pvn

                    def attn_tr(t):
                        nq = 128 if t < 8 else NS
                        q0 = t * 128 if t < 8 else NPP + NS * pas
                        bk = 2 + (t % 2)
                        ptb = PS[:, bk, :].bitcast(BF16)
                        for c in range(8):
                            fw.op(PE, lambda e, c=c: e.transpose(ptb[:, c * 128:c * 128 + nq],
                                                                 osb[0:nq, t, c * 128:(c + 1) * 128],
                                                                 identb[0:nq, 0:nq]),
                                  rd=[b_osb[t], b_const], wr=[b_ps[bk]], inc=(c == 7))
                        fw.op(DVE, lambda e: e.tensor_tensor(
                            out=uT[:, 0:8, q0:q0 + nq], in0=ptb.rearrange("p (c q) -> p c q", c=8)[:, :, 0:nq],
                            in1=uT[:, 0:8, q0:q0 + nq], op=ALU.mult),
                            rd=[b_ps[bk]] + [b_uT[c][t] for c in range(8)], wr=[b_uT[c][t] for c in range(8)])

                    def attention():
                        units = [(t, kv) for t in range(9) for kv in range(4)]
                        fns = [attn_unit(t, kv) for (t, kv) in units]
                        for i in range(len(units) + 2):
                            if i < len(units):
                                fns[i][0]()
                            if i >= 2:
                                fns[i - 2][1]()
                            yield

                    def lru_chain(c):
                        xb_ = c % 2
                        so = 4 + 4 * pas
                        rrc, b_rrc = rr2[:, c % 2, :], b_rr2[c % 2]
                        for gate in range(2):
                            dst, dbuf, bia = (rrc, b_rrc, bga) if gate == 0 else (ii, b_ii, bgx)
                            for (lo, hi) in LGROUPS:
                                bi = accn["banks"][accn["i"] % len(accn["banks"])]
                                accn["i"] += 1
                                accn["last"][bi] = accn["i"]
                                fw.op(PE, lambda e: e.matmul(PS[:, bi, 0:hi - lo], lhsT=wg[:, gate, c, :],
                                                             rhs=xcb[:, xb_, lo:hi], start=True, stop=True),
                                      rd=[b_xcb[xb_], b_const], wr=[b_ps[bi]])
                                fw.op(ACT, lambda e: e.activation(out=dst[:, lo:hi], in_=PS[:, bi, 0:hi - lo],
                                                                  func=AF.Tanh, bias=bia[:, c:c + 1], scale=0.5),
                                      rd=[b_ps[bi], b_const], wr=[dbuf])
                            if gate == 0:
                                fw.op(ACT, lambda e: e.activation(out=aa[:], in_=rrc, func=AF.Exp, scale=lamc[:, c:c + 1],
                                                                  bias=lamc[:, c:c + 1]),
                                      rd=[b_rrc, b_const], wr=[b_aa])
                            yield
                        fw.op(ACT, lambda e: e.activation(out=rrc[:, 0:NCOL], in_=gbuf[:, xb_, :], func=AF.Tanh, scale=0.5),
                              rd=[b_gbuf[xb_], b_rrc], wr=[b_rrc])
                        fw.op(ACT, lambda e: e.activation(out=mm[:], in_=aa[:], func=AF.Square), rd=[b_aa], wr=[b_mm])
                        fw.op(ACT, lambda e: e.activation(out=mm[:], in_=mm[:], func=AF.Sqrt, scale=-0.25, bias=qtr[:, 0:1]),
                              rd=[b_mm, b_const], wr=[b_mm])
                        fw.op(ACT, lambda e: e.activation(out=scr[:, 0:1], in_=scr[:, 1:2], func=AF.Tanh), rd=[b_const],
                              wr=[b_scr])
                        yield
                        fw.op(DVE, lambda e: e.scalar_tensor_tensor(out=ii[:], in0=ii[:], scalar=1.0, in1=xc[:, xb_, :],
                                                                    op0=ALU.add, op1=ALU.mult),
                              rd=[b_ii, b_xc[xb_]], wr=[b_ii])
                        fw.op(DVE, lambda e: e.tensor_tensor(out=ii[:], in0=ii[:], in1=mm[:], op=ALU.mult),
                              rd=[b_ii, b_mm], wr=[b_ii])
                        fw.op(DVE, lambda e: e.scalar_tensor_tensor(out=rrc[:, 0:NCOL], in0=rrc[:, 0:NCOL], scalar=1.0,
                                                                    in1=gbuf[:, xb_, :], op0=ALU.add, op1=ALU.mult),
                              rd=[b_rrc, b_gbuf[xb_]], wr=[b_rrc])
                        yield
                        fw.op(DVE, lambda e: e.tensor_tensor_scan(out=mm[:, 0:NPP], data0=aa[:, 0:NPP], data1=ii[:, 0:NPP],
                                                                  initial=hst[:, c:c + 1], op0=ALU.mult, op1=ALU.add),
                              rd=[b_aa, b_ii, b_hst[c]], wr=[b_mm])
                        fw.op(DVE, lambda e: e.tensor_tensor_scan(out=mm[:, NPP + 3:LW], data0=aa[:, NPP + 3:LW],
                                                                  data1=ii[:, NPP + 3:LW], initial=sh_sb[:, pas, c:c + 1],
                                                                  op0=ALU.mult, op1=ALU.add),
                              rd=[b_aa, b_ii, b_const], wr=[b_mm])
                        yield
                        fw.op(DVE, lambda e: e.tensor_copy(out=hst[:, c:c + 1], in_=mm[:, NPP - 1:NPP]), rd=[b_mm],
                              wr=[b_hst[c]])
                        if pas == 1:
                            fw.op(DVE, lambda e: e.tensor_copy(out=outst[:, c, 0:1], in_=mm[:, NPP - 1:NPP]),
                                  rd=[b_mm], wr=[b_outst[c]])
                        fw.op(DVE, lambda e: e.tensor_copy(out=outst[:, c, so:so + 1], in_=mm[:, LW - 1:LW]),
                              rd=[b_mm], wr=[b_outst[c]])
                        fw.op(DVE, lambda e: e.scalar_tensor_tensor(out=uT[:, 8 + c, 0:NPP], in0=mm[:, 0:NPP], scalar=0.5,
                                                                    in1=rrc[:, 0:NPP], op0=ALU.mult, op1=ALU.mult),
                              rd=[b_mm, b_rrc], wr=b_uT[8 + c][0:8])
                        fw.op(DVE, lambda e: e.scalar_tensor_tensor(out=uT[:, 8 + c, NPP + NS * pas:NCOL + NS * pas], in0=mm[:, NPP + 3:LW],
                                                                    scalar=0.5, in1=rrc[:, NPP:NCOL], op0=ALU.mult,
                                                                    op1=ALU.mult),
                              rd=[b_mm, b_rrc], wr=[b_uT[8 + c][8]])
                        yield

                    def conv(c):
                        xb_ = c % 2
                        so = 4 + 4 * pas
                        xlv = xl[:, xb_, :]
                        fw.op(DVE, lambda e: e.tensor_copy(out=convst[:, c, :], in_=xlv[:, NPP:NPP + 3]),
                              rd=[b_xl[xb_], b_xlp[xb_]], wr=[b_convst[c]])
                        if pas == 1:
                            fw.op(DVE, lambda e: e.tensor_copy(out=outst[:, c, 1:4], in_=xlv[:, NPP:NPP + 3]),
                                  rd=[b_xl[xb_], b_xlp[xb_]], wr=[b_outst[c]])
                        fw.op(DVE, lambda e: e.tensor_copy(out=outst[:, c, so + 1:so + 4], in_=xlv[:, LW:LW + 3]),
                              rd=[b_xl[xb_], b_xlp[xb_]], wr=[b_outst[c]])
                        fw.op(DVE, lambda e: e.tensor_scalar(out=xc[:, xb_, :], in0=xlv[:, 0:LW], scalar1=cw[:, c, 0:1],
                                                             scalar2=cb[:, c:c + 1], op0=ALU.mult, op1=ALU.add),
                              rd=[b_xl[xb_], b_xlp[xb_], b_const], wr=[b_xc[xb_]])
                        for tap in range(1, 4):
                            fw.op(DVE, lambda e, tap=tap: e.scalar_tensor_tensor(
                                out=xc[:, xb_, :], in0=xlv[:, tap:tap + LW], scalar=cw[:, c, tap:tap + 1], in1=xc[:, xb_, :],
                                op0=ALU.mult, op1=ALU.add), rd=[b_xl[xb_], b_xlp[xb_], b_xc[xb_], b_const], wr=[b_xc[xb_]])
                        fw.op(ACT, lambda e: e.copy(out=xcb[:, xb_, :], in_=xc[:, xb_, :]), rd=[b_xc[xb_]],
                              wr=[b_xcb[xb_]])

                    def make_evac(m):
                        bias = bt[:, m:m + 1]
                        if m < 2:
                            def evac(pa, pb_, lo, hi, gi):
                                dlo = lo if gi < 2 else NPP + 128
                                bufs = [b_KT[m][t] for t in tiles_of(lo, hi)] if gi < 2 else [b_KT[m][9]]
                                fw.op(ACT, lambda e: e.activation(out=KT[:, m, dlo:dlo + hi - lo], in_=pa, func=AF.Identity,
                                                                  bias=bias), rd=[pb_, b_bias], wr=bufs)
                        elif m < 10:
                            def evac(pa, pb_, lo, hi, gi):
                                fw.op(ACT, lambda e: e.activation(out=QT[:, m - 2, lo:hi], in_=pa, func=AF.Identity,
                                                                  bias=bias), rd=[pb_, b_bias], wr=[b_QT[m - 2][gi]])
                        elif m < 18:
                            def evac(pa, pb_, lo, hi, gi):
                                c = m - 10
                                w_ = hi - lo
                                tb = accn["e"] % 2
                                accn["e"] += 1
                                fw.op(ACT, lambda e: e.activation(out=tA[:, tb, 0:w_], in_=pa, func=AF.Tanh,
                                                                  bias=bth[:, m:m + 1], scale=0.5),
                                      rd=[pb_, b_const], wr=[b_tA[tb]])
                                fw.op(DVE, lambda e: e.tensor_scalar(out=tA[:, tb, 0:w_], in0=tA[:, tb, 0:w_], scalar1=0.5,
                                                                     scalar2=0.5, op0=ALU.mult, op1=ALU.add),
                                      rd=[b_tA[tb]], wr=[b_tA[tb]])
                                ulo = lo if gi < 2 else NPP + NS * pas
                                fw.op(DVE, lambda e: e.scalar_tensor_tensor(out=uT[:, c, ulo:ulo + hi - lo], in0=pa, scalar=bias,
                                                                            in1=tA[:, tb, 0:w_], op0=ALU.add, op1=ALU.mult),
                                      rd=[pb_, b_tA[tb], b_const], wr=[b_uT[c][t] for t in tiles_of(lo, hi)])
                        elif m < 26:
                            def evac(pa, pb_, lo, hi, gi):
                                c = m - 18
                                fw.op(ACT, lambda e: e.activation(out=gbuf[:, c % 2, lo:hi], in_=pa, func=AF.Identity,
                                                                  bias=bias), rd=[pb_, b_bias], wr=[b_gbuf[c % 2]])
                        else:
                            def evac(pa, pb_, lo, hi, gi):
                                c = m - 26
                                dlo = 3 + lo if gi < 2 else 3 + NPP + 3
                                fw.op(ACT, lambda e: e.activation(out=xl[:, c % 2, dlo:dlo + hi - lo], in_=pa,
                                                                  func=AF.Identity, bias=bias),
                                      rd=[pb_, b_bias], wr=[b_xl[c % 2]])
                        return evac

                    attn = None
                    chain = None
                    for t in range(4):
                        if pas == 1:
                            xT_tile(pas, t, pbanks=((0, 1) if t % 2 == 0 else (2, 3)))
                        else:
                            xT_tile(pas, t)
                    for t in (4, 5, 6):
                        inproj_group(t - 4, 0, make_evac(morder[t - 4]))
                        xT_tile(pas, t)
                    xT_tile(pas, 7)
                    for i3 in range(3):
                        inproj_group(i3, 1, make_evac(morder[i3]))
                    xT_tile(pas, 8)
                    for i3 in range(3):
                        inproj_group(i3, 2, make_evac(morder[i3]))
                        next_wload(pas)
                    for idx, m in enumerate(morder):
                        if idx < 3:
                            continue
                        slot = idx % 3
                        evac = make_evac(m)
                        if idx == 4 and pas == 0:
                            late_init()
                        if 5 <= idx <= 8:
                            q = idx - 5
                            b_wkvq = B(wkv)
                            fw.dma(POOL, wkv[:, q * 4:(q + 1) * 4, :].rearrange("p k n -> p (k n)"),
                                   wkv_d[:, q * 2048:(q + 1) * 2048], wr=[b_wkvq])
                            fw._merge(b_wkv.w, b_wkvq.w)
                        if idx == 3:
                            fw.dma(POOL, ckst[:], ck[pas], wr=[b_ck])
                            fw.dma(POOL, V1[:, 8, :, 0:64], cv[pas].rearrange("t (h d) -> t h d", h=4), wr=[b_V1[8]])
                            fw.dma(SP, kws[pas, 0:96, :], ck[pas, 32:128, :])
                            fw.dma(SP, vws[pas, 0:96, :], cv[pas, 32:128, :])
                        if idx == 10:
                            kv_phase()
                            st_1a.close()
                            xl = st_lru.enter_context(sb("xl", [128, 2, LW + 3], F32))
                            gbuf = st_lru.enter_context(sb("gbuf", [128, 2, NCOL], F32))
                            xc = st_lru.enter_context(sb("xc", [128, 2, LW], F32))
                            xcb = st_lru.enter_context(sb("xcb", [128, 2, LW], BF16))
                            rr2 = st_lru.enter_context(sb("rr", [128, 2, LW], F32))
                            ii = st_lru.enter_context(sb("ii", [128, LW], F32))
                            aa = st_lru.enter_context(sb("aa", [128, LW], F32))
                            mm = st_lru.enter_context(sb("mm", [128, LW], F32))
                            b_xl, b_gbuf, b_xc, b_xcb = [B(xl), B(xl)], [B(gbuf), B(gbuf)], [B(xc), B(xc)], [B(xcb), B(xcb)]
                            b_xlp = [B(xl), B(xl)]
                            b_rr2, b_ii, b_aa, b_mm = [B(rr2), B(rr2)], B(ii), B(aa), B(mm)
                            fw.op(DVE, lambda e: e.memset(rr2[:], 0.0), wr=b_rr2)
                            fw.op(DVE, lambda e: e.memset(ii[:], 0.0), wr=[b_ii])
                        if m == 12:
                            if chain is not None:
                                for _ in chain:
                                    pass
                                chain = None
                            st_lru.close()
                            accn["banks"] = sorted([2, 3, 4, 5], key=lambda b_: accn["last"][b_])
                            accn["i"] = 0
                            ex = _st6.enter_context(sb("ex", [128, 2, 2, 512], BF16))
                            PT = _st6.enter_context(sb("PT", [128, 3, 2, 512], BF16))
                            osb = _st6.enter_context(sb("osb", [128, 9, 1024], BF16))
                            den = _st6.enter_context(sb("den", [128, 2, 8], F32))
                            b_ex = [[B(ex), B(ex)] for _ in range(2)]
                            b_PT = [[B(PT), B(PT)] for _ in range(3)]
                            b_den = [B(den), B(den)]
                            b_osb = [B(osb) for _ in range(9)]
                            woB = st_wob.enter_context(sb("woB", [128, 8, D], BF16, side="right"))
                            wob_state["tensor"] = woB
                        if m == 12:
                            attn = attention()
                        if m >= 26:
                            c = m - 26
                            xb_ = c % 2
                            fw.op(DVE, lambda e: e.tensor_copy(out=xl[:, xb_, 0:3], in_=convst[:, c, :]),
                                  rd=[b_convst[c]], wr=[b_xlp[xb_]])
                            fw.op(DVE, lambda e: e.tensor_copy(out=xl[:, xb_, 3 + NPP:3 + NPP + 3], in_=sconv[:, pas, c, :]),
                                  rd=[b_const], wr=[b_xlp[xb_]])
                        for gi in range(3):
                            if attn is not None:
                                inproj_group(slot, gi, evac, mid=lambda: next(attn, None))
                                next(attn, None)
                            else:
                                inproj_group(slot, gi, evac)
                            if chain is not None:
                                if next(chain, "done") == "done":
                                    chain = None
                        next_wload(pas)
                        if 12 <= m < 18:
                            for _ in range(2):
                                kq = wob_state["next"]
                                if kq < 8:
                                    b_woB[kq] = B(wob_state["tensor"])
                                    fw.dma(POOL, wob_state["tensor"][:, kq, :], wout_d[:, (8 + kq) * 2048:(9 + kq) * 2048],
                                           wr=[b_woB[kq]])
                                    wob_state["next"] = kq + 1
                        if m >= 26:
                            conv(m - 26)
                        elif 18 <= m < 26:
                            if chain is not None:
                                for _ in chain:
                                    pass
                            chain = lru_chain(m - 18)
                    wo3 = wout_d.rearrange("p (k n) -> p k n", k=16)
                    pre = {}
                    for b_ in b_xT:
                        fw._merge(pre, b_.w)
                        fw._merge(pre, b_.r)
                    allw = {}
                    for cg in range(4):
                        b_woA[cg].w = dict(pre)
                        b_woA[cg].r = {}
                        fw.dma(POOL, woA[:, :, cg * 512:(cg + 1) * 512], wo3[:, 0:8, cg * 512:(cg + 1) * 512],
                               wr=[b_woA[cg]])
                        fw._merge(allw, b_woA[cg].w)
                    for b_ in b_xT:
                        b_.w = dict(allw)
                        b_.r = {}
                    if chain is not None:
                        for _ in chain:
                            pass
                    for _ in attn:
                        pass
                    for t in range(9):
                        attn_tr(t)
                    if pas == 0:
                        fw.op(DVE, lambda e: e.tensor_copy(out=KTh[:], in_=KT[:, :, NPP - 128:NPP]),
                              rd=[b_KT[0][7], b_KT[1][7]], wr=[b_KTh])
                        fw.op(DVE, lambda e: e.tensor_copy(out=V1h[:], in_=V1[:, 7, :, :]), rd=[b_V1[7]], wr=[b_V1h])

            with ExitStack() as _st7:
                lng = _st7.enter_context(sb("lng", [128, D], F32))
                lnb = _st7.enter_context(sb("lnb", [128, D], F32))
                xr = _st7.enter_context(sb("xr", [128, 1, D], F32))
                yy = _st7.enter_context(sb("yy", [128, 3, D], F32))
                stat = _st7.enter_context(sb("stat", [128, 3, 4, 6], F32))
                mv = _st7.enter_context(sb("mv", [128, 3, 4], F32))
                b_ln = [B(lng), B(lnb)]
                b_xr = [B(xr)]
                b_yy, b_stat, b_mv = [B(yy) for _ in range(3)], [B(stat) for _ in range(3)], [B(mv) for _ in range(3)]
                p2tiles = list(range(8)) if pas == 0 else list(range(9))

                def xr_load(t):
                    nr = 128 if t < 8 else 2 * NS
                    src = xp[tok0 + t * 128: tok0 + (t + 1) * 128, :] if t < 8 else xs[0:2 * NS, :]
                    fw.dma(SP, xr[0:nr, 0, :], src, wr=[b_xr[0]])

                xr_load(0)
                fw.dma(SP, lng[:], lng_d, wr=[b_ln[0]])
                fw.dma(SP, lnb[:], lnb_d, wr=[b_ln[1]])
                if pas == 0:
                    for t_ in range(3):
                        xload(1, t_)
                    for _ in range(3):
                        next_wload(1, pre=True)

                def tail(t):
                    nr = 128 if t < 8 else 2 * NS
                    yb = t % 3
                    fw.op(DVE, lambda e: e.tensor_tensor(out=yy[0:nr, yb, :], in0=yy[0:nr, yb, :], in1=lng[0:nr, :],
                                                         op=ALU.mult), rd=[b_yy[yb], b_ln[0]], wr=[b_yy[yb]])
                    fw.op(DVE, lambda e: e.tensor_tensor(out=yy[0:nr, yb, :], in0=yy[0:nr, yb, :], in1=lnb[0:nr, :],
                                                         op=ALU.add), rd=[b_yy[yb], b_ln[1]], wr=[b_yy[yb]])
                    dst = yp[tok0 + t * 128: tok0 + (t + 1) * 128, :] if t < 8 else ys[0:2 * NS, :]
                    fw.dma(SP, dst, yy[0:nr, yb, :], rd=[b_yy[yb]])

                def mm_half(t, cg, first):
                    nr = 128 if t < 8 else 2 * NS
                    c0 = t * 128
                    bank = (t % 2) * 4 + cg
                    for ki, k in enumerate(range(8, 16) if first else range(8)):
                        if k < 8:
                            rhs, rb, nr_ = woA[:, k, cg * 512:(cg + 1) * 512], [b_woA[cg]], b_xT
                        else:
                            rhs, rb, nr_ = woB[:, k - 8, cg * 512:(cg + 1) * 512], [b_woB[k - 8]], ()
                        fw.op(PE, lambda e: e.matmul(py[0:nr, bank, :], lhsT=uT[:, k, c0:c0 + nr], rhs=rhs,
                                                     start=(first and ki == 0), stop=((not first) and ki == 7)),
                              rd=[b_uT[k][t]] + rb, wr=[b_py[bank]], inc=(ki == 7), note_r=nr_)

                for t_ in (0, 1):
                    for cg in range(4):
                        mm_half(t_, cg, True)

                for t in p2tiles:
                    nr = 128 if t < 8 else 2 * NS
                    c0 = t * 128
                    pb = t % 2
                    yb = t % 3
                    for cg in range(4):
                        bank = pb * 4 + cg
                        if t >= 2:
                            mm_half(t, cg, True)
                        mm_half(t, cg, False)
                        fw.op(DVE, lambda e: e.scalar_tensor_tensor(
                            out=yy[0:nr, yb, cg * 512:(cg + 1) * 512], in0=xr[0:nr, 0, cg * 512:(cg + 1) * 512],
                            scalar=ALPHA, in1=py[0:nr, bank, :], op0=ALU.mult, op1=ALU.add),
                            rd=[b_xr[0], b_py[bank]], wr=[b_yy[yb]])
                        fw.op(DVE, lambda e: e.bn_stats(out=stat[0:nr, yb, cg, :], in_=yy[0:nr, yb, cg * 512:(cg + 1) * 512]),
                              rd=[b_yy[yb]], wr=[b_stat[yb]])
                    if t + 1 in p2tiles:
                        xr_load(t + 1)
                    if t == 7 and pas == 0:
                        for t_ in range(3):
                            xT_tile(1, t_, pbanks=((0, 1) if t_ % 2 == 0 else (2, 3)))
                    fw.op(DVE, lambda e: e.bn_aggr(out=mv[0:nr, yb, 0:2],
                                                   in_=stat[0:nr, yb, :, :].rearrange("p a b -> p (a b)")),
                          rd=[b_stat[yb]], wr=[b_mv[yb]])
                    fw.op(ACT, lambda e: e.activation(out=mv[0:nr, yb, 2:3], in_=mv[0:nr, yb, 1:2], func=AF.Sqrt,
                                                      bias=epsb[0:nr, :]),
                          rd=[b_mv[yb], b_const], wr=[b_mv[yb]])
                    fw.op(DVE, lambda e: e.reciprocal(out=mv[0:nr, yb, 2:3], in_=mv[0:nr, yb, 2:3]),
                          rd=[b_mv[yb]], wr=[b_mv[yb]])
                    fw.op(DVE, lambda e: e.scalar_tensor_tensor(out=mv[0:nr, yb, 3:4], in0=mv[0:nr, yb, 0:1], scalar=-1.0,
                                                                in1=mv[0:nr, yb, 2:3], op0=ALU.mult, op1=ALU.mult),
                          rd=[b_mv[yb]], wr=[b_mv[yb]])
                    fw.op(ACT, lambda e: e.activation(out=yy[0:nr, yb, :], in_=yy[0:nr, yb, :], func=AF.Identity,
                                                      scale=mv[0:nr, yb, 2:3], bias=mv[0:nr, yb, 3:4]),
                          rd=[b_yy[yb], b_mv[yb]], wr=[b_yy[yb]])
                    if t >= 1:
                        tail(t - 1)
                tail(p2tiles[-1])
            st_wob.close()

        with ExitStack() as _st8:
            ost = _st8.enter_context(sb("ost", [12, 1024], F32))
            psof = PS[:, 0:2, :].rearrange("p a b -> p (a b)")
            b_ost = B(ost)
            for c in range(8):
                fw.op(PE, lambda e, c=c: e.transpose(psof[0:12, c * 128:(c + 1) * 128], outst[:, c, :], identf[:]),
                      rd=[b_outst[c], b_const], wr=b_ps[0:2], inc=(c == 7))
            fw.op(DVE, lambda e: e.tensor_copy(out=ost[:], in_=psof[0:12, :]),
                  rd=b_ps[0:2], wr=[b_ost])
            fw.dma(SP, hp, ost[0:1, :], rd=[b_ost])
            fw.dma(SP, cvp, ost[1:4, :], rd=[b_ost])
            for s in range(2):
                fw.dma(SP, hs[s:s + 1, :], ost[4 + 4 * s:5 + 4 * s, :], rd=[b_ost])
                fw.dma(SP, cvs[s], ost[5 + 4 * s:8 + 4 * s, :], rd=[b_ost])
            fw.barrier()
    return nc


def _alibi_tables():
    H = 16
    slopes = np.exp2(-8.0 * np.arange(1, H + 1, dtype=np.float64) / H)
    k = np.arange(128)[:, None]
    q = np.arange(128)[None, :]
    NEG = -200.0
    abp = np.zeros((128, 4, 2, 4, 128), np.float32)
    for kv in range(4):
        for g in range(4):
            s = slopes[kv * 4 + g]
            dA = (q + 128 - k).astype(np.float64)
            vA = (k // 64) >= (q // 64)
            abp[:, kv, 0, g, :] = np.where(vA, -s * dA, NEG)
            dB = np.abs(q - k).astype(np.float64)
            vB = (k // 64) <= (q // 64)
            abp[:, kv, 1, g, :] = np.where(vB, -s * dB, NEG)
    q2 = np.arange(32)[None, :]
    abs_ = np.zeros((128, 4, 2, 4, 32), np.float32)
    for kv in range(4):
        for g in range(4):
            s = slopes[kv * 4 + g]
            abs_[:, kv, 0, g, :] = -s * (q2 + 128 - k)
            abs_[:, kv, 1, g, :] = np.where(k < 32, -s * np.abs(q2 - k), NEG)
    return abp.reshape(128, 4096), abs_.reshape(128, 1024)


def _chunk_cols():
    cols = []
    for c in range(2):
        cols.append(np.arange(1024 + c * 128, 1024 + (c + 1) * 128))
    for j in range(2):
        for g in range(4):
            h0 = (2 * j) * 4 + g
            h1 = (2 * j + 1) * 4 + g
            cols.append(np.concatenate([np.arange(h0 * 64, h0 * 64 + 64), np.arange(h1 * 64, h1 * 64 + 64)]))
    for c in range(8):
        cols.append(np.arange(1536 + c * 128, 1536 + (c + 1) * 128))
    for c in range(8):
        cols.append(np.arange(3584 + c * 128, 3584 + (c + 1) * 128))
    for c in range(8):
        cols.append(np.arange(2560 + c * 128, 2560 + (c + 1) * 128))
    return cols


_NC_CACHE = {}


def kernel(x_prompt, x_sample, cache_k, cache_v, state_conv, state_h, w_in, b_in, conv_w, conv_b,
           w_gate_a, b_gate_a, w_gate_x, b_gate_x, lru_lambda, attn_sinks, w_out, ln_g, ln_b):
    f = lambda a: np.ascontiguousarray(np.asarray(a, dtype=np.float32))
    x_prompt, x_sample, cache_k, cache_v = f(x_prompt), f(x_sample), f(cache_k), f(cache_v)
    state_conv, state_h = f(state_conv), f(state_h)
    W = f(w_in)[0]; bi = f(b_in)[0]; Wo = f(w_out)[0]
    cols = _chunk_cols()
    win = np.empty((NCH, 128, 16, 128), np.float32)
    btab = np.empty((128, NCH), np.float32)
    for m, cc in enumerate(cols):
        win[m] = W[:, cc].reshape(16, 128, 128).transpose(1, 0, 2)
        btab[:, m] = bi[cc]
    win = win.reshape(NCH, 128, 2048)
    wkv = np.ascontiguousarray(W[:, 1024:1536].reshape(16, 128, 512).transpose(1, 0, 2)).reshape(128, 8192)
    wout = np.ascontiguousarray(Wo.reshape(16, 128, 2048).transpose(1, 0, 2)).reshape(128, 32768)
    bkv = np.ascontiguousarray(np.broadcast_to(bi[1024:1536], (128, 512)))
    cwt = np.ascontiguousarray(f(conv_w)[0].reshape(4, 8, 128).transpose(2, 1, 0)).reshape(128, 32)
    pc = lambda v: np.ascontiguousarray(f(v).reshape(8, 128).T)
    cbt, lamt = pc(f(conv_b)[0]), pc(f(lru_lambda)[0])
    bgat, bgxt = pc(f(b_gate_a)[0]), pc(f(b_gate_x)[0])
    wgt = np.zeros((128, 2, 8, 128), np.float32)
    for a, wsrc in enumerate((f(w_gate_a)[0], f(w_gate_x)[0])):
        for c in range(8):
            wgt[0:64, a, c, 0:64] = wsrc[2 * c]
            wgt[64:128, a, c, 64:128] = wsrc[2 * c + 1]
    wgt = wgt.reshape(128, 2048)
    lngt = np.ascontiguousarray(np.broadcast_to(f(ln_g)[0], (128, D)))
    lnbt = np.ascontiguousarray(np.broadcast_to(f(ln_b)[0], (128, D)))
    sinkt = np.ascontiguousarray(np.broadcast_to(f(attn_sinks)[0], (128, 16)))
    abp, abs_ = _alibi_tables()
    ident = np.eye(128, dtype=np.float32)

    if "nc" not in _NC_CACHE:
        _NC_CACHE["nc"] = build()
    nc = _NC_CACHE["nc"]

    in_maps = []
    for b in range(NCORES):
        sc = state_conv[0, 2 * b:2 * b + 2]
        sct = np.ascontiguousarray(sc.reshape(2, 3, 8, 128).transpose(3, 0, 2, 1)).reshape(128, 48)
        sht = np.ascontiguousarray(state_h[0, 2 * b:2 * b + 2].reshape(2, 8, 128).transpose(2, 0, 1)).reshape(128, 16)
        in_maps.append({
            "xp": x_prompt[b], "xs": np.ascontiguousarray(x_sample[2 * b:2 * b + 2].reshape(2 * NS, D)),
            "ck": np.ascontiguousarray(cache_k[0, 2 * b:2 * b + 2].reshape(2, 128, 256)),
            "cv": np.ascontiguousarray(cache_v[0, 2 * b:2 * b + 2].reshape(2, 128, 256)),
            "sconv": sct, "sh": sht, "win": win, "wkv": wkv, "wout": wout, "bt": btab, "bkv": bkv,
            "cw": cwt, "cb": cbt, "lam": lamt, "bga": bgat, "bgx": bgxt, "wg": wgt, "lng": lngt, "lnb": lnbt,
            "sinks": sinkt, "abp": abp, "abs": abs_, "ident": ident,
        })
    res = run_bass_kernel_spmd(nc, in_maps, core_ids=list(range(NCORES)))
    R = res.results
    y_p = np.stack([R[b]["yp"] for b in range(NCORES)])
    y_s = np.concatenate([R[b]["ys"].reshape(2, NS, D) for b in range(NCORES)])
    kwp = np.stack([R[b]["kwp"].reshape(128, 4, 64) for b in range(NCORES)])[None]
    vwp = np.stack([R[b]["vwp"].reshape(128, 4, 64) for b in range(NCORES)])[None]
    cvp = np.stack([R[b]["cvp"] for b in range(NCORES)])[None]
    hp = np.concatenate([R[b]["hp"] for b in range(NCORES)])[None]
    kws = np.concatenate([R[b]["kws"].reshape(2, 128, 4, 64) for b in range(NCORES)])[None]
    vws = np.concatenate([R[b]["vws"].reshape(2, 128, 4, 64) for b in range(NCORES)])[None]
    cvs = np.concatenate([R[b]["cvs"] for b in range(NCORES)])[None]
    hs = np.concatenate([R[b]["hs"] for b in range(NCORES)])[None]
    return (y_p, y_s, kwp, vwp, cvp, hp, kws, vws, cvs, hs)
```

```python
import numpy as np
from contextlib import ExitStack
import concourse.bass as bass
import concourse.mybir as mybir
from concourse.bass_utils import run_bass_kernel_spmd

F32 = mybir.dt.float32
BF16 = mybir.dt.bfloat16
AF = mybir.ActivationFunctionType
ALU = mybir.AluOpType

NCORES = 8
D = 2048
SEQ = 2048
NPP = 1024
NS = 32
NCOL = NPP + NS
NCH = 34
LW = NPP + 3 + NS
ALPHA = 2.0 ** 0.25
LN_EPS = 1e-5
GROUPS = [(0, 512), (512, 1024), (1024, 1056)]
LGROUPS = [(0, 512), (512, 1024), (1027, 1059)]


class Buf:
    __slots__ = ("w", "r", "name", "reg")

    def __init__(self, name=""):
        self.w = {}
        self.r = {}
        self.name = name
        self.reg = None


class Eng:
    def __init__(self, nc, h, name, in_order=False):
        self.h = h
        self.name = name
        self.sem = nc.alloc_semaphore(name="s_" + name)
        self.cnt = 0
        self.seen = {}
        self.in_order = in_order


class Fw:
    def __init__(self, nc, ndma=24):
        self.nc = nc
        self.pe = Eng(nc, nc.tensor, "pe", in_order=True)
        self.act = Eng(nc, nc.scalar, "act")
        self.dve = Eng(nc, nc.vector, "dve")
        self.pool = Eng(nc, nc.gpsimd, "pool")
        self.sp = Eng(nc, nc.sync, "sp")
        self.engs = [self.pe, self.act, self.dve, self.pool, self.sp]
        self.dsem = [nc.alloc_semaphore(name="d%d" % i) for i in range(ndma)]
        self.dcnt = [0] * ndma
        self.dnext = {"hw": 0, "sw": 0}
        self.half = ndma // 2
        self.limit = {}
        self.semobj = {}
        for e in self.engs:
            self.semobj[id(e.sem)] = e.sem
            self.limit[id(e.sem)] = 0
        for s in self.dsem:
            self.semobj[id(s)] = s
            self.limit[id(s)] = 0

    def _wait(self, eng, toks):
        for k, val in toks.items():
            if eng.in_order and k == id(eng.sem):
                continue
            if eng.seen.get(k, 0) >= val:
                continue
            assert val <= self.limit[k], "wait on a value never signalled (%s)" % eng.name
            eng.h.wait_ge(self.semobj[k], val)
            eng.seen[k] = val

    @staticmethod
    def _merge(dst, src):
        for k, v in src.items():
            if dst.get(k, 0) < v:
                dst[k] = v

    def _deps(self, rd, wr):
        toks = {}
        for b in rd:
            self._merge(toks, b.w)
        for b in wr:
            self._merge(toks, b.w)
            self._merge(toks, b.r)
        return toks

    def op(self, eng, fn, rd=(), wr=(), inc=True, note_r=()):
        self._wait(eng, self._deps(rd, wr))
        ins = fn(eng.h)
        k = id(eng.sem)
        if inc:
            eng.cnt += 1
            ins.then_inc(eng.sem, 1)
            self.limit[k] = eng.cnt
            tok = {k: eng.cnt}
        else:
            tok = {k: eng.cnt + 1}
        for b in rd:
            self._merge(b.r, tok)
        for b in note_r:
            self._merge(b.r, tok)
        for b in wr:
            b.w = dict(tok)
            b.r = {}
        return ins

    def dma(self, q, out, in_, rd=(), wr=()):
        self._wait(q, self._deps(rd, wr))
        kind = "sw" if q is self.pool else "hw"
        j = self.dnext[kind] + (self.half if kind == "sw" else 0)
        self.dnext[kind] = (self.dnext[kind] + 1) % self.half
        sem = self.dsem[j]
        k = id(sem)
        if self.dcnt[j] > 0:
            self._wait(q, {k: 16 * self.dcnt[j]})
        q.h.dma_start(out=out, in_=in_).then_inc(sem, 16)
        self.dcnt[j] += 1
        self.limit[k] = 16 * self.dcnt[j]
        tok = {k: 16 * self.dcnt[j]}
        for b in rd:
            self._merge(b.r, tok)
        for b in wr:
            b.w = dict(tok)
            b.r = {}

    def barrier(self):
        toks = {}
        for e in self.engs:
            if e.cnt > 0:
                toks[id(e.sem)] = e.cnt
        for j, s in enumerate(self.dsem):
            if self.dcnt[j] > 0:
                toks[id(s)] = 16 * self.dcnt[j]
        for e in self.engs:
            self._wait(e, toks)


def build():
    nc = bass.Bass("TRN2", target_bir_lowering=False)
    fw = Fw(nc)
    PE, ACT, DVE, POOL, SP = fw.pe, fw.act, fw.dve, fw.pool, fw.sp

    def din(name, shape):
        return nc.dram_tensor(name, shape, F32, kind="ExternalInput").ap()

    def dout(name, shape):
        return nc.dram_tensor(name, shape, F32, kind="ExternalOutput").ap()

    xp = din("xp", [SEQ, D]); xs = din("xs", [2 * NS, D])
    ck = din("ck", [2, 128, 256]); cv = din("cv", [2, 128, 256])
    sconv_d = din("sconv", [128, 48]); sh_d = din("sh", [128, 16])
    win = din("win", [NCH, 128, 2048]); wkv_d = din("wkv", [128, 8192]); wout_d = din("wout", [128, 32768])
    bt_d = din("bt", [128, NCH]); bkv_d = din("bkv", [128, 512])
    cw_d = din("cw", [128, 32]); cb_d = din("cb", [128, 8]); lam_d = din("lam", [128, 8])
    bga_d = din("bga", [128, 8]); bgx_d = din("bgx", [128, 8]); wg_d = din("wg", [128, 2048])
    lng_d = din("lng", [128, D]); lnb_d = din("lnb", [128, D]); sinks_d = din("sinks", [128, 16])
    abp_d = din("abp", [128, 4096]); abs_d = din("abs", [128, 1024]); ident_d = din("ident", [128, 128])

    yp = dout("yp", [SEQ, D]); ys = dout("ys", [2 * NS, D])
    kwp = dout("kwp", [128, 256]); vwp = dout("vwp", [128, 256])
    cvp = dout("cvp", [3, 1024]); hp = dout("hp", [1, 1024])
    kws = dout("kws", [2, 128, 256]); vws = dout("vws", [2, 128, 256])
    cvs = dout("cvs", [2, 3, 1024]); hs = dout("hs", [2, 1024])

    uniq = {"n": 0}

    def sb(name, shape, dt, side=None):
        uniq["n"] += 1
        return nc.sbuf_tensor("s%d_%s" % (uniq["n"], name), shape, dt, side=side)

    def ps(name, shape, dt):
        uniq["n"] += 1
        return nc.psum_tensor("p%d_%s" % (uniq["n"], name), shape, dt)

    with ExitStack() as _st1:
        uT = _st1.enter_context(sb("uT", [128, 16, NCOL + NS], BF16))
        identb = _st1.enter_context(sb("identb", [128, 128], BF16))
        identf = _st1.enter_context(sb("identf", [128, 128], F32))
        bt = _st1.enter_context(sb("bt", [128, NCH], F32))
        bkv = _st1.enter_context(sb("bkv", [128, 512], F32))
        cw = _st1.enter_context(sb("cw", [128, 8, 4], F32))
        cb = _st1.enter_context(sb("cb", [128, 8], F32))
        lamc = _st1.enter_context(sb("lamc", [128, 8], F32))
        bga = _st1.enter_context(sb("bga", [128, 8], F32))
        bgx = _st1.enter_context(sb("bgx", [128, 8], F32))
        wg = _st1.enter_context(sb("wg", [128, 2, 8, 128], BF16))
        Ep = _st1.enter_context(sb("Ep", [128, 4, 2, 4, 128], BF16))
        Es = _st1.enter_context(sb("Es", [128, 4, 2, 4, 32], BF16))
        esink = _st1.enter_context(sb("esink", [128, 16], F32))
        KTh = _st1.enter_context(sb("KTh", [128, 2, 128], BF16))
        V1h = _st1.enter_context(sb("V1h", [128, 4, 65], BF16))
        convst = _st1.enter_context(sb("convst", [128, 8, 3], F32))
        hst = _st1.enter_context(sb("hst", [128, 8], F32))
        outst = _st1.enter_context(sb("outst", [128, 8, 12], F32))
        sconv = _st1.enter_context(sb("sconv", [128, 2, 8, 3], F32))
        sh_sb = _st1.enter_context(sb("sh", [128, 2, 8], F32))
        epsb = _st1.enter_context(sb("epsb", [128, 1], F32))
        qtr = _st1.enter_context(sb("qtr", [128, 1], F32))
        scr = _st1.enter_context(sb("scr", [128, 2], F32))
        bth = _st1.enter_context(sb("bth", [128, NCH], F32))
        wb = _st1.enter_context(sb("wb", [128, 3, 2048], BF16))
        XW = _st1.enter_context(sb("XW", [128, 16 * NCOL], BF16))
        xT = XW[:, :].rearrange("p (k n) -> p k n", k=16)
        woA = XW[:, 0:8 * D].rearrange("p (k n) -> p k n", k=8)

        PS = _st1.enter_context(ps("PS", [128, 8, 512], F32))
        xst = _st1.enter_context(sb("xst", [128, 3, 2048], BF16))
        po = PS[:, 0:2, :]
        ptr = PS[:, 2, :].bitcast(BF16)
        pacc = PS[:, 3:6, :]
        pst = PS[:, 6:8, :]
        ptx = [PS[:, 6, :].bitcast(BF16), PS[:, 7, :].bitcast(BF16)]
        py = PS

        reg = {"all": []}

        def PB():
            return Buf()

        def B(t):
            b_ = Buf()
            ml = nc.lookup_mloc(t)
            b_.reg = (int(ml.addr), int(ml.addr) + int(ml.dims[1]))
            for o in reg["all"]:
                if o.reg is None or (o.reg[0] < b_.reg[1] and b_.reg[0] < o.reg[1]):
                    fw._merge(b_.w, o.w)
                    fw._merge(b_.w, o.r)
            reg["all"].append(b_)
            return b_

        b_ps = [PB() for _ in range(8)]
        b_po, b_ptr, b_pacc, b_ptx = b_ps[0:2], b_ps[2], b_ps[3:6], b_ps[6:8]
        b_pst2 = b_ps[6:8]
        b_py = b_ps
        b_xT = [PB() for _ in range(9)]
        b_xst = [PB(), PB(), PB()]
        b_wb = [PB() for _ in range(3)]
        b_uT = [[PB() for _ in range(9)] for _ in range(16)]
        b_const = PB()
        b_ident = PB()
        b_bias = PB()
        b_scr = PB()
        b_KTh, b_V1h = PB(), PB()
        b_convst = [PB() for _ in range(8)]
        b_hst = [PB() for _ in range(8)]
        b_outst = [PB() for _ in range(8)]

        init_bufs = []

        def IB():
            b_ = Buf()
            init_bufs.append(b_)
            return b_

        st_init = ExitStack()
        stg = st_init.enter_context(sb("stg", [128, 4096], F32, side="right"))
        stg2 = st_init.enter_context(sb("stg2", [128, 1024], F32, side="right"))
        stg3 = st_init.enter_context(sb("stg3", [128, 16], F32, side="right"))
        lamt = st_init.enter_context(sb("lamt", [128, 16], F32, side="right"))
        b_bga, b_bgx = IB(), IB()
        for dst, src in ((identf, ident_d), (bkv, bkv_d), (cb, cb_d)):
            fw.dma(SP, dst[:], src, wr=[IB()])
        fw.dma(SP, bt[:], bt_d, wr=[b_bias])
        fw.op(DVE, lambda e: e.tensor_scalar(out=bth[:], in0=bt[:], scalar1=0.5, scalar2=None, op0=ALU.mult),
              rd=[b_bias], wr=[IB()])
        fw.dma(SP, bga[:], bga_d, wr=[b_bga])
        fw.dma(SP, bgx[:], bgx_d, wr=[b_bgx])
        fw.dma(SP, cw[:].rearrange("p c t -> p (c t)"), cw_d, wr=[IB()])
        fw.dma(SP, sconv[:].rearrange("p s c t -> p (s c t)"), sconv_d, wr=[IB()])
        fw.dma(SP, sh_sb[:].rearrange("p s c -> p (s c)"), sh_d, wr=[IB()])
        fw.dma(POOL, identb[:], ident_d, wr=[b_ident])
        fw.op(DVE, lambda e: e.tensor_scalar(out=bga[:], in0=bga[:], scalar1=0.5, scalar2=None, op0=ALU.mult),
              rd=[b_bga], wr=[b_bga])
        fw.op(DVE, lambda e: e.tensor_scalar(out=bgx[:], in0=bgx[:], scalar1=0.5, scalar2=None, op0=ALU.mult),
              rd=[b_bgx], wr=[b_bgx])
        fw.op(DVE, lambda e: e.memset(qtr[:], 0.25), wr=[IB()])
        fw.op(DVE, lambda e: e.memset(scr[:], 0.0), wr=[IB()])
        fw.op(DVE, lambda e: e.memset(convst[:], 0.0), wr=b_convst)
        fw.op(DVE, lambda e: e.memset(epsb[:], LN_EPS), wr=[IB()])
        fw.op(DVE, lambda e: e.memset(hst[:], 0.0), wr=b_hst)
        fw.op(DVE, lambda e: e.memset(V1h[:], 1.0), wr=[b_V1h])
        fw.op(DVE, lambda e: e.memset(KTh[:], 0.0), wr=[b_KTh])

        def late_init():
            b_stg, b_stg2, b_stg3, b_lam = IB(), IB(), IB(), IB()
            fw.dma(POOL, wg[:].rearrange("p a c e -> p (a c e)"), wg_d, wr=[IB()])
            fw.dma(SP, stg[:], abp_d, wr=[b_stg])
            fw.dma(SP, stg2[:], abs_d, wr=[b_stg2])
            fw.dma(SP, stg3[:], sinks_d, wr=[b_stg3])
            fw.dma(SP, lamt[:, 0:8], lam_d, wr=[b_lam])
            fw.op(ACT, lambda e: e.activation(out=Ep[:].rearrange("p a b c d -> p (a b c d)"), in_=stg[:], func=AF.Exp),
                  rd=[b_stg], wr=[IB()])
            fw.op(ACT, lambda e: e.activation(out=Es[:].rearrange("p a b c d -> p (a b c d)"), in_=stg2[:], func=AF.Exp),
                  rd=[b_stg2], wr=[IB()])
            fw.op(ACT, lambda e: e.activation(out=esink[:], in_=stg3[:], func=AF.Exp), rd=[b_stg3], wr=[IB()])
            fw.op(ACT, lambda e: e.activation(out=lamt[:, 8:16], in_=lamt[:, 0:8], func=AF.Exp, scale=-1.0),
                  rd=[b_lam], wr=[b_lam])
            fw.op(ACT, lambda e: e.activation(out=lamt[:, 0:8], in_=lamt[:, 8:16], func=AF.Ln, bias=1.0),
                  rd=[b_lam], wr=[b_lam])
            fw.op(DVE, lambda e: e.tensor_scalar(out=lamc[:], in0=lamt[:, 0:8], scalar1=-4.0, scalar2=None,
                                                 op0=ALU.mult), rd=[b_lam], wr=[IB()])
            for b_ in init_bufs:
                fw._merge(b_const.w, b_.w)
                fw._merge(b_const.w, b_.r)
            reg["all"].extend(init_bufs)
            st_init.close()

        morder = list(range(10))
        for c in range(8):
            morder += [26 + c, 18 + c]
        morder += list(range(10, 18))
        wstate = [{"next": 0, "pre": 0}, {"next": 0, "pre": 0}]
        xdone = set()

        def next_wload(pas_, pre=False):
            st_ = wstate[pas_]
            if not pre and st_["pre"] > 0:
                st_["pre"] -= 1
                return
            i = st_["next"]
            if i < NCH:
                fw.dma(POOL, wb[:, i % 3, :], win[morder[i]], wr=[b_wb[i % 3]])
                st_["next"] = i + 1
                if pre:
                    st_["pre"] += 1

        def xload(pas_, t):
            if (pas_, "l", t) in xdone:
                return
            xdone.add((pas_, "l", t))
            nr = 128 if t < 8 else NS
            src = xp[pas_ * NPP + t * 128: pas_ * NPP + (t + 1) * 128, :] if t < 8 else xs[pas_ * NS:(pas_ + 1) * NS, :]
            fw.dma(POOL, xst[0:nr, t % 3, :], src, wr=[b_xst[t % 3]])

        def xT_tile(pas_, t, pbanks=(6, 7), act_only=False):
            if (pas_, "t", t) in xdone:
                return
            xdone.add((pas_, "t", t))
            nr = 128 if t < 8 else NS
            c0 = t * 128
            for half in range(2):
                bank = b_ps[pbanks[half]]
                pv = PS[:, pbanks[half], :].bitcast(BF16)
                for kk in range(8):
                    k = half * 8 + kk
                    fw.op(PE, lambda e, k=k, kk=kk, pv=pv: e.transpose(
                        pv[:, kk * 128: kk * 128 + nr], xst[0:nr, t % 3, k * 128:(k + 1) * 128],
                        identb[0:nr, 0:nr]),
                        rd=[b_xst[t % 3], b_ident], wr=[bank], inc=(kk == 7))
                src_ap = pv.rearrange("p (k c) -> p k c", k=8)[:, :, 0:nr]
                dst_ap = xT[:, half * 8:(half + 1) * 8, c0:c0 + nr]
                if half == 0 or act_only:
                    fw.op(ACT, lambda e, s=src_ap, d=dst_ap: e.copy(out=d, in_=s), rd=[bank], wr=[b_xT[t]])
                else:
                    fw.op(DVE, lambda e, s=src_ap, d=dst_ap: e.tensor_copy(out=d, in_=s), rd=[bank],
                          wr=[b_xT[t]])
            if t + 3 < 9:
                xload(pas_, t + 3)
            if pas_ == 0 and t < 3:
                next_wload(0)

        for pas in range(2):
            tok0 = pas * NPP
            st_wob = ExitStack()
            b_woB = [None] * 8
            wob_state = {"next": 0}
            b_woA = [PB() for _ in range(4)]
            with ExitStack() as _st3:
                QT = _st3.enter_context(sb("QT", [128, 8, NCOL], BF16))
                KT = _st3.enter_context(sb("KT", [128, 2, NPP + 128 + NS], BF16))
                V1 = _st3.enter_context(sb("V1", [128, 10, 4, 65], BF16))
                b_QT = [[B(QT) for _ in range(3)] for _ in range(8)]
                b_KT = [[B(KT) for _ in range(10)] for _ in range(2)]
                b_V1 = [B(V1) for _ in range(10)]
                fw.op(DVE, lambda e: e.memset(V1[:], 1.0), wr=b_V1)

                if True:
                    def kv_tokmajor(t):
                        nr = 128 if t < 8 else NS
                        c0 = t * 128
                        need_k = (t == 8) or (pas == 1 and t == 7)
                        lo = 0 if need_k else 256
                        bank = t % 2
                        for k in range(16):
                            fw.op(PE, lambda e, k=k: e.matmul(po[0:nr, bank, lo:512], lhsT=xT[:, k, c0:c0 + nr],
                                                               rhs=wkv[:, k, lo:512], start=(k == 0), stop=(k == 15)),
                                  rd=[b_xT[t], b_wkv], wr=[b_po[bank]], inc=(k == 15))
                        vt = t if t < 8 else 9
                        if need_k:
                            fw.op(DVE, lambda e: e.tensor_tensor(out=kvst[0:nr, bank, :], in0=po[0:nr, bank, :],
                                                                 in1=bkv[0:nr, :], op=ALU.add),
                                  rd=[b_po[bank], b_const], wr=[b_kvst[bank]])
                            fw.op(ACT, lambda e: e.copy(out=V1[0:nr, vt, :, 0:64],
                                                        in_=kvst[0:nr, bank, 256:512].rearrange("p (h d) -> p h d", h=4)),
                                  rd=[b_kvst[bank]], wr=[b_V1[vt]])
                            if t == 8:
                                fw.dma(SP, kws[pas, 96:128, :], kvst[0:NS, bank, 0:256], rd=[b_kvst[bank]])
                                fw.dma(SP, vws[pas, 96:128, :], kvst[0:NS, bank, 256:512], rd=[b_kvst[bank]])
                            else:
                                fw.dma(SP, kwp, kvst[:, bank, 0:256], rd=[b_kvst[bank]])
                                fw.dma(SP, vwp, kvst[:, bank, 256:512], rd=[b_kvst[bank]])
                        else:
                            fw.op(DVE, lambda e: e.tensor_tensor(
                                out=V1[0:nr, vt, :, 0:64],
                                in0=po[0:nr, bank, 256:512].rearrange("p (h d) -> p h d", h=4),
                                in1=bkv[0:nr, 256:512].rearrange("p (h d) -> p h d", h=4), op=ALU.add),
                                rd=[b_po[bank], b_const], wr=[b_V1[vt]])

                    for t_ in range(3):
                        xload(pas, t_)
                    if pas == 1:
                        for _ in range(3):
                            next_wload(pas)

                    def kv_phase():
                        for t in range(9):
                            kv_tokmajor(t)
                        for c in range(2):
                            fw.op(PE, lambda e, c=c: e.transpose(ptr[:, c * 128:(c + 1) * 128],
                                                                 ckst[:, c * 128:(c + 1) * 128], identb[:]),
                                  rd=[b_ck, b_const], wr=[b_ptr], inc=(c == 1))
                        fw.op(DVE, lambda e: e.tensor_copy(out=KT[:, :, NPP:NPP + 128],
                                                           in_=ptr[:, 0:256].rearrange("p (c t) -> p c t", c=2)),
                              rd=[b_ptr], wr=[b_KT[0][8], b_KT[1][8]])


                with ExitStack() as _st6:
                    tA = _st6.enter_context(sb("tA", [128, 2, 512], F32))
                    b_tA = [B(tA), B(tA)]
                    st_1a = ExitStack()
                    wkv = st_1a.enter_context(sb("wkv", [128, 16, 512], BF16))
                    kvst = st_1a.enter_context(sb("kvst", [128, 2, 512], F32))
                    ckst = st_1a.enter_context(sb("ckst", [128, 256], BF16))
                    b_wkv, b_kvst, b_ck = B(wkv), [B(kvst), B(kvst)], B(ckst)
                    st_lru = ExitStack()
                    accn = {"i": 0, "e": 0, "banks": list(range(8)), "last": [0] * 8}

                    def inproj_group(slot, gi, evac, mid=None):
                        lo, hi = GROUPS[gi]
                        bi = accn["banks"][accn["i"] % len(accn["banks"])]
                        accn["i"] += 1
                        accn["last"][bi] = accn["i"]
                        t_lo, t_hi = lo // 128, (hi + 127) // 128
                        for k in range(16):
                            fw.op(PE, lambda e, k=k: e.matmul(PS[:, bi, 0:hi - lo], lhsT=wb[:, slot, k * 128:(k + 1) * 128],
                                                               rhs=xT[:, k, lo:hi], start=(k == 0), stop=(k == 15)),
                                  rd=[b_wb[slot]] + b_xT[t_lo:t_hi], wr=[b_ps[bi]], inc=(k == 15))
                            if k == 7 and mid is not None:
                                mid()
                        evac(PS[:, bi, 0:hi - lo], b_ps[bi], lo, hi, gi)

                    def tiles_of(lo, hi):
                        return range(lo // 128, (hi + 127) // 128)

                    def attn_unit(t, kv):
                        nq = 128 if t < 8 else NS
                        q0 = t * 128
                        j, half = kv // 2, kv % 2
                        pb = slice(half * 64, half * 64 + 64)
                        xb = kv % 2
                        p3 = (t * 4 + kv) % 4
                        pk = kv % 2
                        tiles = []
                        if t < 8:
                            if t == 0:
                                if pas == 1:
                                    tiles.append((KTh[pb, j, :], b_KTh, V1h[:, kv, :], b_V1h, 128, Ep[:, kv, 0]))
                            else:
                                tiles.append((KT[pb, j, (t - 1) * 128:t * 128], b_KT[j][t - 1],
                                              V1[:, t - 1, kv, :], b_V1[t - 1], 128, Ep[:, kv, 0]))
                            tiles.append((KT[pb, j, t * 128:(t + 1) * 128], b_KT[j][t], V1[:, t, kv, :], b_V1[t],
                                          128, Ep[:, kv, 1]))
                        else:
                            tiles.append((KT[pb, j, NPP:NPP + 128], b_KT[j][8], V1[:, 8, kv, :], b_V1[8], 128,
                                          Es[:, kv, 0]))
                            tiles.append((KT[pb, j, NPP + 128:NPP + 128 + NS], b_KT[j][9], V1[0:NS, 9, kv, :],
                                          b_V1[9], NS, Es[0:NS, kv, 1]))
                        nt = len(tiles)

                        def st():
                            for ti, (kt_ap, kt_buf, v_ap, v_buf, nk, e_ap) in enumerate(tiles):
                                fw.op(PE, lambda e: e.matmul(
                                    pst[0:nk, ti, 0:4 * nq].rearrange("p (g q) -> p g q", g=4),
                                    lhsT=kt_ap, rhs=QT[pb, j * 4:(j + 1) * 4, q0:q0 + nq], start=True, stop=True),
                                    rd=[kt_buf] + [b_QT[j * 4 + g][min(t // 4, 2)] for g in range(4)],
                                    wr=[b_pst2[ti]], inc=(ti == nt - 1))
                            for ti, (kt_ap, kt_buf, v_ap, v_buf, nk, e_ap) in enumerate(tiles):
                                fw.op(ACT, lambda e: e.activation(
                                    out=ex[0:nk, xb, ti, 0:4 * nq], in_=pst[0:nk, ti, 0:4 * nq], func=AF.Exp, scale=0.125),
                                    rd=[b_pst2[ti]], wr=[b_ex[xb][ti]])
                                fw.op(DVE, lambda e: e.tensor_tensor(
                                    out=PT[0:nk, p3, ti, 0:4 * nq], in0=ex[0:nk, xb, ti, 0:4 * nq],
                                    in1=e_ap.rearrange("p g q -> p (g q)"), op=ALU.mult),
                                    rd=[b_ex[xb][ti], b_const], wr=[b_PT[p3][ti]])

                        def pvn():
                            for g in range(4):
                                for ti, (kt_ap, kt_buf, v_ap, v_buf, nk, e_ap) in enumerate(tiles):
                                    last = (g == 3 and ti == nt - 1)
                                    fw.op(PE, lambda e: e.matmul(
                                        po[0:nq, pk, g * 65:(g + 1) * 65], lhsT=PT[0:nk, p3, ti, g * nq:(g + 1) * nq],
                                        rhs=v_ap, start=(ti == 0), stop=(ti == nt - 1)),
                                        rd=[b_PT[p3][ti], v_buf], wr=[b_po[pk]], inc=last)
                            pov = po[0:nq, pk, 0:260].rearrange("p (g d) -> p g d", g=4)
                            fw.op(DVE, lambda e: e.tensor_tensor(
                                out=den[0:nq, pk, 0:4], in0=pov[:, :, 64], in1=esink[0:nq, kv * 4:(kv + 1) * 4], op=ALU.add),
                                rd=[b_po[pk], b_const], wr=[b_den[pk]])
                            fw.op(DVE, lambda e: e.reciprocal(out=den[0:nq, pk, 4:8], in_=den[0:nq, pk, 0:4]),
                                  rd=[b_den[pk]], wr=[b_den[pk]])
                            fw.op(DVE, lambda e: e.tensor_tensor(
                                out=osb[0:nq, t, kv * 256:(kv + 1) * 256].rearrange("p (g d) -> p g d", g=4),
                                in0=pov[:, :, 0:64], in1=den[0:nq, pk, 4:8].unsqueeze(2).to_broadcast([nq, 4, 64]),
                                op=ALU.mult), rd=[b_po[pk], b_den[pk]], wr=[b_osb[t]])
                        return st, pvn

                    def attn_tr(t):
                        nq = 128 if t < 8 else NS
                        q0 = t * 128 if t < 8 else NPP + NS * pas
                        bk = 2 + (t % 2)
                        ptb = PS[:, bk, :].bitcast(BF16)
                        for c in range(8):
                            fw.op(PE, lambda e, c=c: e.transpose(ptb[:, c * 128:c * 128 + nq],
                                                                 osb[0:nq, t, c * 128:(c + 1) * 128],
                                                                 identb[0:nq, 0:nq]),
                                  rd=[b_osb[t], b_const], wr=[b_ps[bk]], inc=(c == 7))
                        fw.op(DVE, lambda e: e.tensor_tensor(
                            out=uT[:, 0:8, q0:q0 + nq], in0=ptb.rearrange("p (c q) -> p c q", c=8)[:, :, 0:nq],
                            in1=uT[:, 0:8, q0:q0 + nq], op=ALU.mult),
                            rd=[b_ps[bk]] + [b_uT[c][t] for c in range(8)], wr=[b_uT[c][t] for c in range(8)])

                    def attention():
                        units = [(t, kv) for t in range(9) for kv in range(4)]
                        fns = [attn_unit(t, kv) for (t, kv) in units]
                        for i in range(len(units) + 3):
                            if i < len(units):
                                fns[i][0]()
                            if i >= 3:
                                fns[i - 3][1]()
                            yield

                    def lru_chain(c):
                        xb_ = c % 2
                        so = 4 + 4 * pas
                        rrc, b_rrc = rr2[:, c % 2, :], b_rr2[c % 2]
                        for gate in range(2):
                            dst, dbuf, bia = (rrc, b_rrc, bga) if gate == 0 else (ii, b_ii, bgx)
                            for (lo, hi) in LGROUPS:
                                bi = accn["banks"][accn["i"] % len(accn["banks"])]
                                accn["i"] += 1
                                accn["last"][bi] = accn["i"]
                                fw.op(PE, lambda e: e.matmul(PS[:, bi, 0:hi - lo], lhsT=wg[:, gate, c, :],
                                                             rhs=xcb[:, xb_, lo:hi], start=True, stop=True),
                                      rd=[b_xcb[xb_], b_const], wr=[b_ps[bi]])
                                fw.op(ACT, lambda e: e.activation(out=dst[:, lo:hi], in_=PS[:, bi, 0:hi - lo],
                                                                  func=AF.Tanh, bias=bia[:, c:c + 1], scale=0.5),
                                      rd=[b_ps[bi], b_const], wr=[dbuf])
                            if gate == 0:
                                fw.op(ACT, lambda e: e.activation(out=aa[:], in_=rrc, func=AF.Exp, scale=lamc[:, c:c + 1],
                                                                  bias=lamc[:, c:c + 1]),
                                      rd=[b_rrc, b_const], wr=[b_aa])
                            yield
                        fw.op(ACT, lambda e: e.activation(out=rrc[:, 0:NCOL], in_=gbuf[:, xb_, :], func=AF.Tanh, scale=0.5),
                              rd=[b_gbuf[xb_], b_rrc], wr=[b_rrc])
                        fw.op(ACT, lambda e: e.activation(out=mm[:], in_=aa[:], func=AF.Square), rd=[b_aa], wr=[b_mm])
                        fw.op(ACT, lambda e: e.activation(out=mm[:], in_=mm[:], func=AF.Sqrt, scale=-0.25, bias=qtr[:, 0:1]),
                              rd=[b_mm, b_const], wr=[b_mm])
                        fw.op(ACT, lambda e: e.activation(out=scr[:, 0:1], in_=scr[:, 1:2], func=AF.Tanh), rd=[b_const],
                              wr=[b_scr])
                        yield
                        fw.op(DVE, lambda e: e.scalar_tensor_tensor(out=ii[:], in0=ii[:], scalar=1.0, in1=xc[:, xb_, :],
                                                                    op0=ALU.add, op1=ALU.mult),
                              rd=[b_ii, b_xc[xb_]], wr=[b_ii])
                        fw.op(DVE, lambda e: e.tensor_tensor(out=ii[:], in0=ii[:], in1=mm[:], op=ALU.mult),
                              rd=[b_ii, b_mm], wr=[b_ii])
                        fw.op(DVE, lambda e: e.scalar_tensor_tensor(out=rrc[:, 0:NCOL], in0=rrc[:, 0:NCOL], scalar=1.0,
                                                                    in1=gbuf[:, xb_, :], op0=ALU.add, op1=ALU.mult),
                              rd=[b_rrc, b_gbuf[xb_]], wr=[b_rrc])
                        yield
                        fw.op(DVE, lambda e: e.tensor_tensor_scan(out=mm[:, 0:NPP], data0=aa[:, 0:NPP], data1=ii[:, 0:NPP],
                                                                  initial=hst[:, c:c + 1], op0=ALU.mult, op1=ALU.add),
                              rd=[b_aa, b_ii, b_hst[c]], wr=[b_mm])
                        fw.op(DVE, lambda e: e.tensor_tensor_scan(out=mm[:, NPP + 3:LW], data0=aa[:, NPP + 3:LW],
                                                                  data1=ii[:, NPP + 3:LW], initial=sh_sb[:, pas, c:c + 1],
                                                                  op0=ALU.mult, op1=ALU.add),
                              rd=[b_aa, b_ii, b_const], wr=[b_mm])
                        yield
                        fw.op(DVE, lambda e: e.tensor_copy(out=hst[:, c:c + 1], in_=mm[:, NPP - 1:NPP]), rd=[b_mm],
                              wr=[b_hst[c]])
                        if pas == 1:
                            fw.op(DVE, lambda e: e.tensor_copy(out=outst[:, c, 0:1], in_=mm[:, NPP - 1:NPP]),
                                  rd=[b_mm], wr=[b_outst[c]])
                        fw.op(DVE, lambda e: e.tensor_copy(out=outst[:, c, so:so + 1], in_=mm[:, LW - 1:LW]),
                              rd=[b_mm], wr=[b_outst[c]])
                        fw.op(DVE, lambda e: e.scalar_tensor_tensor(out=uT[:, 8 + c, 0:NPP], in0=mm[:, 0:NPP], scalar=0.5,
                                                                    in1=rrc[:, 0:NPP], op0=ALU.mult, op1=ALU.mult),
                              rd=[b_mm, b_rrc], wr=b_uT[8 + c][0:8])
                        fw.op(DVE, lambda e: e.scalar_tensor_tensor(out=uT[:, 8 + c, NPP + NS * pas:NCOL + NS * pas], in0=mm[:, NPP + 3:LW],
                                                                    scalar=0.5, in1=rrc[:, NPP:NCOL], op0=ALU.mult,
                                                                    op1=ALU.mult),
                              rd=[b_mm, b_rrc], wr=[b_uT[8 + c][8]])
                        yield

                    def conv(c):
                        xb_ = c % 2
                        so = 4 + 4 * pas
                        xlv = xl[:, xb_, :]
                        fw.op(DVE, lambda e: e.tensor_copy(out=convst[:, c, :], in_=xlv[:, NPP:NPP + 3]),
                              rd=[b_xl[xb_], b_xlp[xb_]], wr=[b_convst[c]])
                        if pas == 1:
                            fw.op(DVE, lambda e: e.tensor_copy(out=outst[:, c, 1:4], in_=xlv[:, NPP:NPP + 3]),
                                  rd=[b_xl[xb_], b_xlp[xb_]], wr=[b_outst[c]])
                        fw.op(DVE, lambda e: e.tensor_copy(out=outst[:, c, so + 1:so + 4], in_=xlv[:, LW:LW + 3]),
                              rd=[b_xl[xb_], b_xlp[xb_]], wr=[b_outst[c]])
                        fw.op(DVE, lambda e: e.tensor_scalar(out=xc[:, xb_, :], in0=xlv[:, 0:LW], scalar1=cw[:, c, 0:1],
                                                             scalar2=cb[:, c:c + 1], op0=ALU.mult, op1=ALU.add),
                              rd=[b_xl[xb_], b_xlp[xb_], b_const], wr=[b_xc[xb_]])
                        for tap in range(1, 4):
                            fw.op(DVE, lambda e, tap=tap: e.scalar_tensor_tensor(
                                out=xc[:, xb_, :], in0=xlv[:, tap:tap + LW], scalar=cw[:, c, tap:tap + 1], in1=xc[:, xb_, :],
                                op0=ALU.mult, op1=ALU.add), rd=[b_xl[xb_], b_xlp[xb_], b_xc[xb_], b_const], wr=[b_xc[xb_]])
                        fw.op(ACT, lambda e: e.copy(out=xcb[:, xb_, :], in_=xc[:, xb_, :]), rd=[b_xc[xb_]],
                              wr=[b_xcb[xb_]])

                    def make_evac(m):
                        bias = bt[:, m:m + 1]
                        if m < 2:
                            def evac(pa, pb_, lo, hi, gi):
                                dlo = lo if gi < 2 else NPP + 128
                                bufs = [b_KT[m][t] for t in tiles_of(lo, hi)] if gi < 2 else [b_KT[m][9]]
                                fw.op(ACT, lambda e: e.activation(out=KT[:, m, dlo:dlo + hi - lo], in_=pa, func=AF.Identity,
                                                                  bias=bias), rd=[pb_, b_bias], wr=bufs)
                        elif m < 10:
                            def evac(pa, pb_, lo, hi, gi):
                                fw.op(ACT, lambda e: e.activation(out=QT[:, m - 2, lo:hi], in_=pa, func=AF.Identity,
                                                                  bias=bias), rd=[pb_, b_bias], wr=[b_QT[m - 2][gi]])
                        elif m < 18:
                            def evac(pa, pb_, lo, hi, gi):
                                c = m - 10
                                w_ = hi - lo
                                tb = accn["e"] % 2
                                accn["e"] += 1
                                fw.op(ACT, lambda e: e.activation(out=tA[:, tb, 0:w_], in_=pa, func=AF.Tanh,
                                                                  bias=bth[:, m:m + 1], scale=0.5),
                                      rd=[pb_, b_const], wr=[b_tA[tb]])
                                fw.op(DVE, lambda e: e.tensor_scalar(out=tA[:, tb, 0:w_], in0=tA[:, tb, 0:w_], scalar1=0.5,
                                                                     scalar2=0.5, op0=ALU.mult, op1=ALU.add),
                                      rd=[b_tA[tb]], wr=[b_tA[tb]])
                                ulo = lo if gi < 2 else NPP + NS * pas
                                fw.op(DVE, lambda e: e.scalar_tensor_tensor(out=uT[:, c, ulo:ulo + hi - lo], in0=pa, scalar=bias,
                                                                            in1=tA[:, tb, 0:w_], op0=ALU.add, op1=ALU.mult),
                                      rd=[pb_, b_tA[tb], b_const], wr=[b_uT[c][t] for t in tiles_of(lo, hi)])
                        elif m < 26:
                            def evac(pa, pb_, lo, hi, gi):
                                c = m - 18
                                fw.op(ACT, lambda e: e.activation(out=gbuf[:, c % 2, lo:hi], in_=pa, func=AF.Identity,
                                                                  bias=bias), rd=[pb_, b_bias], wr=[b_gbuf[c % 2]])
                        else:
                            def evac(pa, pb_, lo, hi, gi):
                                c = m - 26
                                dlo = 3 + lo if gi < 2 else 3 + NPP + 3
                                fw.op(ACT, lambda e: e.activation(out=xl[:, c % 2, dlo:dlo + hi - lo], in_=pa,
                                                                  func=AF.Identity, bias=bias),
                                      rd=[pb_, b_bias], wr=[b_xl[c % 2]])
                        return evac

                    attn = None
                    chain = None
                    for t in range(4):
                        if pas == 1:
                            xT_tile(pas, t, pbanks=((0, 1) if t % 2 == 0 else (2, 3)), act_only=True)
                        else:
                            xT_tile(pas, t)
                    for t in (4, 5, 6):
                        inproj_group(t - 4, 0, make_evac(morder[t - 4]))
                        xT_tile(pas, t)
                    xT_tile(pas, 7)
                    for i3 in range(3):
                        inproj_group(i3, 1, make_evac(morder[i3]))
                    xT_tile(pas, 8)
                    for i3 in range(3):
                        inproj_group(i3, 2, make_evac(morder[i3]))
                        next_wload(pas)
                    for idx, m in enumerate(morder):
                        if idx < 3:
                            continue
                        slot = idx % 3
                        evac = make_evac(m)
                        if idx == 4 and pas == 0:
                            late_init()
                        if 5 <= idx <= 8:
                            q = idx - 5
                            b_wkvq = B(wkv)
                            fw.dma(POOL, wkv[:, q * 4:(q + 1) * 4, :].rearrange("p k n -> p (k n)"),
                                   wkv_d[:, q * 2048:(q + 1) * 2048], wr=[b_wkvq])
                            fw._merge(b_wkv.w, b_wkvq.w)
                        if idx == 3:
                            fw.dma(POOL, ckst[:], ck[pas], wr=[b_ck])
                            fw.dma(POOL, V1[:, 8, :, 0:64], cv[pas].rearrange("t (h d) -> t h d", h=4), wr=[b_V1[8]])
                            fw.dma(SP, kws[pas, 0:96, :], ck[pas, 32:128, :])
                            fw.dma(SP, vws[pas, 0:96, :], cv[pas, 32:128, :])
                        if idx == 10:
                            kv_phase()
                            st_1a.close()
                            xl = st_lru.enter_context(sb("xl", [128, 2, LW + 3], F32))
                            gbuf = st_lru.enter_context(sb("gbuf", [128, 2, NCOL], F32))
                            xc = st_lru.enter_context(sb("xc", [128, 2, LW], F32))
                            xcb = st_lru.enter_context(sb("xcb", [128, 2, LW], BF16))
                            rr2 = st_lru.enter_context(sb("rr", [128, 2, LW], F32))
                            ii = st_lru.enter_context(sb("ii", [128, LW], F32))
                            aa = st_lru.enter_context(sb("aa", [128, LW], F32))
                            mm = st_lru.enter_context(sb("mm", [128, LW], F32))
                            b_xl, b_gbuf, b_xc, b_xcb = [B(xl), B(xl)], [B(gbuf), B(gbuf)], [B(xc), B(xc)], [B(xcb), B(xcb)]
                            b_xlp = [B(xl), B(xl)]
                            b_rr2, b_ii, b_aa, b_mm = [B(rr2), B(rr2)], B(ii), B(aa), B(mm)
                            fw.op(DVE, lambda e: e.memset(rr2[:], 0.0), wr=b_rr2)
                            fw.op(DVE, lambda e: e.memset(ii[:], 0.0), wr=[b_ii])
                        if m == 12:
                            if chain is not None:
                                for _ in chain:
                                    pass
                                chain = None
                            st_lru.close()
                            accn["banks"] = sorted([2, 3, 4, 5], key=lambda b_: accn["last"][b_])
                            accn["i"] = 0
                            ex = _st6.enter_context(sb("ex", [128, 2, 2, 512], BF16))
                            PT = _st6.enter_context(sb("PT", [128, 4, 2, 512], BF16))
                            osb = _st6.enter_context(sb("osb", [128, 9, 1024], BF16))
                            den = _st6.enter_context(sb("den", [128, 2, 8], F32))
                            b_ex = [[B(ex), B(ex)] for _ in range(2)]
                            b_PT = [[B(PT), B(PT)] for _ in range(4)]
                            b_den = [B(den), B(den)]
                            b_osb = [B(osb) for _ in range(9)]
                            woB = st_wob.enter_context(sb("woB", [128, 8, D], BF16, side="right"))
                            wob_state["tensor"] = woB
                        if m == 12:
                            attn = attention()
                        if m >= 26:
                            c = m - 26
                            xb_ = c % 2
                            fw.op(DVE, lambda e: e.tensor_copy(out=xl[:, xb_, 0:3], in_=convst[:, c, :]),
                                  rd=[b_convst[c]], wr=[b_xlp[xb_]])
                            fw.op(DVE, lambda e: e.tensor_copy(out=xl[:, xb_, 3 + NPP:3 + NPP + 3], in_=sconv[:, pas, c, :]),
                                  rd=[b_const], wr=[b_xlp[xb_]])
                        for gi in range(3):
                            if attn is not None:
                                inproj_group(slot, gi, evac, mid=lambda: next(attn, None))
                                next(attn, None)
                            else:
                                inproj_group(slot, gi, evac)
                            if chain is not None:
                                if next(chain, "done") == "done":
                                    chain = None
                        next_wload(pas)
                        if 12 <= m < 18:
                            for _ in range(2):
                                kq = wob_state["next"]
                                if kq < 8:
                                    b_woB[kq] = B(wob_state["tensor"])
                                    fw.dma(POOL, wob_state["tensor"][:, kq, :], wout_d[:, (8 + kq) * 2048:(9 + kq) * 2048],
                                           wr=[b_woB[kq]])
                                    wob_state["next"] = kq + 1
                        if m >= 26:
                            conv(m - 26)
                        elif 18 <= m < 26:
                            if chain is not None:
                                for _ in chain:
                                    pass
                            chain = lru_chain(m - 18)
                    wo3 = wout_d.rearrange("p (k n) -> p k n", k=16)
                    pre = {}
                    for b_ in b_xT:
                        fw._merge(pre, b_.w)
                        fw._merge(pre, b_.r)
                    allw = {}
                    for cg in range(4):
                        b_woA[cg].w = dict(pre)
                        b_woA[cg].r = {}
                        fw.dma(POOL, woA[:, :, cg * 512:(cg + 1) * 512], wo3[:, 0:8, cg * 512:(cg + 1) * 512],
                               wr=[b_woA[cg]])
                        fw._merge(allw, b_woA[cg].w)
                    for b_ in b_xT:
                        b_.w = dict(allw)
                        b_.r = {}
                    if chain is not None:
                        for _ in chain:
                            pass
                    for _ in attn:
                        pass
                    for t in range(9):
                        attn_tr(t)
                    if pas == 0:
                        fw.op(DVE, lambda e: e.tensor_copy(out=KTh[:], in_=KT[:, :, NPP - 128:NPP]),
                              rd=[b_KT[0][7], b_KT[1][7]], wr=[b_KTh])
                        fw.op(DVE, lambda e: e.tensor_copy(out=V1h[:], in_=V1[:, 7, :, :]), rd=[b_V1[7]], wr=[b_V1h])

            with ExitStack() as _st7:
                lng = _st7.enter_context(sb("lng", [128, D], F32))
                lnb = _st7.enter_context(sb("lnb", [128, D], F32))
                xr = _st7.enter_context(sb("xr", [128, 1, D], F32))
                yy = _st7.enter_context(sb("yy", [128, 3, D], F32))
                stat = _st7.enter_context(sb("stat", [128, 3, 4, 6], F32))
                mv = _st7.enter_context(sb("mv", [128, 3, 4], F32))
                b_ln = [B(lng), B(lnb)]
                b_xr = [B(xr)]
                b_yy, b_stat, b_mv = [B(yy) for _ in range(3)], [B(stat) for _ in range(3)], [B(mv) for _ in range(3)]
                p2tiles = list(range(8)) if pas == 0 else list(range(9))

                def xr_load(t):
                    nr = 128 if t < 8 else 2 * NS
                    src = xp[tok0 + t * 128: tok0 + (t + 1) * 128, :] if t < 8 else xs[0:2 * NS, :]
                    fw.dma(SP, xr[0:nr, 0, :], src, wr=[b_xr[0]])

                xr_load(0)
                fw.dma(SP, lng[:], lng_d, wr=[b_ln[0]])
                fw.dma(SP, lnb[:], lnb_d, wr=[b_ln[1]])
                if pas == 0:
                    for t_ in range(3):
                        xload(1, t_)
                    for _ in range(3):
                        next_wload(1, pre=True)

                def tail(t):
                    nr = 128 if t < 8 else 2 * NS
                    yb = t % 3
                    fw.op(DVE, lambda e: e.tensor_tensor(out=yy[0:nr, yb, :], in0=yy[0:nr, yb, :], in1=lng[0:nr, :],
                                                         op=ALU.mult), rd=[b_yy[yb], b_ln[0]], wr=[b_yy[yb]])
                    fw.op(DVE, lambda e: e.tensor_tensor(out=yy[0:nr, yb, :], in0=yy[0:nr, yb, :], in1=lnb[0:nr, :],
                                                         op=ALU.add), rd=[b_yy[yb], b_ln[1]], wr=[b_yy[yb]])
                    dst = yp[tok0 + t * 128: tok0 + (t + 1) * 128, :] if t < 8 else ys[0:2 * NS, :]
                    fw.dma(SP, dst, yy[0:nr, yb, :], rd=[b_yy[yb]])

                def mm_half(t, cg, first):
                    nr = 128 if t < 8 else 2 * NS
                    c0 = t * 128
                    bank = (t % 2) * 4 + cg
                    for ki, k in enumerate(range(8, 16) if first else range(8)):
                        if k < 8:
                            rhs, rb, nr_ = woA[:, k, cg * 512:(cg + 1) * 512], [b_woA[cg]], b_xT
                        else:
                            rhs, rb, nr_ = woB[:, k - 8, cg * 512:(cg + 1) * 512], [b_woB[k - 8]], ()
                        fw.op(PE, lambda e: e.matmul(py[0:nr, bank, :], lhsT=uT[:, k, c0:c0 + nr], rhs=rhs,
                                                     start=(first and ki == 0), stop=((not first) and ki == 7)),
                              rd=[b_uT[k][t]] + rb, wr=[b_py[bank]], inc=(ki == 7), note_r=nr_)

                for t_ in (0, 1):
                    for cg in range(4):
                        mm_half(t_, cg, True)

                for t in p2tiles:
                    nr = 128 if t < 8 else 2 * NS
                    c0 = t * 128
                    pb = t % 2
                    yb = t % 3
                    for cg in range(4):
                        bank = pb * 4 + cg
                        if t >= 2:
                            mm_half(t, cg, True)
                        mm_half(t, cg, False)
                        fw.op(DVE, lambda e: e.scalar_tensor_tensor(
                            out=yy[0:nr, yb, cg * 512:(cg + 1) * 512], in0=xr[0:nr, 0, cg * 512:(cg + 1) * 512],
                            scalar=ALPHA, in1=py[0:nr, bank, :], op0=ALU.mult, op1=ALU.add),
                            rd=[b_xr[0], b_py[bank]], wr=[b_yy[yb]])
                        fw.op(DVE, lambda e: e.bn_stats(out=stat[0:nr, yb, cg, :], in_=yy[0:nr, yb, cg * 512:(cg + 1) * 512]),
                              rd=[b_yy[yb]], wr=[b_stat[yb]])
                    if t + 1 in p2tiles:
                        xr_load(t + 1)
                    if t == 7 and pas == 0:
                        for t_ in range(3):
                            xT_tile(1, t_, pbanks=((0, 1) if t_ % 2 == 0 else (2, 3)), act_only=True)
                    fw.op(DVE, lambda e: e.bn_aggr(out=mv[0:nr, yb, 0:2],
                                                   in_=stat[0:nr, yb, :, :].rearrange("p a b -> p (a b)")),
                          rd=[b_stat[yb]], wr=[b_mv[yb]])
                    fw.op(ACT, lambda e: e.activation(out=mv[0:nr, yb, 2:3], in_=mv[0:nr, yb, 1:2], func=AF.Sqrt,
                                                      bias=epsb[0:nr, :]),
                          rd=[b_mv[yb], b_const], wr=[b_mv[yb]])
                    fw.op(DVE, lambda e: e.reciprocal(out=mv[0:nr, yb, 2:3], in_=mv[0:nr, yb, 2:3]),
                          rd=[b_mv[yb]], wr=[b_mv[yb]])
                    fw.op(DVE, lambda e: e.scalar_tensor_tensor(out=mv[0:nr, yb, 3:4], in0=mv[0:nr, yb, 0:1], scalar=-1.0,
                                                                in1=mv[0:nr, yb, 2:3], op0=ALU.mult, op1=ALU.mult),
                          rd=[b_mv[yb]], wr=[b_mv[yb]])
                    fw.op(ACT, lambda e: e.activation(out=yy[0:nr, yb, :], in_=yy[0:nr, yb, :], func=AF.Identity,
                                                      scale=mv[0:nr, yb, 2:3], bias=mv[0:nr, yb, 3:4]),
                          rd=[b_yy[yb], b_mv[yb]], wr=[b_yy[yb]])
                    if t >= 1:
                        tail(t - 1)
                tail(p2tiles[-1])
            st_wob.close()

        with ExitStack() as _st8:
            ost = _st8.enter_context(sb("ost", [12, 1024], F32))
            psof = PS[:, 0:2, :].rearrange("p a b -> p (a b)")
            b_ost = B(ost)
            for c in range(8):
                fw.op(PE, lambda e, c=c: e.transpose(psof[0:12, c * 128:(c + 1) * 128], outst[:, c, :], identf[:]),
                      rd=[b_outst[c], b_const], wr=b_ps[0:2], inc=(c == 7))
            fw.op(DVE, lambda e: e.tensor_copy(out=ost[:], in_=psof[0:12, :]),
                  rd=b_ps[0:2], wr=[b_ost])
            fw.dma(SP, hp, ost[0:1, :], rd=[b_ost])
            fw.dma(SP, cvp, ost[1:4, :], rd=[b_ost])
            for s in range(2):
                fw.dma(SP, hs[s:s + 1, :], ost[4 + 4 * s:5 + 4 * s, :], rd=[b_ost])
                fw.dma(SP, cvs[s], ost[5 + 4 * s:8 + 4 * s, :], rd=[b_ost])
            fw.barrier()
    return nc


def _alibi_tables():
    H = 16
    slopes = np.exp2(-8.0 * np.arange(1, H + 1, dtype=np.float64) / H)
    k = np.arange(128)[:, None]
    q = np.arange(128)[None, :]
    NEG = -200.0
    abp = np.zeros((128, 4, 2, 4, 128), np.float32)
    for kv in range(4):
        for g in range(4):
            s = slopes[kv * 4 + g]
            dA = (q + 128 - k).astype(np.float64)
            vA = (k // 64) >= (q // 64)
            abp[:, kv, 0, g, :] = np.where(vA, -s * dA, NEG)
            dB = np.abs(q - k).astype(np.float64)
            vB = (k // 64) <= (q // 64)
            abp[:, kv, 1, g, :] = np.where(vB, -s * dB, NEG)
    q2 = np.arange(32)[None, :]
    abs_ = np.zeros((128, 4, 2, 4, 32), np.float32)
    for kv in range(4):
        for g in range(4):
            s = slopes[kv * 4 + g]
            abs_[:, kv, 0, g, :] = -s * (q2 + 128 - k)
            abs_[:, kv, 1, g, :] = np.where(k < 32, -s * np.abs(q2 - k), NEG)
    return abp.reshape(128, 4096), abs_.reshape(128, 1024)


def _chunk_cols():
    cols = []
    for c in range(2):
        cols.append(np.arange(1024 + c * 128, 1024 + (c + 1) * 128))
    for j in range(2):
        for g in range(4):
            h0 = (2 * j) * 4 + g
            h1 = (2 * j + 1) * 4 + g
            cols.append(np.concatenate([np.arange(h0 * 64, h0 * 64 + 64), np.arange(h1 * 64, h1 * 64 + 64)]))
    for c in range(8):
        cols.append(np.arange(1536 + c * 128, 1536 + (c + 1) * 128))
    for c in range(8):
        cols.append(np.arange(3584 + c * 128, 3584 + (c + 1) * 128))
    for c in range(8):
        cols.append(np.arange(2560 + c * 128, 2560 + (c + 1) * 128))
    return cols


_NC_CACHE = {}


def kernel(x_prompt, x_sample, cache_k, cache_v, state_conv, state_h, w_in, b_in, conv_w, conv_b,
           w_gate_a, b_gate_a, w_gate_x, b_gate_x, lru_lambda, attn_sinks, w_out, ln_g, ln_b):
    f = lambda a: np.ascontiguousarray(np.asarray(a, dtype=np.float32))
    x_prompt, x_sample, cache_k, cache_v = f(x_prompt), f(x_sample), f(cache_k), f(cache_v)
    state_conv, state_h = f(state_conv), f(state_h)
    W = f(w_in)[0]; bi = f(b_in)[0]; Wo = f(w_out)[0]
    cols = _chunk_cols()
    win = np.empty((NCH, 128, 16, 128), np.float32)
    btab = np.empty((128, NCH), np.float32)
    for m, cc in enumerate(cols):
        win[m] = W[:, cc].reshape(16, 128, 128).transpose(1, 0, 2)
        btab[:, m] = bi[cc]
    win = win.reshape(NCH, 128, 2048)
    wkv = np.ascontiguousarray(W[:, 1024:1536].reshape(16, 128, 512).transpose(1, 0, 2)).reshape(128, 8192)
    wout = np.ascontiguousarray(Wo.reshape(16, 128, 2048).transpose(1, 0, 2)).reshape(128, 32768)
    bkv = np.ascontiguousarray(np.broadcast_to(bi[1024:1536], (128, 512)))
    cwt = np.ascontiguousarray(f(conv_w)[0].reshape(4, 8, 128).transpose(2, 1, 0)).reshape(128, 32)
    pc = lambda v: np.ascontiguousarray(f(v).reshape(8, 128).T)
    cbt, lamt = pc(f(conv_b)[0]), pc(f(lru_lambda)[0])
    bgat, bgxt = pc(f(b_gate_a)[0]), pc(f(b_gate_x)[0])
    wgt = np.zeros((128, 2, 8, 128), np.float32)
    for a, wsrc in enumerate((f(w_gate_a)[0], f(w_gate_x)[0])):
        for c in range(8):
            wgt[0:64, a, c, 0:64] = wsrc[2 * c]
            wgt[64:128, a, c, 64:128] = wsrc[2 * c + 1]
    wgt = wgt.reshape(128, 2048)
    lngt = np.ascontiguousarray(np.broadcast_to(f(ln_g)[0], (128, D)))
    lnbt = np.ascontiguousarray(np.broadcast_to(f(ln_b)[0], (128, D)))
    sinkt = np.ascontiguousarray(np.broadcast_to(f(attn_sinks)[0], (128, 16)))
    abp, abs_ = _alibi_tables()
    ident = np.eye(128, dtype=np.float32)

    if "nc" not in _NC_CACHE:
        _NC_CACHE["nc"] = build()
    nc = _NC_CACHE["nc"]

    in_maps = []
    for b in range(NCORES):
        sc = state_conv[0, 2 * b:2 * b + 2]
        sct = np.ascontiguousarray(sc.reshape(2, 3, 8, 128).transpose(3, 0, 2, 1)).reshape(128, 48)
        sht = np.ascontiguousarray(state_h[0, 2 * b:2 * b + 2].reshape(2, 8, 128).transpose(2, 0, 1)).reshape(128, 16)
        in_maps.append({
            "xp": x_prompt[b], "xs": np.ascontiguousarray(x_sample[2 * b:2 * b + 2].reshape(2 * NS, D)),
            "ck": np.ascontiguousarray(cache_k[0, 2 * b:2 * b + 2].reshape(2, 128, 256)),
            "cv": np.ascontiguousarray(cache_v[0, 2 * b:2 * b + 2].reshape(2, 128, 256)),
            "sconv": sct, "sh": sht, "win": win, "wkv": wkv, "wout": wout, "bt": btab, "bkv": bkv,
            "cw": cwt, "cb": cbt, "lam": lamt, "bga": bgat, "bgx": bgxt, "wg": wgt, "lng": lngt, "lnb": lnbt,
            "sinks": sinkt, "abp": abp, "abs": abs_, "ident": ident,
        })
    res = run_bass_kernel_spmd(nc, in_maps, core_ids=list(range(NCORES)))
    R = res.results
    y_p = np.stack([R[b]["yp"] for b in range(NCORES)])
    y_s = np.concatenate([R[b]["ys"].reshape(2, NS, D) for b in range(NCORES)])
    kwp = np.stack([R[b]["kwp"].reshape(128, 4, 64) for b in range(NCORES)])[None]
    vwp = np.stack([R[b]["vwp"].reshape(128, 4, 64) for b in range(NCORES)])[None]
    cvp = np.stack([R[b]["cvp"] for b in range(NCORES)])[None]
    hp = np.concatenate([R[b]["hp"] for b in range(NCORES)])[None]
    kws = np.concatenate([R[b]["kws"].reshape(2, 128, 4, 64) for b in range(NCORES)])[None]
    vws = np.concatenate([R[b]["vws"].reshape(2, 128, 4, 64) for b in range(NCORES)])[None]
    cvs = np.concatenate([R[b]["cvs"] for b in range(NCORES)])[None]
    hs = np.concatenate([R[b]["hs"] for b in range(NCORES)])[None]
    return (y_p, y_s, kwp, vwp, cvp, hp, kws, vws, cvs, hs)
```

```python
import numpy as np
from contextlib import ExitStack
import concourse.bass as bass
import concourse.mybir as mybir
from concourse.bass_utils import run_bass_kernel_spmd

F32 = mybir.dt.float32
BF16 = mybir.dt.bfloat16
AF = mybir.ActivationFunctionType
ALU = mybir.AluOpType

NCORES = 8
D = 2048
SEQ = 2048
NPP = 1024
NS = 32
NCOL = NPP + NS
NCH = 34
LW = NPP + 3 + NS
ALPHA = 2.0 ** 0.25
LN_EPS = 1e-5
GROUPS = [(0, 512), (512, 768), (768, 1056)]
LGROUPS = [(0, 512), (512, 1024), (1027, 1059)]


class Buf:
    __slots__ = ("w", "r", "name", "reg")

    def __init__(self, name=""):
        self.w = {}
        self.r = {}
        self.name = name
        self.reg = None


class Eng:
    def __init__(self, nc, h, name, in_order=False):
        self.h = h
        self.name = name
        self.sem = nc.alloc_semaphore(name="s_" + name)
        self.cnt = 0
        self.seen = {}
        self.in_order = in_order


class Fw:
    def __init__(self, nc, ndma=24):
        self.nc = nc
        self.pe = Eng(nc, nc.tensor, "pe", in_order=True)
        self.act = Eng(nc, nc.scalar, "act")
        self.dve = Eng(nc, nc.vector, "dve")
        self.pool = Eng(nc, nc.gpsimd, "pool")
        self.sp = Eng(nc, nc.sync, "sp")
        self.engs = [self.pe, self.act, self.dve, self.pool, self.sp]
        self.dsem = [nc.alloc_semaphore(name="d%d" % i) for i in range(ndma)]
        self.dcnt = [0] * ndma
        self.dnext = {"hw": 0, "sw": 0}
        self.half = ndma // 2
        self.limit = {}
        self.semobj = {}
        for e in self.engs:
            self.semobj[id(e.sem)] = e.sem
            self.limit[id(e.sem)] = 0
        for s in self.dsem:
            self.semobj[id(s)] = s
            self.limit[id(s)] = 0

    def _wait(self, eng, toks):
        for k, val in toks.items():
            if eng.in_order and k == id(eng.sem):
                continue
            if eng.seen.get(k, 0) >= val:
                continue
            assert val <= self.limit[k], "wait on a value never signalled (%s)" % eng.name
            eng.h.wait_ge(self.semobj[k], val)
            eng.seen[k] = val

    @staticmethod
    def _merge(dst, src):
        for k, v in src.items():
            if dst.get(k, 0) < v:
                dst[k] = v

    def _deps(self, rd, wr):
        toks = {}
        for b in rd:
            self._merge(toks, b.w)
        for b in wr:
            self._merge(toks, b.w)
            self._merge(toks, b.r)
        return toks

    def op(self, eng, fn, rd=(), wr=(), inc=True, note_r=()):
        self._wait(eng, self._deps(rd, wr))
        ins = fn(eng.h)
        k = id(eng.sem)
        if inc:
            eng.cnt += 1
            ins.then_inc(eng.sem, 1)
            self.limit[k] = eng.cnt
            tok = {k: eng.cnt}
        else:
            tok = {k: eng.cnt + 1}
        for b in rd:
            self._merge(b.r, tok)
        for b in note_r:
            self._merge(b.r, tok)
        for b in wr:
            b.w = dict(tok)
            b.r = {}
        return ins

    def dma(self, q, out, in_, rd=(), wr=()):
        self._wait(q, self._deps(rd, wr))
        kind = "sw" if q is self.pool else "hw"
        j = self.dnext[kind] + (self.half if kind == "sw" else 0)
        self.dnext[kind] = (self.dnext[kind] + 1) % self.half
        sem = self.dsem[j]
        k = id(sem)
        if self.dcnt[j] > 0:
            self._wait(q, {k: 16 * self.dcnt[j]})
        q.h.dma_start(out=out, in_=in_).then_inc(sem, 16)
        self.dcnt[j] += 1
        self.limit[k] = 16 * self.dcnt[j]
        tok = {k: 16 * self.dcnt[j]}
        for b in rd:
            self._merge(b.r, tok)
        for b in wr:
            b.w = dict(tok)
            b.r = {}

    def barrier(self):
        toks = {}
        for e in self.engs:
            if e.cnt > 0:
                toks[id(e.sem)] = e.cnt
        for j, s in enumerate(self.dsem):
            if self.dcnt[j] > 0:
                toks[id(s)] = 16 * self.dcnt[j]
        for e in self.engs:
            self._wait(e, toks)


def build():
    nc = bass.Bass("TRN2", target_bir_lowering=False)
    fw = Fw(nc)
    PE, ACT, DVE, POOL, SP = fw.pe, fw.act, fw.dve, fw.pool, fw.sp

    def din(name, shape):
        return nc.dram_tensor(name, shape, F32, kind="ExternalInput").ap()

    def dout(name, shape):
        return nc.dram_tensor(name, shape, F32, kind="ExternalOutput").ap()

    xp = din("xp", [SEQ, D]); xs = din("xs", [2 * NS, D])
    ck = din("ck", [2, 128, 256]); cv = din("cv", [2, 128, 256])
    sconv_d = din("sconv", [128, 48]); sh_d = din("sh", [128, 16])
    win = din("win", [NCH, 128, 2048]); wkv_d = din("wkv", [128, 8192]); wout_d = din("wout", [128, 32768])
    bt_d = din("bt", [128, NCH]); bkv_d = din("bkv", [128, 512])
    cw_d = din("cw", [128, 32]); cb_d = din("cb", [128, 8]); lam_d = din("lam", [128, 8])
    bga_d = din("bga", [128, 8]); bgx_d = din("bgx", [128, 8]); wg_d = din("wg", [128, 2048])
    lng_d = din("lng", [128, D]); lnb_d = din("lnb", [128, D]); sinks_d = din("sinks", [128, 16])
    abp_d = din("abp", [128, 4096]); abs_d = din("abs", [128, 1024]); ident_d = din("ident", [128, 128])

    yp = dout("yp", [SEQ, D]); ys = dout("ys", [2 * NS, D])
    kwp = dout("kwp", [128, 256]); vwp = dout("vwp", [128, 256])
    cvp = dout("cvp", [3, 1024]); hp = dout("hp", [1, 1024])
    kws = dout("kws", [2, 128, 256]); vws = dout("vws", [2, 128, 256])
    cvs = dout("cvs", [2, 3, 1024]); hs = dout("hs", [2, 1024])

    uniq = {"n": 0}

    def sb(name, shape, dt, side=None):
        uniq["n"] += 1
        return nc.sbuf_tensor("s%d_%s" % (uniq["n"], name), shape, dt, side=side)

    def ps(name, shape, dt):
        uniq["n"] += 1
        return nc.psum_tensor("p%d_%s" % (uniq["n"], name), shape, dt)

    with ExitStack() as _st1:
        uT = _st1.enter_context(sb("uT", [128, 16, NCOL + NS], BF16))
        identb = _st1.enter_context(sb("identb", [128, 128], BF16))
        identf = _st1.enter_context(sb("identf", [128, 128], F32))
        bt = _st1.enter_context(sb("bt", [128, NCH], F32))
        bkv = _st1.enter_context(sb("bkv", [128, 512], F32))
        cw = _st1.enter_context(sb("cw", [128, 8, 4], F32))
        cb = _st1.enter_context(sb("cb", [128, 8], F32))
        lamc = _st1.enter_context(sb("lamc", [128, 8], F32))
        bga = _st1.enter_context(sb("bga", [128, 8], F32))
        bgx = _st1.enter_context(sb("bgx", [128, 8], F32))
        wg = _st1.enter_context(sb("wg", [128, 2, 8, 128], BF16))
        Ep = _st1.enter_context(sb("Ep", [128, 4, 2, 4, 128], BF16))
        Es = _st1.enter_context(sb("Es", [128, 4, 2, 4, 32], BF16))
        esink = _st1.enter_context(sb("esink", [128, 16], F32))
        KTh = _st1.enter_context(sb("KTh", [128, 2, 128], BF16))
        V1h = _st1.enter_context(sb("V1h", [128, 4, 65], BF16))
        convst = _st1.enter_context(sb("convst", [128, 8, 3], F32))
        hst = _st1.enter_context(sb("hst", [128, 8], F32))
        outst = _st1.enter_context(sb("outst", [128, 8, 12], F32))
        sconv = _st1.enter_context(sb("sconv", [128, 2, 8, 3], F32))
        sh_sb = _st1.enter_context(sb("sh", [128, 2, 8], F32))
        epsb = _st1.enter_context(sb("epsb", [128, 1], F32))
        qtr = _st1.enter_context(sb("qtr", [128, 1], F32))
        scr = _st1.enter_context(sb("scr", [128, 2], F32))
        bth = _st1.enter_context(sb("bth", [128, NCH], F32))
        wb = _st1.enter_context(sb("wb", [128, 3, 2048], BF16))
        XW = _st1.enter_context(sb("XW", [128, 16 * NCOL], BF16))
        xT = XW[:, :].rearrange("p (k n) -> p k n", k=16)
        woA = XW[:, 0:8 * D].rearrange("p (k n) -> p k n", k=8)

        PS = _st1.enter_context(ps("PS", [128, 8, 512], F32))
        xst = _st1.enter_context(sb("xst", [128, 3, 2048], BF16))
        po = PS[:, 0:2, :]
        ptr = PS[:, 2, :].bitcast(BF16)
        pacc = PS[:, 3:6, :]
        pst = PS[:, 6:8, :]
        ptx = [PS[:, 6, :].bitcast(BF16), PS[:, 7, :].bitcast(BF16)]
        py = PS

        reg = {"all": []}

        def PB():
            return Buf()

        def B(t):
            b_ = Buf()
            ml = nc.lookup_mloc(t)
            b_.reg = (int(ml.addr), int(ml.addr) + int(ml.dims[1]))
            for o in reg["all"]:
                if o.reg is None or (o.reg[0] < b_.reg[1] and b_.reg[0] < o.reg[1]):
                    fw._merge(b_.w, o.w)
                    fw._merge(b_.w, o.r)
            reg["all"].append(b_)
            return b_

        b_ps = [PB() for _ in range(8)]
        b_po, b_ptr, b_pacc, b_ptx = b_ps[0:2], b_ps[2], b_ps[3:6], b_ps[6:8]
        b_pst2 = b_ps[6:8]
        b_py = b_ps
        b_xT = [PB() for _ in range(9)]
        b_xst = [PB(), PB(), PB()]
        b_wb = [PB() for _ in range(3)]
        b_uT = [[PB() for _ in range(9)] for _ in range(16)]
        b_const = PB()
        b_ident = PB()
        b_bias = PB()
        b_scr = PB()
        b_KTh, b_V1h = PB(), PB()
        b_convst = [PB() for _ in range(8)]
        b_hst = [PB() for _ in range(8)]
        b_outst = [PB() for _ in range(8)]

        init_bufs = []

        def IB():
            b_ = Buf()
            init_bufs.append(b_)
            return b_

        st_init = ExitStack()
        stg = st_init.enter_context(sb("stg", [128, 4096], F32, side="right"))
        stg2 = st_init.enter_context(sb("stg2", [128, 1024], F32, side="right"))
        stg3 = st_init.enter_context(sb("stg3", [128, 16], F32, side="right"))
        lamt = st_init.enter_context(sb("lamt", [128, 16], F32, side="right"))
        b_bga, b_bgx = IB(), IB()
        for dst, src in ((identf, ident_d), (bkv, bkv_d), (cb, cb_d)):
            fw.dma(SP, dst[:], src, wr=[IB()])
        fw.dma(SP, bt[:], bt_d, wr=[b_bias])
        fw.op(DVE, lambda e: e.tensor_scalar(out=bth[:], in0=bt[:], scalar1=0.5, scalar2=None, op0=ALU.mult),
              rd=[b_bias], wr=[IB()])
        fw.dma(SP, bga[:], bga_d, wr=[b_bga])
        fw.dma(SP, bgx[:], bgx_d, wr=[b_bgx])
        fw.dma(SP, cw[:].rearrange("p c t -> p (c t)"), cw_d, wr=[IB()])
        fw.dma(SP, sconv[:].rearrange("p s c t -> p (s c t)"), sconv_d, wr=[IB()])
        fw.dma(SP, sh_sb[:].rearrange("p s c -> p (s c)"), sh_d, wr=[IB()])
        fw.dma(POOL, identb[:], ident_d, wr=[b_ident])
        fw.op(DVE, lambda e: e.tensor_scalar(out=bga[:], in0=bga[:], scalar1=0.5, scalar2=None, op0=ALU.mult),
              rd=[b_bga], wr=[b_bga])
        fw.op(DVE, lambda e: e.tensor_scalar(out=bgx[:], in0=bgx[:], scalar1=0.5, scalar2=None, op0=ALU.mult),
              rd=[b_bgx], wr=[b_bgx])
        fw.op(DVE, lambda e: e.memset(qtr[:], 0.25), wr=[IB()])
        fw.op(DVE, lambda e: e.memset(scr[:], 0.0), wr=[IB()])
        fw.op(DVE, lambda e: e.memset(convst[:], 0.0), wr=b_convst)
        fw.op(DVE, lambda e: e.memset(epsb[:], LN_EPS), wr=[IB()])
        fw.op(DVE, lambda e: e.memset(hst[:], 0.0), wr=b_hst)
        fw.op(DVE, lambda e: e.memset(V1h[:], 1.0), wr=[b_V1h])
        fw.op(DVE, lambda e: e.memset(KTh[:], 0.0), wr=[b_KTh])

        def late_init():
            b_stg, b_stg2, b_stg3, b_lam = IB(), IB(), IB(), IB()
            fw.dma(POOL, wg[:].rearrange("p a c e -> p (a c e)"), wg_d, wr=[IB()])
            fw.dma(SP, stg[:], abp_d, wr=[b_stg])
            fw.dma(SP, stg2[:], abs_d, wr=[b_stg2])
            fw.dma(SP, stg3[:], sinks_d, wr=[b_stg3])
            fw.dma(SP, lamt[:, 0:8], lam_d, wr=[b_lam])
            fw.op(ACT, lambda e: e.activation(out=Ep[:].rearrange("p a b c d -> p (a b c d)"), in_=stg[:], func=AF.Exp),
                  rd=[b_stg], wr=[IB()])
            fw.op(ACT, lambda e: e.activation(out=Es[:].rearrange("p a b c d -> p (a b c d)"), in_=stg2[:], func=AF.Exp),
                  rd=[b_stg2], wr=[IB()])
            fw.op(ACT, lambda e: e.activation(out=esink[:], in_=stg3[:], func=AF.Exp), rd=[b_stg3], wr=[IB()])
            fw.op(ACT, lambda e: e.activation(out=lamt[:, 8:16], in_=lamt[:, 0:8], func=AF.Exp, scale=-1.0),
                  rd=[b_lam], wr=[b_lam])
            fw.op(ACT, lambda e: e.activation(out=lamt[:, 0:8], in_=lamt[:, 8:16], func=AF.Ln, bias=1.0),
                  rd=[b_lam], wr=[b_lam])
            fw.op(DVE, lambda e: e.tensor_scalar(out=lamc[:], in0=lamt[:, 0:8], scalar1=-4.0, scalar2=None,
                                                 op0=ALU.mult), rd=[b_lam], wr=[IB()])
            for b_ in init_bufs:
                fw._merge(b_const.w, b_.w)
                fw._merge(b_const.w, b_.r)
            reg["all"].extend(init_bufs)
            st_init.close()

        morder = list(range(10))
        for c in range(8):
            morder += [26 + c, 18 + c]
        morder += list(range(10, 18))
        wstate = [{"next": 0, "pre": 0}, {"next": 0, "pre": 0}]
        xdone = set()

        def next_wload(pas_, pre=False):
            st_ = wstate[pas_]
            if not pre and st_["pre"] > 0:
                st_["pre"] -= 1
                return
            i = st_["next"]
            if i < NCH:
                fw.dma(POOL, wb[:, i % 3, :], win[morder[i]], wr=[b_wb[i % 3]])
                st_["next"] = i + 1
                if pre:
                    st_["pre"] += 1

        def xload(pas_, t):
            if (pas_, "l", t) in xdone:
                return
            xdone.add((pas_, "l", t))
            nr = 128 if t < 8 else NS
            src = xp[pas_ * NPP + t * 128: pas_ * NPP + (t + 1) * 128, :] if t < 8 else xs[pas_ * NS:(pas_ + 1) * NS, :]
            fw.dma(POOL, xst[0:nr, t % 3, :], src, wr=[b_xst[t % 3]])

        def xT_tile(pas_, t, pbanks=(6, 7), act_only=False):
            if (pas_, "t", t) in xdone:
                return
            xdone.add((pas_, "t", t))
            nr = 128 if t < 8 else NS
            c0 = t * 128
            for half in range(2):
                bank = b_ps[pbanks[half]]
                pv = PS[:, pbanks[half], :].bitcast(BF16)
                for kk in range(8):
                    k = half * 8 + kk
                    fw.op(PE, lambda e, k=k, kk=kk, pv=pv: e.transpose(
                        pv[:, kk * 128: kk * 128 + nr], xst[0:nr, t % 3, k * 128:(k + 1) * 128],
                        identb[0:nr, 0:nr]),
                        rd=[b_xst[t % 3], b_ident], wr=[bank], inc=(kk == 7))
                src_ap = pv.rearrange("p (k c) -> p k c", k=8)[:, :, 0:nr]
                dst_ap = xT[:, half * 8:(half + 1) * 8, c0:c0 + nr]
                if half == 0 or act_only:
                    fw.op(ACT, lambda e, s=src_ap, d=dst_ap: e.copy(out=d, in_=s), rd=[bank], wr=[b_xT[t]])
                else:
                    fw.op(DVE, lambda e, s=src_ap, d=dst_ap: e.tensor_copy(out=d, in_=s), rd=[bank],
                          wr=[b_xT[t]])
            if t + 3 < 9:
                xload(pas_, t + 3)
            if pas_ == 0 and t < 3:
                next_wload(0)

        for pas in range(2):
            tok0 = pas * NPP
            st_wob = ExitStack()
            b_woB = [None] * 8
            wob_state = {"next": 0}
            b_woA = [PB() for _ in range(4)]
            with ExitStack() as _st3:
                QT = _st3.enter_context(sb("QT", [128, 8, NCOL], BF16))
                KT = _st3.enter_context(sb("KT", [128, 2, NPP + 128 + NS], BF16))
                V1 = _st3.enter_context(sb("V1", [128, 10, 4, 65], BF16))
                b_QT = [[B(QT) for _ in range(3)] for _ in range(8)]
                b_KT = [[B(KT) for _ in range(10)] for _ in range(2)]
                b_V1 = [B(V1) for _ in range(10)]
                fw.op(DVE, lambda e: e.memset(V1[:], 1.0), wr=b_V1)

                if True:
                    def kv_tokmajor(t):
                        nr = 128 if t < 8 else NS
                        c0 = t * 128
                        need_k = (t == 8) or (pas == 1 and t == 7)
                        lo = 0 if need_k else 256
                        bank = t % 2
                        for k in range(16):
                            fw.op(PE, lambda e, k=k: e.matmul(po[0:nr, bank, lo:512], lhsT=xT[:, k, c0:c0 + nr],
                                                               rhs=wkv[:, k, lo:512], start=(k == 0), stop=(k == 15)),
                                  rd=[b_xT[t], b_wkv], wr=[b_po[bank]], inc=(k == 15))
                        vt = t if t < 8 else 9
                        if need_k:
                            fw.op(DVE, lambda e: e.tensor_tensor(out=kvst[0:nr, bank, :], in0=po[0:nr, bank, :],
                                                                 in1=bkv[0:nr, :], op=ALU.add),
                                  rd=[b_po[bank], b_const], wr=[b_kvst[bank]])
                            fw.op(ACT, lambda e: e.copy(out=V1[0:nr, vt, :, 0:64],
                                                        in_=kvst[0:nr, bank, 256:512].rearrange("p (h d) -> p h d", h=4)),
                                  rd=[b_kvst[bank]], wr=[b_V1[vt]])
                            if t == 8:
                                fw.dma(SP, kws[pas, 96:128, :], kvst[0:NS, bank, 0:256], rd=[b_kvst[bank]])
                                fw.dma(SP, vws[pas, 96:128, :], kvst[0:NS, bank, 256:512], rd=[b_kvst[bank]])
                            else:
                                fw.dma(SP, kwp, kvst[:, bank, 0:256], rd=[b_kvst[bank]])
                                fw.dma(SP, vwp, kvst[:, bank, 256:512], rd=[b_kvst[bank]])
                        else:
                            fw.op(DVE, lambda e: e.tensor_tensor(
                                out=V1[0:nr, vt, :, 0:64],
                                in0=po[0:nr, bank, 256:512].rearrange("p (h d) -> p h d", h=4),
                                in1=bkv[0:nr, 256:512].rearrange("p (h d) -> p h d", h=4), op=ALU.add),
                                rd=[b_po[bank], b_const], wr=[b_V1[vt]])

                    for t_ in range(3):
                        xload(pas, t_)
                    if pas == 1:
                        for _ in range(3):
                            next_wload(pas)

                    def kv_phase():
                        for t in range(9):
                            kv_tokmajor(t)
                        for c in range(2):
                            fw.op(PE, lambda e, c=c: e.transpose(ptr[:, c * 128:(c + 1) * 128],
                                                                 ckst[:, c * 128:(c + 1) * 128], identb[:]),
                                  rd=[b_ck, b_const], wr=[b_ptr], inc=(c == 1))
                        fw.op(DVE, lambda e: e.tensor_copy(out=KT[:, :, NPP:NPP + 128],
                                                           in_=ptr[:, 0:256].rearrange("p (c t) -> p c t", c=2)),
                              rd=[b_ptr], wr=[b_KT[0][8], b_KT[1][8]])


                with ExitStack() as _st6:
                    tA = _st6.enter_context(sb("tA", [128, 2, 512], F32))
                    b_tA = [B(tA), B(tA)]
                    st_1a = ExitStack()
                    wkv = st_1a.enter_context(sb("wkv", [128, 16, 512], BF16))
                    kvst = st_1a.enter_context(sb("kvst", [128, 2, 512], F32))
                    ckst = st_1a.enter_context(sb("ckst", [128, 256], BF16))
                    b_wkv, b_kvst, b_ck = B(wkv), [B(kvst), B(kvst)], B(ckst)
                    st_lru = ExitStack()
                    accn = {"i": 0, "e": 0, "banks": list(range(8)), "last": [0] * 8}

                    def inproj_group(slot, gi, evac, mid=None):
                        lo, hi = GROUPS[gi]
                        bi = accn["banks"][accn["i"] % len(accn["banks"])]
                        accn["i"] += 1
                        accn["last"][bi] = accn["i"]
                        t_lo, t_hi = lo // 128, (hi + 127) // 128
                        for k in range(16):
                            fw.op(PE, lambda e, k=k: e.matmul(PS[:, bi, 0:hi - lo], lhsT=wb[:, slot, k * 128:(k + 1) * 128],
                                                               rhs=xT[:, k, lo:hi], start=(k == 0), stop=(k == 15)),
                                  rd=[b_wb[slot]] + b_xT[t_lo:t_hi], wr=[b_ps[bi]], inc=(k == 15))
                            if k == 7 and mid is not None:
                                mid()
                        evac(PS[:, bi, 0:hi - lo], b_ps[bi], lo, hi, gi)

                    def tiles_of(lo, hi):
                        return range(lo // 128, (hi + 127) // 128)

                    def attn_unit(t, kv):
                        nq = 128 if t < 8 else NS
                        q0 = t * 128
                        j, half = kv // 2, kv % 2
                        pb = slice(half * 64, half * 64 + 64)
                        xb = kv % 2
                        p3 = (t * 4 + kv) % 3
                        pk = kv % 2
                        tiles = []
                        if t < 8:
                            if t == 0:
                                if pas == 1:
                                    tiles.append((KTh[pb, j, :], b_KTh, V1h[:, kv, :], b_V1h, 128, Ep[:, kv, 0]))
                            else:
                                tiles.append((KT[pb, j, (t - 1) * 128:t * 128], b_KT[j][t - 1],
                                              V1[:, t - 1, kv, :], b_V1[t - 1], 128, Ep[:, kv, 0]))
                            tiles.append((KT[pb, j, t * 128:(t + 1) * 128], b_KT[j][t], V1[:, t, kv, :], b_V1[t],
                                          128, Ep[:, kv, 1]))
                        else:
                            tiles.append((KT[pb, j, NPP:NPP + 128], b_KT[j][8], V1[:, 8, kv, :], b_V1[8], 128,
                                          Es[:, kv, 0]))
                            tiles.append((KT[pb, j, NPP + 128:NPP + 128 + NS], b_KT[j][9], V1[0:NS, 9, kv, :],
                                          b_V1[9], NS, Es[0:NS, kv, 1]))
                        nt = len(tiles)

                        def st():
                            for ti, (kt_ap, kt_buf, v_ap, v_buf, nk, e_ap) in enumerate(tiles):
                                fw.op(PE, lambda e: e.matmul(
                                    pst[0:nk, ti, 0:4 * nq].rearrange("p (g q) -> p g q", g=4),
                                    lhsT=kt_ap, rhs=QT[pb, j * 4:(j + 1) * 4, q0:q0 + nq], start=True, stop=True),
                                    rd=[kt_buf] + [b_QT[j * 4 + g][0 if t < 4 else (1 if t < 6 else 2)] for g in range(4)],
                                    wr=[b_pst2[ti]], inc=(ti == nt - 1))
                            for ti, (kt_ap, kt_buf, v_ap, v_buf, nk, e_ap) in enumerate(tiles):
                                fw.op(ACT, lambda e: e.activation(
                                    out=ex[0:nk, xb, ti, 0:4 * nq], in_=pst[0:nk, ti, 0:4 * nq], func=AF.Exp, scale=0.125),
                                    rd=[b_pst2[ti]], wr=[b_ex[xb][ti]])
                                fw.op(DVE, lambda e: e.tensor_tensor(
                                    out=PT[0:nk, p3, ti, 0:4 * nq], in0=ex[0:nk, xb, ti, 0:4 * nq],
                                    in1=e_ap.rearrange("p g q -> p (g q)"), op=ALU.mult),
                                    rd=[b_ex[xb][ti], b_const], wr=[b_PT[p3][ti]])

                        def pvn():
                            for g in range(4):
                                for ti, (kt_ap, kt_buf, v_ap, v_buf, nk, e_ap) in enumerate(tiles):
                                    last = (g == 3 and ti == nt - 1)
                                    fw.op(PE, lambda e: e.matmul(
                                        po[0:nq, pk, g * 65:(g + 1) * 65], lhsT=PT[0:nk, p3, ti, g * nq:(g + 1) * nq],
                                        rhs=v_ap, start=(ti == 0), stop=(ti == nt - 1)),
                                        rd=[b_PT[p3][ti], v_buf], wr=[b_po[pk]], inc=last)
                            pov = po[0:nq, pk, 0:260].rearrange("p (g d) -> p g d", g=4)
                            fw.op(DVE, lambda e: e.tensor_tensor(
                                out=den[0:nq, pk, 0:4], in0=pov[:, :, 64], in1=esink[0:nq, kv * 4:(kv + 1) * 4], op=ALU.add),
                                rd=[b_po[pk], b_const], wr=[b_den[pk]])
                            fw.op(DVE, lambda e: e.reciprocal(out=den[0:nq, pk, 4:8], in_=den[0:nq, pk, 0:4]),
                                  rd=[b_den[pk]], wr=[b_den[pk]])
                            fw.op(DVE, lambda e: e.tensor_tensor(
                                out=osb[0:nq, t, kv * 256:(kv + 1) * 256].rearrange("p (g d) -> p g d", g=4),
                                in0=pov[:, :, 0:64], in1=den[0:nq, pk, 4:8].unsqueeze(2).to_broadcast([nq, 4, 64]),
                                op=ALU.mult), rd=[b_po[pk], b_den[pk]], wr=[b_osb[t]])
                        return st, pvn

                    def attn_tr(t):
                        nq = 128 if t < 8 else NS
                        q0 = t * 128 if t < 8 else NPP + NS * pas
                        bk = 2 + (t % 2)
                        ptb = PS[:, bk, :].bitcast(BF16)
                        for c in range(8):
                            fw.op(PE, lambda e, c=c: e.transpose(ptb[:, c * 128:c * 128 + nq],
                                                                 osb[0:nq, t, c * 128:(c + 1) * 128],
                                                                 identb[0:nq, 0:nq]),
                                  rd=[b_osb[t], b_const], wr=[b_ps[bk]], inc=(c == 7))
                        fw.op(DVE, lambda e: e.tensor_tensor(
                            out=uT[:, 0:8, q0:q0 + nq], in0=ptb.rearrange("p (c q) -> p c q", c=8)[:, :, 0:nq],
                            in1=uT[:, 0:8, q0:q0 + nq], op=ALU.mult),
                            rd=[b_ps[bk]] + [b_uT[c][t] for c in range(8)], wr=[b_uT[c][t] for c in range(8)])

                    def attention():
                        units = [(t, kv) for t in range(9) for kv in range(4)]
                        fns = [attn_unit(t, kv) for (t, kv) in units]
                        for i in range(len(units) + 2):
                            if i < len(units):
                                fns[i][0]()
                            if i >= 2:
                                fns[i - 2][1]()
                            yield

                    def lru_chain(c):
                        xb_ = c % 2
                        so = 4 + 4 * pas
                        rrc, b_rrc = rr2[:, c % 2, :], b_rr2[c % 2]
                        for gate in range(2):
                            dst, dbuf, bia = (rrc, b_rrc, bga) if gate == 0 else (ii, b_ii, bgx)
                            for (lo, hi) in LGROUPS:
                                bi = accn["banks"][accn["i"] % len(accn["banks"])]
                                accn["i"] += 1
                                accn["last"][bi] = accn["i"]
                                fw.op(PE, lambda e: e.matmul(PS[:, bi, 0:hi - lo], lhsT=wg[:, gate, c, :],
                                                             rhs=xcb[:, xb_, lo:hi], start=True, stop=True),
                                      rd=[b_xcb[xb_], b_const], wr=[b_ps[bi]])
                                fw.op(ACT, lambda e: e.activation(out=dst[:, lo:hi], in_=PS[:, bi, 0:hi - lo],
                                                                  func=AF.Tanh, bias=bia[:, c:c + 1], scale=0.5),
                                      rd=[b_ps[bi], b_const], wr=[dbuf])
                            if gate == 0:
                                fw.op(ACT, lambda e: e.activation(out=aa[:], in_=rrc, func=AF.Exp, scale=lamc[:, c:c + 1],
                                                                  bias=lamc[:, c:c + 1]),
                                      rd=[b_rrc, b_const], wr=[b_aa])
                            yield
                        fw.op(ACT, lambda e: e.activation(out=rrc[:, 0:NCOL], in_=gbuf[:, xb_, :], func=AF.Tanh, scale=0.5),
                              rd=[b_gbuf[xb_], b_rrc], wr=[b_rrc])
                        fw.op(ACT, lambda e: e.activation(out=mm[:], in_=aa[:], func=AF.Square), rd=[b_aa], wr=[b_mm])
                        fw.op(ACT, lambda e: e.activation(out=mm[:], in_=mm[:], func=AF.Sqrt, scale=-0.25, bias=qtr[:, 0:1]),
                              rd=[b_mm, b_const], wr=[b_mm])
                        fw.op(ACT, lambda e: e.activation(out=scr[:, 0:1], in_=scr[:, 1:2], func=AF.Tanh), rd=[b_const],
                              wr=[b_scr])
                        yield
                        fw.op(DVE, lambda e: e.scalar_tensor_tensor(out=ii[:], in0=ii[:], scalar=1.0, in1=xc[:, xb_, :],
                                                                    op0=ALU.add, op1=ALU.mult),
                              rd=[b_ii, b_xc[xb_]], wr=[b_ii])
                        fw.op(DVE, lambda e: e.tensor_tensor(out=ii[:], in0=ii[:], in1=mm[:], op=ALU.mult),
                              rd=[b_ii, b_mm], wr=[b_ii])
                        fw.op(DVE, lambda e: e.scalar_tensor_tensor(out=rrc[:, 0:NCOL], in0=rrc[:, 0:NCOL], scalar=1.0,
                                                                    in1=gbuf[:, xb_, :], op0=ALU.add, op1=ALU.mult),
                              rd=[b_rrc, b_gbuf[xb_]], wr=[b_rrc])
                        yield
                        fw.op(DVE, lambda e: e.tensor_tensor_scan(out=mm[:, 0:NPP], data0=aa[:, 0:NPP], data1=ii[:, 0:NPP],
                                                                  initial=hst[:, c:c + 1], op0=ALU.mult, op1=ALU.add),
                              rd=[b_aa, b_ii, b_hst[c]], wr=[b_mm])
                        fw.op(DVE, lambda e: e.tensor_tensor_scan(out=mm[:, NPP + 3:LW], data0=aa[:, NPP + 3:LW],
                                                                  data1=ii[:, NPP + 3:LW], initial=sh_sb[:, pas, c:c + 1],
                                                                  op0=ALU.mult, op1=ALU.add),
                              rd=[b_aa, b_ii, b_const], wr=[b_mm])
                        yield
                        fw.op(DVE, lambda e: e.tensor_copy(out=hst[:, c:c + 1], in_=mm[:, NPP - 1:NPP]), rd=[b_mm],
                              wr=[b_hst[c]])
                        if pas == 1:
                            fw.op(DVE, lambda e: e.tensor_copy(out=outst[:, c, 0:1], in_=mm[:, NPP - 1:NPP]),
                                  rd=[b_mm], wr=[b_outst[c]])
                        fw.op(DVE, lambda e: e.tensor_copy(out=outst[:, c, so:so + 1], in_=mm[:, LW - 1:LW]),
                              rd=[b_mm], wr=[b_outst[c]])
                        fw.op(DVE, lambda e: e.scalar_tensor_tensor(out=uT[:, 8 + c, 0:NPP], in0=mm[:, 0:NPP], scalar=0.5,
                                                                    in1=rrc[:, 0:NPP], op0=ALU.mult, op1=ALU.mult),
                              rd=[b_mm, b_rrc], wr=b_uT[8 + c][0:8])
                        fw.op(DVE, lambda e: e.scalar_tensor_tensor(out=uT[:, 8 + c, NPP + NS * pas:NCOL + NS * pas], in0=mm[:, NPP + 3:LW],
                                                                    scalar=0.5, in1=rrc[:, NPP:NCOL], op0=ALU.mult,
                                                                    op1=ALU.mult),
                              rd=[b_mm, b_rrc], wr=[b_uT[8 + c][8]])
                        yield

                    def conv(c):
                        xb_ = c % 2
                        so = 4 + 4 * pas
                        xlv = xl[:, xb_, :]
                        fw.op(DVE, lambda e: e.tensor_copy(out=convst[:, c, :], in_=xlv[:, NPP:NPP + 3]),
                              rd=[b_xl[xb_], b_xlp[xb_]], wr=[b_convst[c]])
                        if pas == 1:
                            fw.op(DVE, lambda e: e.tensor_copy(out=outst[:, c, 1:4], in_=xlv[:, NPP:NPP + 3]),
                                  rd=[b_xl[xb_], b_xlp[xb_]], wr=[b_outst[c]])
                        fw.op(DVE, lambda e: e.tensor_copy(out=outst[:, c, so + 1:so + 4], in_=xlv[:, LW:LW + 3]),
                              rd=[b_xl[xb_], b_xlp[xb_]], wr=[b_outst[c]])
                        fw.op(DVE, lambda e: e.tensor_scalar(out=xc[:, xb_, :], in0=xlv[:, 0:LW], scalar1=cw[:, c, 0:1],
                                                             scalar2=cb[:, c:c + 1], op0=ALU.mult, op1=ALU.add),
                              rd=[b_xl[xb_], b_xlp[xb_], b_const], wr=[b_xc[xb_]])
                        for tap in range(1, 4):
                            fw.op(DVE, lambda e, tap=tap: e.scalar_tensor_tensor(
                                out=xc[:, xb_, :], in0=xlv[:, tap:tap + LW], scalar=cw[:, c, tap:tap + 1], in1=xc[:, xb_, :],
                                op0=ALU.mult, op1=ALU.add), rd=[b_xl[xb_], b_xlp[xb_], b_xc[xb_], b_const], wr=[b_xc[xb_]])
                        fw.op(ACT, lambda e: e.copy(out=xcb[:, xb_, :], in_=xc[:, xb_, :]), rd=[b_xc[xb_]],
                              wr=[b_xcb[xb_]])

                    def pieces(lo, hi):
                        out = []
                        if lo < NPP:
                            out.append((0, min(hi, NPP) - lo, False))
                        if hi > NPP:
                            out.append((max(lo, NPP) - lo, hi - max(lo, NPP), True))
                        return out

                    def make_evac(m):
                        bias = bt[:, m:m + 1]
                        if m < 2:
                            def evac(pa, pb_, lo, hi, gi):
                                for (po_, w_, smp) in pieces(lo, hi):
                                    dlo = (NPP + 128) if smp else lo
                                    bufs = [b_KT[m][9]] if smp else [b_KT[m][t] for t in tiles_of(lo, min(hi, NPP))]
                                    fw.op(ACT, lambda e: e.activation(out=KT[:, m, dlo:dlo + w_], in_=pa[:, po_:po_ + w_],
                                                                      func=AF.Identity, bias=bias),
                                          rd=[pb_, b_bias], wr=bufs)
                        elif m < 10:
                            def evac(pa, pb_, lo, hi, gi):
                                fw.op(ACT, lambda e: e.activation(out=QT[:, m - 2, lo:hi], in_=pa, func=AF.Identity,
                                                                  bias=bias), rd=[pb_, b_bias], wr=[b_QT[m - 2][gi]])
                        elif m < 18:
                            def evac(pa, pb_, lo, hi, gi):
                                c = m - 10
                                w_all = hi - lo
                                tb = accn["e"] % 2
                                accn["e"] += 1
                                fw.op(ACT, lambda e: e.activation(out=tA[:, tb, 0:w_all], in_=pa, func=AF.Tanh,
                                                                  bias=bth[:, m:m + 1], scale=0.5),
                                      rd=[pb_, b_const], wr=[b_tA[tb]])
                                fw.op(DVE, lambda e: e.tensor_scalar(out=tA[:, tb, 0:w_all], in0=tA[:, tb, 0:w_all], scalar1=0.5,
                                                                     scalar2=0.5, op0=ALU.mult, op1=ALU.add),
                                      rd=[b_tA[tb]], wr=[b_tA[tb]])
                                for (po_, w_, smp) in pieces(lo, hi):
                                    ulo = (NPP + NS * pas) if smp else lo
                                    bufs = [b_uT[c][8]] if smp else [b_uT[c][t] for t in tiles_of(lo, min(hi, NPP))]
                                    fw.op(DVE, lambda e: e.scalar_tensor_tensor(
                                        out=uT[:, c, ulo:ulo + w_], in0=pa[:, po_:po_ + w_], scalar=bias,
                                        in1=tA[:, tb, po_:po_ + w_], op0=ALU.add, op1=ALU.mult),
                                        rd=[pb_, b_tA[tb], b_const], wr=bufs)
                        elif m < 26:
                            def evac(pa, pb_, lo, hi, gi):
                                c = m - 18
                                fw.op(ACT, lambda e: e.activation(out=gbuf[:, c % 2, lo:hi], in_=pa, func=AF.Identity,
                                                                  bias=bias), rd=[pb_, b_bias], wr=[b_gbuf[c % 2]])
                        else:
                            def evac(pa, pb_, lo, hi, gi):
                                c = m - 26
                                for (po_, w_, smp) in pieces(lo, hi):
                                    dlo = (3 + NPP + 3) if smp else 3 + lo
                                    fw.op(ACT, lambda e: e.activation(out=xl[:, c % 2, dlo:dlo + w_], in_=pa[:, po_:po_ + w_],
                                                                      func=AF.Identity, bias=bias),
                                          rd=[pb_, b_bias], wr=[b_xl[c % 2]])
                        return evac

                    attn = None
                    chain = None
                    for t in range(4):
                        if pas == 1:
                            xT_tile(pas, t, pbanks=((0, 1) if t % 2 == 0 else (2, 3)), act_only=True)
                        else:
                            xT_tile(pas, t)
                    inproj_group(0, 0, make_evac(morder[0]))
                    xT_tile(pas, 4)
                    inproj_group(1, 0, make_evac(morder[1]))
                    xT_tile(pas, 5)
                    inproj_group(2, 0, make_evac(morder[2]))
                    inproj_group(0, 1, make_evac(morder[0]))
                    xT_tile(pas, 6)
                    inproj_group(1, 1, make_evac(morder[1]))
                    xT_tile(pas, 7)
                    inproj_group(2, 1, make_evac(morder[2]))
                    xT_tile(pas, 8)
                    for i3 in range(3):
                        inproj_group(i3, 2, make_evac(morder[i3]))
                        next_wload(pas)
                    for idx, m in enumerate(morder):
                        if idx < 3:
                            continue
                        slot = idx % 3
                        evac = make_evac(m)
                        if idx == 4 and pas == 0:
                            late_init()
                        if 5 <= idx <= 8:
                            q = idx - 5
                            b_wkvq = B(wkv)
                            fw.dma(POOL, wkv[:, q * 4:(q + 1) * 4, :].rearrange("p k n -> p (k n)"),
                                   wkv_d[:, q * 2048:(q + 1) * 2048], wr=[b_wkvq])
                            fw._merge(b_wkv.w, b_wkvq.w)
                        if idx == 3:
                            fw.dma(POOL, ckst[:], ck[pas], wr=[b_ck])
                            fw.dma(POOL, V1[:, 8, :, 0:64], cv[pas].rearrange("t (h d) -> t h d", h=4), wr=[b_V1[8]])
                            fw.dma(SP, kws[pas, 0:96, :], ck[pas, 32:128, :])
                            fw.dma(SP, vws[pas, 0:96, :], cv[pas, 32:128, :])
                        if idx == 10:
                            kv_phase()
                            st_1a.close()
                            xl = st_lru.enter_context(sb("xl", [128, 2, LW + 3], F32))
                            gbuf = st_lru.enter_context(sb("gbuf", [128, 2, NCOL], F32))
                            xc = st_lru.enter_context(sb("xc", [128, 2, LW], F32))
                            xcb = st_lru.enter_context(sb("xcb", [128, 2, LW], BF16))
                            rr2 = st_lru.enter_context(sb("rr", [128, 2, LW], F32))
                            ii = st_lru.enter_context(sb("ii", [128, LW], F32))
                            aa = st_lru.enter_context(sb("aa", [128, LW], F32))
                            mm = st_lru.enter_context(sb("mm", [128, LW], F32))
                            b_xl, b_gbuf, b_xc, b_xcb = [B(xl), B(xl)], [B(gbuf), B(gbuf)], [B(xc), B(xc)], [B(xcb), B(xcb)]
                            b_xlp = [B(xl), B(xl)]
                            b_rr2, b_ii, b_aa, b_mm = [B(rr2), B(rr2)], B(ii), B(aa), B(mm)
                            fw.op(DVE, lambda e: e.memset(rr2[:], 0.0), wr=b_rr2)
                            fw.op(DVE, lambda e: e.memset(ii[:], 0.0), wr=[b_ii])
                        if m == 12:
                            if chain is not None:
                                for _ in chain:
                                    pass
                                chain = None
                            st_lru.close()
                            accn["banks"] = sorted([2, 3, 4, 5], key=lambda b_: accn["last"][b_])
                            accn["i"] = 0
                            ex = _st6.enter_context(sb("ex", [128, 2, 2, 512], BF16))
                            PT = _st6.enter_context(sb("PT", [128, 3, 2, 512], BF16))
                            osb = _st6.enter_context(sb("osb", [128, 9, 1024], BF16))
                            den = _st6.enter_context(sb("den", [128, 2, 8], F32))
                            b_ex = [[B(ex), B(ex)] for _ in range(2)]
                            b_PT = [[B(PT), B(PT)] for _ in range(3)]
                            b_den = [B(den), B(den)]
                            b_osb = [B(osb) for _ in range(9)]
                            woB = st_wob.enter_context(sb("woB", [128, 8, D], BF16, side="right"))
                            wob_state["tensor"] = woB
                        if m == 12:
                            attn = attention()
                        if m >= 26:
                            c = m - 26
                            xb_ = c % 2
                            fw.op(DVE, lambda e: e.tensor_copy(out=xl[:, xb_, 0:3], in_=convst[:, c, :]),
                                  rd=[b_convst[c]], wr=[b_xlp[xb_]])
                            fw.op(DVE, lambda e: e.tensor_copy(out=xl[:, xb_, 3 + NPP:3 + NPP + 3], in_=sconv[:, pas, c, :]),
                                  rd=[b_const], wr=[b_xlp[xb_]])
                        for gi in range(3):
                            if attn is not None:
                                inproj_group(slot, gi, evac, mid=lambda: next(attn, None))
                                next(attn, None)
                            else:
                                inproj_group(slot, gi, evac)
                            if chain is not None:
                                if next(chain, "done") == "done":
                                    chain = None
                        next_wload(pas)
                        if 12 <= m < 18:
                            for _ in range(2):
                                kq = wob_state["next"]
                                if kq < 8:
                                    b_woB[kq] = B(wob_state["tensor"])
                                    fw.dma(POOL, wob_state["tensor"][:, kq, :], wout_d[:, (8 + kq) * 2048:(9 + kq) * 2048],
                                           wr=[b_woB[kq]])
                                    wob_state["next"] = kq + 1
                        if m >= 26:
                            conv(m - 26)
                        elif 18 <= m < 26:
                            if chain is not None:
                                for _ in chain:
                                    pass
                            chain = lru_chain(m - 18)
                    wo3 = wout_d.rearrange("p (k n) -> p k n", k=16)
                    pre = {}
                    for b_ in b_xT:
                        fw._merge(pre, b_.w)
                        fw._merge(pre, b_.r)
                    allw = {}
                    for cg in range(4):
                        b_woA[cg].w = dict(pre)
                        b_woA[cg].r = {}
                        fw.dma(POOL, woA[:, :, cg * 512:(cg + 1) * 512], wo3[:, 0:8, cg * 512:(cg + 1) * 512],
                               wr=[b_woA[cg]])
                        fw._merge(allw, b_woA[cg].w)
                    for b_ in b_xT:
                        b_.w = dict(allw)
                        b_.r = {}
                    if chain is not None:
                        for _ in chain:
                            pass
                    for _ in attn:
                        pass
                    for t in range(9):
                        attn_tr(t)
                    if pas == 0:
                        fw.op(DVE, lambda e: e.tensor_copy(out=KTh[:], in_=KT[:, :, NPP - 128:NPP]),
                              rd=[b_KT[0][7], b_KT[1][7]], wr=[b_KTh])
                        fw.op(DVE, lambda e: e.tensor_copy(out=V1h[:], in_=V1[:, 7, :, :]), rd=[b_V1[7]], wr=[b_V1h])

            with ExitStack() as _st7:
                lng = _st7.enter_context(sb("lng", [128, D], F32))
                lnb = _st7.enter_context(sb("lnb", [128, D], F32))
                xr = _st7.enter_context(sb("xr", [128, 1, D], F32))
                yy = _st7.enter_context(sb("yy", [128, 3, D], F32))
                stat = _st7.enter_context(sb("stat", [128, 3, 4, 6], F32))
                mv = _st7.enter_context(sb("mv", [128, 3, 4], F32))
                b_ln = [B(lng), B(lnb)]
                b_xr = [B(xr)]
                b_yy, b_stat, b_mv = [B(yy) for _ in range(3)], [B(stat) for _ in range(3)], [B(mv) for _ in range(3)]
                p2tiles = list(range(8)) if pas == 0 else list(range(9))

                def xr_load(t):
                    nr = 128 if t < 8 else 2 * NS
                    src = xp[tok0 + t * 128: tok0 + (t + 1) * 128, :] if t < 8 else xs[0:2 * NS, :]
                    fw.dma(SP, xr[0:nr, 0, :], src, wr=[b_xr[0]])

                xr_load(0)
                fw.dma(SP, lng[:], lng_d, wr=[b_ln[0]])
                fw.dma(SP, lnb[:], lnb_d, wr=[b_ln[1]])
                if pas == 0:
                    for t_ in range(3):
                        xload(1, t_)
                    for _ in range(3):
                        next_wload(1, pre=True)

                def tail(t):
                    nr = 128 if t < 8 else 2 * NS
                    yb = t % 3
                    fw.op(DVE, lambda e: e.tensor_tensor(out=yy[0:nr, yb, :], in0=yy[0:nr, yb, :], in1=lng[0:nr, :],
                                                         op=ALU.mult), rd=[b_yy[yb], b_ln[0]], wr=[b_yy[yb]])
                    fw.op(DVE, lambda e: e.tensor_tensor(out=yy[0:nr, yb, :], in0=yy[0:nr, yb, :], in1=lnb[0:nr, :],
                                                         op=ALU.add), rd=[b_yy[yb], b_ln[1]], wr=[b_yy[yb]])
                    dst = yp[tok0 + t * 128: tok0 + (t + 1) * 128, :] if t < 8 else ys[0:2 * NS, :]
                    fw.dma(SP, dst, yy[0:nr, yb, :], rd=[b_yy[yb]])

                def mm_half(t, cg, first):
                    nr = 128 if t < 8 else 2 * NS
                    c0 = t * 128
                    bank = (t % 2) * 4 + cg
                    for ki, k in enumerate(range(8, 16) if first else range(8)):
                        if k < 8:
                            rhs, rb, nr_ = woA[:, k, cg * 512:(cg + 1) * 512], [b_woA[cg]], b_xT
                        else:
                            rhs, rb, nr_ = woB[:, k - 8, cg * 512:(cg + 1) * 512], [b_woB[k - 8]], ()
                        fw.op(PE, lambda e: e.matmul(py[0:nr, bank, :], lhsT=uT[:, k, c0:c0 + nr], rhs=rhs,
                                                     start=(first and ki == 0), stop=((not first) and ki == 7)),
                              rd=[b_uT[k][t]] + rb, wr=[b_py[bank]], inc=(ki == 7), note_r=nr_)

                for t_ in (0, 1):
                    for cg in range(4):
                        mm_half(t_, cg, True)

                for t in p2tiles:
                    nr = 128 if t < 8 else 2 * NS
                    c0 = t * 128
                    pb = t % 2
                    yb = t % 3
                    for cg in range(4):
                        bank = pb * 4 + cg
                        if t >= 2:
                            mm_half(t, cg, True)
                        mm_half(t, cg, False)
                        fw.op(DVE, lambda e: e.scalar_tensor_tensor(
                            out=yy[0:nr, yb, cg * 512:(cg + 1) * 512], in0=xr[0:nr, 0, cg * 512:(cg + 1) * 512],
                            scalar=ALPHA, in1=py[0:nr, bank, :], op0=ALU.mult, op1=ALU.add),
                            rd=[b_xr[0], b_py[bank]], wr=[b_yy[yb]])
                        fw.op(DVE, lambda e: e.bn_stats(out=stat[0:nr, yb, cg, :], in_=yy[0:nr, yb, cg * 512:(cg + 1) * 512]),
                              rd=[b_yy[yb]], wr=[b_stat[yb]])
                    if t + 1 in p2tiles:
                        xr_load(t + 1)
                    if t == 7 and pas == 0:
                        for t_ in range(3):
                            xT_tile(1, t_, pbanks=((0, 1) if t_ % 2 == 0 else (2, 3)), act_only=True)
                    fw.op(DVE, lambda e: e.bn_aggr(out=mv[0:nr, yb, 0:2],
                                                   in_=stat[0:nr, yb, :, :].rearrange("p a b -> p (a b)")),
                          rd=[b_stat[yb]], wr=[b_mv[yb]])
                    fw.op(ACT, lambda e: e.activation(out=mv[0:nr, yb, 2:3], in_=mv[0:nr, yb, 1:2], func=AF.Sqrt,
                                                      bias=epsb[0:nr, :]),
                          rd=[b_mv[yb], b_const], wr=[b_mv[yb]])
                    fw.op(DVE, lambda e: e.reciprocal(out=mv[0:nr, yb, 2:3], in_=mv[0:nr, yb, 2:3]),
                          rd=[b_mv[yb]], wr=[b_mv[yb]])
                    fw.op(DVE, lambda e: e.scalar_tensor_tensor(out=mv[0:nr, yb, 3:4], in0=mv[0:nr, yb, 0:1], scalar=-1.0,
                                                                in1=mv[0:nr, yb, 2:3], op0=ALU.mult, op1=ALU.mult),
                          rd=[b_mv[yb]], wr=[b_mv[yb]])
                    fw.op(ACT, lambda e: e.activation(out=yy[0:nr, yb, :], in_=yy[0:nr, yb, :], func=AF.Identity,
                                                      scale=mv[0:nr, yb, 2:3], bias=mv[0:nr, yb, 3:4]),
                          rd=[b_yy[yb], b_mv[yb]], wr=[b_yy[yb]])
                    if t >= 1:
                        tail(t - 1)
                tail(p2tiles[-1])
            st_wob.close()

        with ExitStack() as _st8:
            ost = _st8.enter_context(sb("ost", [12, 1024], F32))
            psof = PS[:, 0:2, :].rearrange("p a b -> p (a b)")
            b_ost = B(ost)
            for c in range(8):
                fw.op(PE, lambda e, c=c: e.transpose(psof[0:12, c * 128:(c + 1) * 128], outst[:, c, :], identf[:]),
                      rd=[b_outst[c], b_const], wr=b_ps[0:2], inc=(c == 7))
            fw.op(DVE, lambda e: e.tensor_copy(out=ost[:], in_=psof[0:12, :]),
                  rd=b_ps[0:2], wr=[b_ost])
            fw.dma(SP, hp, ost[0:1, :], rd=[b_ost])
            fw.dma(SP, cvp, ost[1:4, :], rd=[b_ost])
            for s in range(2):
                fw.dma(SP, hs[s:s + 1, :], ost[4 + 4 * s:5 + 4 * s, :], rd=[b_ost])
                fw.dma(SP, cvs[s], ost[5 + 4 * s:8 + 4 * s, :], rd=[b_ost])
            fw.barrier()
    return nc


def _alibi_tables():
    H = 16
    slopes = np.exp2(-8.0 * np.arange(1, H + 1, dtype=np.float64) / H)
    k = np.arange(128)[:, None]
    q = np.arange(128)[None, :]
    NEG = -200.0
    abp = np.zeros((128, 4, 2, 4, 128), np.float32)
    for kv in range(4):
        for g in range(4):
            s = slopes[kv * 4 + g]
            dA = (q + 128 - k).astype(np.float64)
            vA = (k // 64) >= (q // 64)
            abp[:, kv, 0, g, :] = np.where(vA, -s * dA, NEG)
            dB = np.abs(q - k).astype(np.float64)
            vB = (k // 64) <= (q // 64)
            abp[:, kv, 1, g, :] = np.where(vB, -s * dB, NEG)
    q2 = np.arange(32)[None, :]
    abs_ = np.zeros((128, 4, 2, 4, 32), np.float32)
    for kv in range(4):
        for g in range(4):
            s = slopes[kv * 4 + g]
            abs_[:, kv, 0, g, :] = -s * (q2 + 128 - k)
            abs_[:, kv, 1, g, :] = np.where(k < 32, -s * np.abs(q2 - k), NEG)
    return abp.reshape(128, 4096), abs_.reshape(128, 1024)


def _chunk_cols():
    cols = []
    for c in range(2):
        cols.append(np.arange(1024 + c * 128, 1024 + (c + 1) * 128))
    for j in range(2):
        for g in range(4):
            h0 = (2 * j) * 4 + g
            h1 = (2 * j + 1) * 4 + g
            cols.append(np.concatenate([np.arange(h0 * 64, h0 * 64 + 64), np.arange(h1 * 64, h1 * 64 + 64)]))
    for c in range(8):
        cols.append(np.arange(1536 + c * 128, 1536 + (c + 1) * 128))
    for c in range(8):
        cols.append(np.arange(3584 + c * 128, 3584 + (c + 1) * 128))
    for c in range(8):
        cols.append(np.arange(2560 + c * 128, 2560 + (c + 1) * 128))
    return cols


_NC_CACHE = {}


def kernel(x_prompt, x_sample, cache_k, cache_v, state_conv, state_h, w_in, b_in, conv_w, conv_b,
           w_gate_a, b_gate_a, w_gate_x, b_gate_x, lru_lambda, attn_sinks, w_out, ln_g, ln_b):
    f = lambda a: np.ascontiguousarray(np.asarray(a, dtype=np.float32))
    x_prompt, x_sample, cache_k, cache_v = f(x_prompt), f(x_sample), f(cache_k), f(cache_v)
    state_conv, state_h = f(state_conv), f(state_h)
    W = f(w_in)[0]; bi = f(b_in)[0]; Wo = f(w_out)[0]
    cols = _chunk_cols()
    win = np.empty((NCH, 128, 16, 128), np.float32)
    btab = np.empty((128, NCH), np.float32)
    for m, cc in enumerate(cols):
        win[m] = W[:, cc].reshape(16, 128, 128).transpose(1, 0, 2)
        btab[:, m] = bi[cc]
    win = win.reshape(NCH, 128, 2048)
    wkv = np.ascontiguousarray(W[:, 1024:1536].reshape(16, 128, 512).transpose(1, 0, 2)).reshape(128, 8192)
    wout = np.ascontiguousarray(Wo.reshape(16, 128, 2048).transpose(1, 0, 2)).reshape(128, 32768)
    bkv = np.ascontiguousarray(np.broadcast_to(bi[1024:1536], (128, 512)))
    cwt = np.ascontiguousarray(f(conv_w)[0].reshape(4, 8, 128).transpose(2, 1, 0)).reshape(128, 32)
    pc = lambda v: np.ascontiguousarray(f(v).reshape(8, 128).T)
    cbt, lamt = pc(f(conv_b)[0]), pc(f(lru_lambda)[0])
    bgat, bgxt = pc(f(b_gate_a)[0]), pc(f(b_gate_x)[0])
    wgt = np.zeros((128, 2, 8, 128), np.float32)
    for a, wsrc in enumerate((f(w_gate_a)[0], f(w_gate_x)[0])):
        for c in range(8):
            wgt[0:64, a, c, 0:64] = wsrc[2 * c]
            wgt[64:128, a, c, 64:128] = wsrc[2 * c + 1]
    wgt = wgt.reshape(128, 2048)
    lngt = np.ascontiguousarray(np.broadcast_to(f(ln_g)[0], (128, D)))
    lnbt = np.ascontiguousarray(np.broadcast_to(f(ln_b)[0], (128, D)))
    sinkt = np.ascontiguousarray(np.broadcast_to(f(attn_sinks)[0], (128, 16)))
    abp, abs_ = _alibi_tables()
    ident = np.eye(128, dtype=np.float32)

    if "nc" not in _NC_CACHE:
        _NC_CACHE["nc"] = build()
    nc = _NC_CACHE["nc"]

    in_maps = []
    for b in range(NCORES):
        sc = state_conv[0, 2 * b:2 * b + 2]
        sct = np.ascontiguousarray(sc.reshape(2, 3, 8, 128).transpose(3, 0, 2, 1)).reshape(128, 48)
        sht = np.ascontiguousarray(state_h[0, 2 * b:2 * b + 2].reshape(2, 8, 128).transpose(2, 0, 1)).reshape(128, 16)
        in_maps.append({
            "xp": x_prompt[b], "xs": np.ascontiguousarray(x_sample[2 * b:2 * b + 2].reshape(2 * NS, D)),
            "ck": np.ascontiguousarray(cache_k[0, 2 * b:2 * b + 2].reshape(2, 128, 256)),
            "cv": np.ascontiguousarray(cache_v[0, 2 * b:2 * b + 2].reshape(2, 128, 256)),
            "sconv": sct, "sh": sht, "win": win, "wkv": wkv, "wout": wout, "bt": btab, "bkv": bkv,
            "cw": cwt, "cb": cbt, "lam": lamt, "bga": bgat, "bgx": bgxt, "wg": wgt, "lng": lngt, "lnb": lnbt,
            "sinks": sinkt, "abp": abp, "abs": abs_, "ident": ident,
        })
    res = run_bass_kernel_spmd(nc, in_maps, core_ids=list(range(NCORES)))
    R = res.results
    y_p = np.stack([R[b]["yp"] for b in range(NCORES)])
    y_s = np.concatenate([R[b]["ys"].reshape(2, NS, D) for b in range(NCORES)])
    kwp = np.stack([R[b]["kwp"].reshape(128, 4, 64) for b in range(NCORES)])[None]
    vwp = np.stack([R[b]["vwp"].reshape(128, 4, 64) for b in range(NCORES)])[None]
    cvp = np.stack([R[b]["cvp"] for b in range(NCORES)])[None]
    hp = np.concatenate([R[b]["hp"] for b in range(NCORES)])[None]
    kws = np.concatenate([R[b]["kws"].reshape(2, 128, 4, 64) for b in range(NCORES)])[None]
    vws = np.concatenate([R[b]["vws"].reshape(2, 128, 4, 64) for b in range(NCORES)])[None]
    cvs = np.concatenate([R[b]["cvs"] for b in range(NCORES)])[None]
    hs = np.concatenate([R[b]["hs"] for b in range(NCORES)])[None]
    return (y_p, y_s, kwp, vwp, cvp, hp, kws, vws, cvs, hs)
```

```python
import numpy as np
from contextlib import ExitStack
import concourse.bass as bass
import concourse.mybir as mybir
from concourse.bass_utils import run_bass_kernel_spmd

F32 = mybir.dt.float32
BF16 = mybir.dt.bfloat16
AF = mybir.ActivationFunctionType
ALU = mybir.AluOpType

NCORES = 8
D = 2048
SEQ = 2048
NPP = 1024
NS = 32
NCOL = NPP + NS
NCH = 34
LW = NPP + 3 + NS
ALPHA = 2.0 ** 0.25
LN_EPS = 1e-5
GROUPS = [(0, 512), (512, 768), (768, 1056)]
LGROUPS = [(0, 512), (512, 1024), (1027, 1059)]


class Buf:
    __slots__ = ("w", "r", "name", "reg")

    def __init__(self, name=""):
        self.w = {}
        self.r = {}
        self.name = name
        self.reg = None


class Eng:
    def __init__(self, nc, h, name, in_order=False):
        self.h = h
        self.name = name
        self.sem = nc.alloc_semaphore(name="s_" + name)
        self.cnt = 0
        self.seen = {}
        self.in_order = in_order


class Fw:
    def __init__(self, nc, ndma=24):
        self.nc = nc
        self.pe = Eng(nc, nc.tensor, "pe", in_order=True)
        self.act = Eng(nc, nc.scalar, "act")
        self.dve = Eng(nc, nc.vector, "dve")
        self.pool = Eng(nc, nc.gpsimd, "pool")
        self.sp = Eng(nc, nc.sync, "sp")
        self.engs = [self.pe, self.act, self.dve, self.pool, self.sp]
        self.dsem = [nc.alloc_semaphore(name="d%d" % i) for i in range(ndma)]
        self.dcnt = [0] * ndma
        self.dnext = {"hw": 0, "sw": 0}
        self.half = ndma // 2
        self.limit = {}
        self.semobj = {}
        for e in self.engs:
            self.semobj[id(e.sem)] = e.sem
            self.limit[id(e.sem)] = 0
        for s in self.dsem:
            self.semobj[id(s)] = s
            self.limit[id(s)] = 0

    def _wait(self, eng, toks):
        for k, val in toks.items():
            if eng.in_order and k == id(eng.sem):
                continue
            if eng.seen.get(k, 0) >= val:
                continue
            assert val <= self.limit[k], "wait on a value never signalled (%s)" % eng.name
            eng.h.wait_ge(self.semobj[k], val)
            eng.seen[k] = val

    @staticmethod
    def _merge(dst, src):
        for k, v in src.items():
            if dst.get(k, 0) < v:
                dst[k] = v

    def _deps(self, rd, wr):
        toks = {}
        for b in rd:
            self._merge(toks, b.w)
        for b in wr:
            self._merge(toks, b.w)
            self._merge(toks, b.r)
        return toks

    def op(self, eng, fn, rd=(), wr=(), inc=True, note_r=()):
        self._wait(eng, self._deps(rd, wr))
        ins = fn(eng.h)
        k = id(eng.sem)
        if inc:
            eng.cnt += 1
            ins.then_inc(eng.sem, 1)
            self.limit[k] = eng.cnt
            tok = {k: eng.cnt}
        else:
            tok = {k: eng.cnt + 1}
        for b in rd:
            self._merge(b.r, tok)
        for b in note_r:
            self._merge(b.r, tok)
        for b in wr:
            b.w = dict(tok)
            b.r = {}
        return ins

    def dma(self, q, out, in_, rd=(), wr=()):
        self._wait(q, self._deps(rd, wr))
        kind = "sw" if q is self.pool else "hw"
        j = self.dnext[kind] + (self.half if kind == "sw" else 0)
        self.dnext[kind] = (self.dnext[kind] + 1) % self.half
        sem = self.dsem[j]
        k = id(sem)
        if self.dcnt[j] > 0:
            self._wait(q, {k: 16 * self.dcnt[j]})
        q.h.dma_start(out=out, in_=in_).then_inc(sem, 16)
        self.dcnt[j] += 1
        self.limit[k] = 16 * self.dcnt[j]
        tok = {k: 16 * self.dcnt[j]}
        for b in rd:
            self._merge(b.r, tok)
        for b in wr:
            b.w = dict(tok)
            b.r = {}

    def barrier(self):
        toks = {}
        for e in self.engs:
            if e.cnt > 0:
                toks[id(e.sem)] = e.cnt
        for j, s in enumerate(self.dsem):
            if self.dcnt[j] > 0:
                toks[id(s)] = 16 * self.dcnt[j]
        for e in self.engs:
            self._wait(e, toks)


def build():
    nc = bass.Bass("TRN2", target_bir_lowering=False)
    fw = Fw(nc)
    PE, ACT, DVE, POOL, SP = fw.pe, fw.act, fw.dve, fw.pool, fw.sp

    def din(name, shape):
        return nc.dram_tensor(name, shape, F32, kind="ExternalInput").ap()

    def dout(name, shape):
        return nc.dram_tensor(name, shape, F32, kind="ExternalOutput").ap()

    xp = din("xp", [SEQ, D]); xs = din("xs", [2 * NS, D])
    ck = din("ck", [2, 128, 256]); cv = din("cv", [2, 128, 256])
    sconv_d = din("sconv", [128, 48]); sh_d = din("sh", [128, 16])
    win = din("win", [NCH, 128, 2048]); wkv_d = din("wkv", [128, 8192]); wout_d = din("wout", [128, 32768])
    bt_d = din("bt", [128, NCH]); bkv_d = din("bkv", [128, 512])
    cw_d = din("cw", [128, 32]); cb_d = din("cb", [128, 8]); lam_d = din("lam", [128, 8])
    bga_d = din("bga", [128, 8]); bgx_d = din("bgx", [128, 8]); wg_d = din("wg", [128, 2048])
    lng_d = din("lng", [128, D]); lnb_d = din("lnb", [128, D]); sinks_d = din("sinks", [128, 16])
    abp_d = din("abp", [128, 4096]); abs_d = din("abs", [128, 1024]); ident_d = din("ident", [128, 128])

    yp = dout("yp", [SEQ, D]); ys = dout("ys", [2 * NS, D])
    kwp = dout("kwp", [128, 256]); vwp = dout("vwp", [128, 256])
    cvp = dout("cvp", [3, 1024]); hp = dout("hp", [1, 1024])
    kws = dout("kws", [2, 128, 256]); vws = dout("vws", [2, 128, 256])
    cvs = dout("cvs", [2, 3, 1024]); hs = dout("hs", [2, 1024])

    uniq = {"n": 0}

    def sb(name, shape, dt, side=None):
        uniq["n"] += 1
        return nc.sbuf_tensor("s%d_%s" % (uniq["n"], name), shape, dt, side=side)

    def ps(name, shape, dt):
        uniq["n"] += 1
        return nc.psum_tensor("p%d_%s" % (uniq["n"], name), shape, dt)

    with ExitStack() as _st1:
        uT = _st1.enter_context(sb("uT", [128, 16, NCOL + NS], BF16))
        identb = _st1.enter_context(sb("identb", [128, 128], BF16))
        identf = _st1.enter_context(sb("identf", [128, 128], F32))
        bt = _st1.enter_context(sb("bt", [128, NCH], F32))
        bkv = _st1.enter_context(sb("bkv", [128, 512], F32))
        cw = _st1.enter_context(sb("cw", [128, 8, 4], F32))
        cb = _st1.enter_context(sb("cb", [128, 8], F32))
        lamc = _st1.enter_context(sb("lamc", [128, 8], F32))
        bga = _st1.enter_context(sb("bga", [128, 8], F32))
        bgx = _st1.enter_context(sb("bgx", [128, 8], F32))
        wg = _st1.enter_context(sb("wg", [128, 2, 8, 128], BF16))
        Ep = _st1.enter_context(sb("Ep", [128, 4, 2, 4, 128], BF16))
        Es = _st1.enter_context(sb("Es", [128, 4, 2, 4, 32], BF16))
        esink = _st1.enter_context(sb("esink", [128, 16], F32))
        KTh = _st1.enter_context(sb("KTh", [128, 2, 128], BF16))
        V1h = _st1.enter_context(sb("V1h", [128, 4, 65], BF16))
        convst = _st1.enter_context(sb("convst", [128, 8, 3], F32))
        hst = _st1.enter_context(sb("hst", [128, 8], F32))
        outst = _st1.enter_context(sb("outst", [128, 8, 12], F32))
        sconv = _st1.enter_context(sb("sconv", [128, 2, 8, 3], F32))
        sh_sb = _st1.enter_context(sb("sh", [128, 2, 8], F32))
        epsb = _st1.enter_context(sb("epsb", [128, 1], F32))
        qtr = _st1.enter_context(sb("qtr", [128, 1], F32))
        scr = _st1.enter_context(sb("scr", [128, 2], F32))
        bth = _st1.enter_context(sb("bth", [128, NCH], F32))
        wb = _st1.enter_context(sb("wb", [128, 3, 2048], BF16))
        XW = _st1.enter_context(sb("XW", [128, 16 * NCOL], BF16))
        xT = XW[:, :].rearrange("p (k n) -> p k n", k=16)
        woA = XW[:, 0:8 * D].rearrange("p (k n) -> p k n", k=8)

        PS = _st1.enter_context(ps("PS", [128, 8, 512], F32))
        xst = _st1.enter_context(sb("xst", [128, 3, 2048], BF16))
        po = PS[:, 0:2, :]
        ptr = PS[:, 2, :].bitcast(BF16)
        pacc = PS[:, 3:6, :]
        pst = PS[:, 6:8, :]
        ptx = [PS[:, 6, :].bitcast(BF16), PS[:, 7, :].bitcast(BF16)]
        py = PS

        reg = {"all": []}

        def PB():
            return Buf()

        def B(t):
            b_ = Buf()
            ml = nc.lookup_mloc(t)
            b_.reg = (int(ml.addr), int(ml.addr) + int(ml.dims[1]))
            for o in reg["all"]:
                if o.reg is None or (o.reg[0] < b_.reg[1] and b_.reg[0] < o.reg[1]):
                    fw._merge(b_.w, o.w)
                    fw._merge(b_.w, o.r)
            reg["all"].append(b_)
            return b_

        b_ps = [PB() for _ in range(8)]
        b_po, b_ptr, b_pacc, b_ptx = b_ps[0:2], b_ps[2], b_ps[3:6], b_ps[6:8]
        b_pst2 = b_ps[6:8]
        b_py = b_ps
        b_xT = [PB() for _ in range(9)]
        b_xst = [PB(), PB(), PB()]
        b_wb = [PB() for _ in range(3)]
        b_uT = [[PB() for _ in range(9)] for _ in range(16)]
        b_const = PB()
        b_ident = PB()
        b_bias = PB()
        b_scr = PB()
        b_KTh, b_V1h = PB(), PB()
        b_convst = [PB() for _ in range(8)]
        b_hst = [PB() for _ in range(8)]
        b_outst = [PB() for _ in range(8)]

        init_bufs = []

        def IB():
            b_ = Buf()
            init_bufs.append(b_)
            return b_

        st_init = ExitStack()
        stg = st_init.enter_context(sb("stg", [128, 4096], F32, side="right"))
        stg2 = st_init.enter_context(sb("stg2", [128, 1024], F32, side="right"))
        stg3 = st_init.enter_context(sb("stg3", [128, 16], F32, side="right"))
        lamt = st_init.enter_context(sb("lamt", [128, 16], F32, side="right"))
        b_bga, b_bgx = IB(), IB()
        for dst, src in ((identf, ident_d), (bkv, bkv_d), (cb, cb_d)):
            fw.dma(SP, dst[:], src, wr=[IB()])
        fw.dma(SP, bt[:], bt_d, wr=[b_bias])
        fw.op(DVE, lambda e: e.tensor_scalar(out=bth[:], in0=bt[:], scalar1=0.5, scalar2=None, op0=ALU.mult),
              rd=[b_bias], wr=[IB()])
        fw.dma(SP, bga[:], bga_d, wr=[b_bga])
        fw.dma(SP, bgx[:], bgx_d, wr=[b_bgx])
        fw.dma(SP, cw[:].rearrange("p c t -> p (c t)"), cw_d, wr=[IB()])
        fw.dma(SP, sconv[:].rearrange("p s c t -> p (s c t)"), sconv_d, wr=[IB()])
        fw.dma(SP, sh_sb[:].rearrange("p s c -> p (s c)"), sh_d, wr=[IB()])
        fw.dma(POOL, identb[:], ident_d, wr=[b_ident])
        fw.op(DVE, lambda e: e.tensor_scalar(out=bga[:], in0=bga[:], scalar1=0.5, scalar2=None, op0=ALU.mult),
              rd=[b_bga], wr=[b_bga])
        fw.op(DVE, lambda e: e.tensor_scalar(out=bgx[:], in0=bgx[:], scalar1=0.5, scalar2=None, op0=ALU.mult),
              rd=[b_bgx], wr=[b_bgx])
        fw.op(DVE, lambda e: e.memset(qtr[:], 0.25), wr=[IB()])
        fw.op(DVE, lambda e: e.memset(scr[:], 0.0), wr=[IB()])
        fw.op(DVE, lambda e: e.memset(convst[:], 0.0), wr=b_convst)
        fw.op(DVE, lambda e: e.memset(epsb[:], LN_EPS), wr=[IB()])
        fw.op(DVE, lambda e: e.memset(hst[:], 0.0), wr=b_hst)
        fw.op(DVE, lambda e: e.memset(V1h[:], 1.0), wr=[b_V1h])
        fw.op(DVE, lambda e: e.memset(KTh[:], 0.0), wr=[b_KTh])

        def late_init():
            b_stg, b_stg2, b_stg3, b_lam = IB(), IB(), IB(), IB()
            fw.dma(POOL, wg[:].rearrange("p a c e -> p (a c e)"), wg_d, wr=[IB()])
            fw.dma(SP, stg[:], abp_d, wr=[b_stg])
            fw.dma(SP, stg2[:], abs_d, wr=[b_stg2])
            fw.dma(SP, stg3[:], sinks_d, wr=[b_stg3])
            fw.dma(SP, lamt[:, 0:8], lam_d, wr=[b_lam])
            fw.op(ACT, lambda e: e.activation(out=Ep[:].rearrange("p a b c d -> p (a b c d)"), in_=stg[:], func=AF.Exp),
                  rd=[b_stg], wr=[IB()])
            fw.op(ACT, lambda e: e.activation(out=Es[:].rearrange("p a b c d -> p (a b c d)"), in_=stg2[:], func=AF.Exp),
                  rd=[b_stg2], wr=[IB()])
            fw.op(ACT, lambda e: e.activation(out=esink[:], in_=stg3[:], func=AF.Exp), rd=[b_stg3], wr=[IB()])
            fw.op(ACT, lambda e: e.activation(out=lamt[:, 8:16], in_=lamt[:, 0:8], func=AF.Exp, scale=-1.0),
                  rd=[b_lam], wr=[b_lam])
            fw.op(ACT, lambda e: e.activation(out=lamt[:, 0:8], in_=lamt[:, 8:16], func=AF.Ln, bias=1.0),
                  rd=[b_lam], wr=[b_lam])
            fw.op(DVE, lambda e: e.tensor_scalar(out=lamc[:], in0=lamt[:, 0:8], scalar1=-4.0, scalar2=None,
                                                 op0=ALU.mult), rd=[b_lam], wr=[IB()])
            for b_ in init_bufs:
                fw._merge(b_const.w, b_.w)
                fw._merge(b_const.w, b_.r)
            reg["all"].extend(init_bufs)
            st_init.close()

        morder = list(range(10))
        for c in range(8):
            morder += [26 + c, 18 + c]
        morder += list(range(10, 18))
        wstate = [{"next": 0, "pre": 0}, {"next": 0, "pre": 0}]
        xdone = set()

        def next_wload(pas_, pre=False):
            st_ = wstate[pas_]
            if not pre and st_["pre"] > 0:
                st_["pre"] -= 1
                return
            i = st_["next"]
            if i < NCH:
                fw.dma(POOL, wb[:, i % 3, :], win[morder[i]], wr=[b_wb[i % 3]])
                st_["next"] = i + 1
                if pre:
                    st_["pre"] += 1

        def xload(pas_, t):
            if (pas_, "l", t) in xdone:
                return
            xdone.add((pas_, "l", t))
            nr = 128 if t < 8 else NS
            src = xp[pas_ * NPP + t * 128: pas_ * NPP + (t + 1) * 128, :] if t < 8 else xs[pas_ * NS:(pas_ + 1) * NS, :]
            fw.dma(POOL, xst[0:nr, t % 3, :], src, wr=[b_xst[t % 3]])

        def xT_tile(pas_, t, pbanks=(6, 7), act_only=False):
            if (pas_, "t", t) in xdone:
                return
            xdone.add((pas_, "t", t))
            nr = 128 if t < 8 else NS
            c0 = t * 128
            for half in range(2):
                bank = b_ps[pbanks[half]]
                pv = PS[:, pbanks[half], :].bitcast(BF16)
                for kk in range(8):
                    k = half * 8 + kk
                    fw.op(PE, lambda e, k=k, kk=kk, pv=pv: e.transpose(
                        pv[:, kk * 128: kk * 128 + nr], xst[0:nr, t % 3, k * 128:(k + 1) * 128],
                        identb[0:nr, 0:nr]),
                        rd=[b_xst[t % 3], b_ident], wr=[bank], inc=(kk == 7))
                src_ap = pv.rearrange("p (k c) -> p k c", k=8)[:, :, 0:nr]
                dst_ap = xT[:, half * 8:(half + 1) * 8, c0:c0 + nr]
                if half == 0 or act_only:
                    fw.op(ACT, lambda e, s=src_ap, d=dst_ap: e.copy(out=d, in_=s), rd=[bank], wr=[b_xT[t]])
                else:
                    fw.op(DVE, lambda e, s=src_ap, d=dst_ap: e.tensor_copy(out=d, in_=s), rd=[bank],
                          wr=[b_xT[t]])
            if t + 3 < 9:
                xload(pas_, t + 3)
            if pas_ == 0 and t < 3:
                next_wload(0)

        for pas in range(2):
            tok0 = pas * NPP
            st_wob = ExitStack()
            b_woB = [None] * 8
            wob_state = {"next": 0}
            b_woA = [PB() for _ in range(4)]
            with ExitStack() as _st3:
                QT = _st3.enter_context(sb("QT", [128, 8, NCOL], BF16))
                KT = _st3.enter_context(sb("KT", [128, 2, NPP + 128 + NS], BF16))
                V1 = _st3.enter_context(sb("V1", [128, 10, 4, 65], BF16))
                b_QT = [[B(QT) for _ in range(3)] for _ in range(8)]
                b_KT = [[B(KT) for _ in range(10)] for _ in range(2)]
                b_V1 = [B(V1) for _ in range(10)]
                fw.op(DVE, lambda e: e.memset(V1[:], 1.0), wr=b_V1)

                if True:
                    def kv_tokmajor(t):
                        nr = 128 if t < 8 else NS
                        c0 = t * 128
                        need_k = (t == 8) or (pas == 1 and t == 7)
                        lo = 0 if need_k else 256
                        bank = t % 2
                        for k in range(16):
                            fw.op(PE, lambda e, k=k: e.matmul(po[0:nr, bank, lo:512], lhsT=xT[:, k, c0:c0 + nr],
                                                               rhs=wkv[:, k, lo:512], start=(k == 0), stop=(k == 15)),
                                  rd=[b_xT[t], b_wkv], wr=[b_po[bank]], inc=(k == 15))
                        vt = t if t < 8 else 9
                        if need_k:
                            fw.op(DVE, lambda e: e.tensor_tensor(out=kvst[0:nr, bank, :], in0=po[0:nr, bank, :],
                                                                 in1=bkv[0:nr, :], op=ALU.add),
                                  rd=[b_po[bank], b_const], wr=[b_kvst[bank]])
                            fw.op(ACT, lambda e: e.copy(out=V1[0:nr, vt, :, 0:64],
                                                        in_=kvst[0:nr, bank, 256:512].rearrange("p (h d) -> p h d", h=4)),
                                  rd=[b_kvst[bank]], wr=[b_V1[vt]])
                            if t == 8:
                                fw.dma(SP, kws[pas, 96:128, :], kvst[0:NS, bank, 0:256], rd=[b_kvst[bank]])
                                fw.dma(SP, vws[pas, 96:128, :], kvst[0:NS, bank, 256:512], rd=[b_kvst[bank]])
                            else:
                                fw.dma(SP, kwp, kvst[:, bank, 0:256], rd=[b_kvst[bank]])
                                fw.dma(SP, vwp, kvst[:, bank, 256:512], rd=[b_kvst[bank]])
                        else:
                            fw.op(DVE, lambda e: e.tensor_tensor(
                                out=V1[0:nr, vt, :, 0:64],
                                in0=po[0:nr, bank, 256:512].rearrange("p (h d) -> p h d", h=4),
                                in1=bkv[0:nr, 256:512].rearrange("p (h d) -> p h d", h=4), op=ALU.add),
                                rd=[b_po[bank], b_const], wr=[b_V1[vt]])

                    for t_ in range(3):
                        xload(pas, t_)
                    if pas == 1:
                        for _ in range(3):
                            next_wload(pas)

                    def kv_phase():
                        for t in range(9):
                            kv_tokmajor(t)
                        for c in range(2):
                            fw.op(PE, lambda e, c=c: e.transpose(ptr[:, c * 128:(c + 1) * 128],
                                                                 ckst[:, c * 128:(c + 1) * 128], identb[:]),
                                  rd=[b_ck, b_const], wr=[b_ptr], inc=(c == 1))
                        fw.op(DVE, lambda e: e.tensor_copy(out=KT[:, :, NPP:NPP + 128],
                                                           in_=ptr[:, 0:256].rearrange("p (c t) -> p c t", c=2)),
                              rd=[b_ptr], wr=[b_KT[0][8], b_KT[1][8]])


                with ExitStack() as _st6:
                    tA = _st6.enter_context(sb("tA", [128, 2, 512], F32))
                    b_tA = [B(tA), B(tA)]
                    st_1a = ExitStack()
                    wkv = st_1a.enter_context(sb("wkv", [128, 16, 512], BF16))
                    kvst = st_1a.enter_context(sb("kvst", [128, 2, 512], F32))
                    ckst = st_1a.enter_context(sb("ckst", [128, 256], BF16))
                    b_wkv, b_kvst, b_ck = B(wkv), [B(kvst), B(kvst)], B(ckst)
                    st_lru = ExitStack()
                    accn = {"i": 0, "e": 0, "banks": list(range(8)), "last": [0] * 8}

                    def inproj_group(slot, gi, evac, mid=None):
                        lo, hi = GROUPS[gi]
                        bi = accn["banks"][accn["i"] % len(accn["banks"])]
                        accn["i"] += 1
                        accn["last"][bi] = accn["i"]
                        t_lo, t_hi = lo // 128, (hi + 127) // 128
                        for k in range(16):
                            fw.op(PE, lambda e, k=k: e.matmul(PS[:, bi, 0:hi - lo], lhsT=wb[:, slot, k * 128:(k + 1) * 128],
                                                               rhs=xT[:, k, lo:hi], start=(k == 0), stop=(k == 15)),
                                  rd=[b_wb[slot]] + b_xT[t_lo:t_hi], wr=[b_ps[bi]], inc=(k == 15))
                            if k == 7 and mid is not None:
                                mid()
                        evac(PS[:, bi, 0:hi - lo], b_ps[bi], lo, hi, gi)

                    def tiles_of(lo, hi):
                        return range(lo // 128, (hi + 127) // 128)

                    def attn_unit(t, kv):
                        nq = 128 if t < 8 else NS
                        q0 = t * 128
                        j, half = kv // 2, kv % 2
                        pb = slice(half * 64, half * 64 + 64)
                        xb = kv % 2
                        p3 = (t * 4 + kv) % 3
                        pk = kv % 2
                        tiles = []
                        if t < 8:
                            if t == 0:
                                if pas == 1:
                                    tiles.append((KTh[pb, j, :], b_KTh, V1h[:, kv, :], b_V1h, 128, Ep[:, kv, 0]))
                            else:
                                tiles.append((KT[pb, j, (t - 1) * 128:t * 128], b_KT[j][t - 1],
                                              V1[:, t - 1, kv, :], b_V1[t - 1], 128, Ep[:, kv, 0]))
                            tiles.append((KT[pb, j, t * 128:(t + 1) * 128], b_KT[j][t], V1[:, t, kv, :], b_V1[t],
                                          128, Ep[:, kv, 1]))
                        else:
                            tiles.append((KT[pb, j, NPP:NPP + 128], b_KT[j][8], V1[:, 8, kv, :], b_V1[8], 128,
                                          Es[:, kv, 0]))
                            tiles.append((KT[pb, j, NPP + 128:NPP + 128 + NS], b_KT[j][9], V1[0:NS, 9, kv, :],
                                          b_V1[9], NS, Es[0:NS, kv, 1]))
                        nt = len(tiles)

                        def st():
                            for ti, (kt_ap, kt_buf, v_ap, v_buf, nk, e_ap) in enumerate(tiles):
                                fw.op(PE, lambda e: e.matmul(
                                    pst[0:nk, ti, 0:4 * nq].rearrange("p (g q) -> p g q", g=4),
                                    lhsT=kt_ap, rhs=QT[pb, j * 4:(j + 1) * 4, q0:q0 + nq], start=True, stop=True),
                                    rd=[kt_buf] + [b_QT[j * 4 + g][0 if t < 4 else (1 if t < 6 else 2)] for g in range(4)],
                                    wr=[b_pst2[ti]], inc=(ti == nt - 1))
                            for ti, (kt_ap, kt_buf, v_ap, v_buf, nk, e_ap) in enumerate(tiles):
                                fw.op(ACT, lambda e: e.activation(
                                    out=ex[0:nk, xb, ti, 0:4 * nq], in_=pst[0:nk, ti, 0:4 * nq], func=AF.Exp, scale=0.125),
                                    rd=[b_pst2[ti]], wr=[b_ex[xb][ti]])
                                fw.op(DVE, lambda e: e.tensor_tensor(
                                    out=PT[0:nk, p3, ti, 0:4 * nq], in0=ex[0:nk, xb, ti, 0:4 * nq],
                                    in1=e_ap.rearrange("p g q -> p (g q)"), op=ALU.mult),
                                    rd=[b_ex[xb][ti], b_const], wr=[b_PT[p3][ti]])

                        def pvn():
                            for g in range(4):
                                for ti, (kt_ap, kt_buf, v_ap, v_buf, nk, e_ap) in enumerate(tiles):
                                    last = (g == 3 and ti == nt - 1)
                                    fw.op(PE, lambda e: e.matmul(
                                        po[0:nq, pk, g * 65:(g + 1) * 65], lhsT=PT[0:nk, p3, ti, g * nq:(g + 1) * nq],
                                        rhs=v_ap, start=(ti == 0), stop=(ti == nt - 1)),
                                        rd=[b_PT[p3][ti], v_buf], wr=[b_po[pk]], inc=last)
                            pov = po[0:nq, pk, 0:260].rearrange("p (g d) -> p g d", g=4)
                            fw.op(DVE, lambda e: e.tensor_tensor(
                                out=den[0:nq, pk, 0:4], in0=pov[:, :, 64], in1=esink[0:nq, kv * 4:(kv + 1) * 4], op=ALU.add),
                                rd=[b_po[pk], b_const], wr=[b_den[pk]])
                            fw.op(DVE, lambda e: e.reciprocal(out=den[0:nq, pk, 4:8], in_=den[0:nq, pk, 0:4]),
                                  rd=[b_den[pk]], wr=[b_den[pk]])
                            fw.op(DVE, lambda e: e.tensor_tensor(
                                out=osb[0:nq, t, kv * 256:(kv + 1) * 256].rearrange("p (g d) -> p g d", g=4),
                                in0=pov[:, :, 0:64], in1=den[0:nq, pk, 4:8].unsqueeze(2).to_broadcast([nq, 4, 64]),
                                op=ALU.mult), rd=[b_po[pk], b_den[pk]], wr=[b_osb[t]])
                        return st, pvn

                    def attn_tr(t):
                        nq = 128 if t < 8 else NS
                        q0 = t * 128 if t < 8 else NPP + NS * pas
                        bk = 2 + (t % 2)
                        ptb = PS[:, bk, :].bitcast(BF16)
                        for c in range(8):
                            fw.op(PE, lambda e, c=c: e.transpose(ptb[:, c * 128:c * 128 + nq],
                                                                 osb[0:nq, t, c * 128:(c + 1) * 128],
                                                                 identb[0:nq, 0:nq]),
                                  rd=[b_osb[t], b_const], wr=[b_ps[bk]], inc=(c == 7))
                        fw.op(DVE, lambda e: e.tensor_tensor(
                            out=uT[:, 0:8, q0:q0 + nq], in0=ptb.rearrange("p (c q) -> p c q", c=8)[:, :, 0:nq],
                            in1=uT[:, 0:8, q0:q0 + nq], op=ALU.mult),
                            rd=[b_ps[bk]] + [b_uT[c][t] for c in range(8)], wr=[b_uT[c][t] for c in range(8)])

                    def attention():
                        units = [(t, kv) for t in range(9) for kv in range(4)]
                        fns = [attn_unit(t, kv) for (t, kv) in units]
                        for i in range(len(units) + 2):
                            if i < len(units):
                                fns[i][0]()
                            if i >= 2:
                                fns[i - 2][1]()
                            yield

                    def lru_chain(c):
                        xb_ = c % 2
                        so = 4 + 4 * pas
                        rrc, b_rrc = rr2[:, c % 2, :], b_rr2[c % 2]
                        for gate in range(2):
                            dst, dbuf, bia = (rrc, b_rrc, bga) if gate == 0 else (ii, b_ii, bgx)
                            assert len(accn["banks"]) == 8
                            if accn["i"] % 8 > 5:
                                accn["i"] += 8 - accn["i"] % 8
                            b0 = accn["i"] % 8
                            for gi_, (lo, hi) in enumerate(LGROUPS):
                                bi = b0 + gi_
                                accn["i"] += 1
                                accn["last"][bi] = accn["i"]
                                off = lo - 512 * gi_
                                fw.op(PE, lambda e: e.matmul(PS[:, bi, off:off + hi - lo], lhsT=wg[:, gate, c, :],
                                                             rhs=xcb[:, xb_, lo:hi], start=True, stop=True),
                                      rd=[b_xcb[xb_], b_const], wr=[b_ps[bi]])
                            fw.op(ACT, lambda e: e.activation(
                                out=dst[:, 0:LW], in_=PS[:, b0:b0 + 3, :].rearrange("p a b -> p (a b)")[:, 0:LW],
                                func=AF.Tanh, bias=bia[:, c:c + 1], scale=0.5),
                                rd=b_ps[b0:b0 + 3] + [b_const], wr=[dbuf])
                            if gate == 0:
                                fw.op(ACT, lambda e: e.activation(out=aa[:], in_=rrc, func=AF.Exp, scale=lamc[:, c:c + 1],
                                                                  bias=lamc[:, c:c + 1]),
                                      rd=[b_rrc, b_const], wr=[b_aa])
                            yield
                        fw.op(ACT, lambda e: e.activation(out=rrc[:, 0:NCOL], in_=gbuf[:, xb_, :], func=AF.Tanh, scale=0.5),
                              rd=[b_gbuf[xb_], b_rrc], wr=[b_rrc])
                        fw.op(ACT, lambda e: e.activation(out=mm[:], in_=aa[:], func=AF.Square), rd=[b_aa], wr=[b_mm])
                        fw.op(ACT, lambda e: e.activation(out=mm[:], in_=mm[:], func=AF.Sqrt, scale=-0.25, bias=qtr[:, 0:1]),
                              rd=[b_mm, b_const], wr=[b_mm])
                        fw.op(ACT, lambda e: e.activation(out=scr[:, 0:1], in_=scr[:, 1:2], func=AF.Tanh), rd=[b_const],
                              wr=[b_scr])
                        yield
                        fw.op(DVE, lambda e: e.scalar_tensor_tensor(out=ii[:], in0=ii[:], scalar=1.0, in1=xc[:, xb_, :],
                                                                    op0=ALU.add, op1=ALU.mult),
                              rd=[b_ii, b_xc[xb_]], wr=[b_ii])
                        fw.op(DVE, lambda e: e.tensor_tensor(out=ii[:], in0=ii[:], in1=mm[:], op=ALU.mult),
                              rd=[b_ii, b_mm], wr=[b_ii])
                        fw.op(DVE, lambda e: e.scalar_tensor_tensor(out=rrc[:, 0:NCOL], in0=rrc[:, 0:NCOL], scalar=1.0,
                                                                    in1=gbuf[:, xb_, :], op0=ALU.add, op1=ALU.mult),
                              rd=[b_rrc, b_gbuf[xb_]], wr=[b_rrc])
                        yield
                        fw.op(DVE, lambda e: e.tensor_tensor_scan(out=mm[:, 0:NPP], data0=aa[:, 0:NPP], data1=ii[:, 0:NPP],
                                                                  initial=hst[:, c:c + 1], op0=ALU.mult, op1=ALU.add),
                              rd=[b_aa, b_ii, b_hst[c]], wr=[b_mm])
                        fw.op(DVE, lambda e: e.tensor_tensor_scan(out=mm[:, NPP + 3:LW], data0=aa[:, NPP + 3:LW],
                                                                  data1=ii[:, NPP + 3:LW], initial=sh_sb[:, pas, c:c + 1],
                                                                  op0=ALU.mult, op1=ALU.add),
                              rd=[b_aa, b_ii, b_const], wr=[b_mm])
                        yield
                        fw.op(DVE, lambda e: e.tensor_copy(out=hst[:, c:c + 1], in_=mm[:, NPP - 1:NPP]), rd=[b_mm],
                              wr=[b_hst[c]])
                        if pas == 1:
                            fw.op(DVE, lambda e: e.tensor_copy(out=outst[:, c, 0:1], in_=mm[:, NPP - 1:NPP]),
                                  rd=[b_mm], wr=[b_outst[c]])
                        fw.op(DVE, lambda e: e.tensor_copy(out=outst[:, c, so:so + 1], in_=mm[:, LW - 1:LW]),
                              rd=[b_mm], wr=[b_outst[c]])
                        fw.op(DVE, lambda e: e.scalar_tensor_tensor(out=uT[:, 8 + c, 0:NPP], in0=mm[:, 0:NPP], scalar=0.5,
                                                                    in1=rrc[:, 0:NPP], op0=ALU.mult, op1=ALU.mult),
                              rd=[b_mm, b_rrc], wr=b_uT[8 + c][0:8])
                        fw.op(DVE, lambda e: e.scalar_tensor_tensor(out=uT[:, 8 + c, NPP + NS * pas:NCOL + NS * pas], in0=mm[:, NPP + 3:LW],
                                                                    scalar=0.5, in1=rrc[:, NPP:NCOL], op0=ALU.mult,
                                                                    op1=ALU.mult),
                              rd=[b_mm, b_rrc], wr=[b_uT[8 + c][8]])
                        yield

                    def conv(c):
                        xb_ = c % 2
                        so = 4 + 4 * pas
                        xlv = xl[:, xb_, :]
                        fw.op(DVE, lambda e: e.tensor_copy(out=convst[:, c, :], in_=xlv[:, NPP:NPP + 3]),
                              rd=[b_xl[xb_], b_xlp[xb_]], wr=[b_convst[c]])
                        if pas == 1:
                            fw.op(DVE, lambda e: e.tensor_copy(out=outst[:, c, 1:4], in_=xlv[:, NPP:NPP + 3]),
                                  rd=[b_xl[xb_], b_xlp[xb_]], wr=[b_outst[c]])
                        fw.op(DVE, lambda e: e.tensor_copy(out=outst[:, c, so + 1:so + 4], in_=xlv[:, LW:LW + 3]),
                              rd=[b_xl[xb_], b_xlp[xb_]], wr=[b_outst[c]])
                        fw.op(DVE, lambda e: e.tensor_scalar(out=xc[:, xb_, :], in0=xlv[:, 0:LW], scalar1=cw[:, c, 0:1],
                                                             scalar2=cb[:, c:c + 1], op0=ALU.mult, op1=ALU.add),
                              rd=[b_xl[xb_], b_xlp[xb_], b_const], wr=[b_xc[xb_]])
                        for tap in range(1, 4):
                            fw.op(DVE, lambda e, tap=tap: e.scalar_tensor_tensor(
                                out=xc[:, xb_, :], in0=xlv[:, tap:tap + LW], scalar=cw[:, c, tap:tap + 1], in1=xc[:, xb_, :],
                                op0=ALU.mult, op1=ALU.add), rd=[b_xl[xb_], b_xlp[xb_], b_xc[xb_], b_const], wr=[b_xc[xb_]])
                        fw.op(ACT, lambda e: e.copy(out=xcb[:, xb_, :], in_=xc[:, xb_, :]), rd=[b_xc[xb_]],
                              wr=[b_xcb[xb_]])

                    def pieces(lo, hi):
                        out = []
                        if lo < NPP:
                            out.append((0, min(hi, NPP) - lo, False))
                        if hi > NPP:
                            out.append((max(lo, NPP) - lo, hi - max(lo, NPP), True))
                        return out

                    def make_evac(m):
                        bias = bt[:, m:m + 1]
                        if m < 2:
                            def evac(pa, pb_, lo, hi, gi):
                                for (po_, w_, smp) in pieces(lo, hi):
                                    dlo = (NPP + 128) if smp else lo
                                    bufs = [b_KT[m][9]] if smp else [b_KT[m][t] for t in tiles_of(lo, min(hi, NPP))]
                                    fw.op(ACT, lambda e: e.activation(out=KT[:, m, dlo:dlo + w_], in_=pa[:, po_:po_ + w_],
                                                                      func=AF.Identity, bias=bias),
                                          rd=[pb_, b_bias], wr=bufs)
                        elif m < 10:
                            def evac(pa, pb_, lo, hi, gi):
                                fw.op(ACT, lambda e: e.activation(out=QT[:, m - 2, lo:hi], in_=pa, func=AF.Identity,
                                                                  bias=bias), rd=[pb_, b_bias], wr=[b_QT[m - 2][gi]])
                        elif m < 18:
                            def evac(pa, pb_, lo, hi, gi):
                                c = m - 10
                                w_all = hi - lo
                                tb = accn["e"] % 2
                                accn["e"] += 1
                                fw.op(ACT, lambda e: e.activation(out=tA[:, tb, 0:w_all], in_=pa, func=AF.Tanh,
                                                                  bias=bth[:, m:m + 1], scale=0.5),
                                      rd=[pb_, b_const], wr=[b_tA[tb]])
                                fw.op(DVE, lambda e: e.tensor_scalar(out=tA[:, tb, 0:w_all], in0=tA[:, tb, 0:w_all], scalar1=0.5,
                                                                     scalar2=0.5, op0=ALU.mult, op1=ALU.add),
                                      rd=[b_tA[tb]], wr=[b_tA[tb]])
                                for (po_, w_, smp) in pieces(lo, hi):
                                    ulo = (NPP + NS * pas) if smp else lo
                                    bufs = [b_uT[c][8]] if smp else [b_uT[c][t] for t in tiles_of(lo, min(hi, NPP))]
                                    fw.op(DVE, lambda e: e.scalar_tensor_tensor(
                                        out=uT[:, c, ulo:ulo + w_], in0=pa[:, po_:po_ + w_], scalar=bias,
                                        in1=tA[:, tb, po_:po_ + w_], op0=ALU.add, op1=ALU.mult),
                                        rd=[pb_, b_tA[tb], b_const], wr=bufs)
                        elif m < 26:
                            def evac(pa, pb_, lo, hi, gi):
                                c = m - 18
                                fw.op(ACT, lambda e: e.activation(out=gbuf[:, c % 2, lo:hi], in_=pa, func=AF.Identity,
                                                                  bias=bias), rd=[pb_, b_bias], wr=[b_gbuf[c % 2]])
                        else:
                            def evac(pa, pb_, lo, hi, gi):
                                c = m - 26
                                for (po_, w_, smp) in pieces(lo, hi):
                                    dlo = (3 + NPP + 3) if smp else 3 + lo
                                    fw.op(ACT, lambda e: e.activation(out=xl[:, c % 2, dlo:dlo + w_], in_=pa[:, po_:po_ + w_],
                                                                      func=AF.Identity, bias=bias),
                                          rd=[pb_, b_bias], wr=[b_xl[c % 2]])
                        return evac

                    attn = None
                    chain = None
                    for t in range(4):
                        if pas == 1:
                            xT_tile(pas, t, pbanks=((0, 1) if t % 2 == 0 else (2, 3)), act_only=True)
                        else:
                            xT_tile(pas, t)
                    inproj_group(0, 0, make_evac(morder[0]))
                    xT_tile(pas, 4)
                    inproj_group(1, 0, make_evac(morder[1]))
                    xT_tile(pas, 5)
                    inproj_group(2, 0, make_evac(morder[2]))
                    inproj_group(0, 1, make_evac(morder[0]))
                    xT_tile(pas, 6)
                    inproj_group(1, 1, make_evac(morder[1]))
                    xT_tile(pas, 7)
                    inproj_group(2, 1, make_evac(morder[2]))
                    xT_tile(pas, 8)
                    for i3 in range(3):
                        inproj_group(i3, 2, make_evac(morder[i3]))
                        next_wload(pas)
                    for idx, m in enumerate(morder):
                        if idx < 3:
                            continue
                        slot = idx % 3
                        evac = make_evac(m)
                        if idx == 4 and pas == 0:
                            late_init()
                        if 5 <= idx <= 8:
                            q = idx - 5
                            b_wkvq = B(wkv)
                            fw.dma(POOL, wkv[:, q * 4:(q + 1) * 4, :].rearrange("p k n -> p (k n)"),
                                   wkv_d[:, q * 2048:(q + 1) * 2048], wr=[b_wkvq])
                            fw._merge(b_wkv.w, b_wkvq.w)
                        if idx == 3:
                            fw.dma(POOL, ckst[:], ck[pas], wr=[b_ck])
                            fw.dma(POOL, V1[:, 8, :, 0:64], cv[pas].rearrange("t (h d) -> t h d", h=4), wr=[b_V1[8]])
                            fw.dma(SP, kws[pas, 0:96, :], ck[pas, 32:128, :])
                            fw.dma(SP, vws[pas, 0:96, :], cv[pas, 32:128, :])
                        if idx == 10:
                            kv_phase()
                            st_1a.close()
                            xl = st_lru.enter_context(sb("xl", [128, 2, LW + 3], F32))
                            gbuf = st_lru.enter_context(sb("gbuf", [128, 2, NCOL], F32))
                            xc = st_lru.enter_context(sb("xc", [128, 2, LW], F32))
                            xcb = st_lru.enter_context(sb("xcb", [128, 2, LW], BF16))
                            rr2 = st_lru.enter_context(sb("rr", [128, 2, LW], F32))
                            ii = st_lru.enter_context(sb("ii", [128, LW], F32))
                            aa = st_lru.enter_context(sb("aa", [128, LW], F32))
                            mm = st_lru.enter_context(sb("mm", [128, LW], F32))
                            b_xl, b_gbuf, b_xc, b_xcb = [B(xl), B(xl)], [B(gbuf), B(gbuf)], [B(xc), B(xc)], [B(xcb), B(xcb)]
                            b_xlp = [B(xl), B(xl)]
                            b_rr2, b_ii, b_aa, b_mm = [B(rr2), B(rr2)], B(ii), B(aa), B(mm)
                            fw.op(DVE, lambda e: e.memset(rr2[:], 0.0), wr=b_rr2)
                            fw.op(DVE, lambda e: e.memset(ii[:], 0.0), wr=[b_ii])
                        if m == 12:
                            if chain is not None:
                                for _ in chain:
                                    pass
                                chain = None
                            st_lru.close()
                            accn["banks"] = sorted([2, 3, 4, 5], key=lambda b_: accn["last"][b_])
                            accn["i"] = 0
                            ex = _st6.enter_context(sb("ex", [128, 2, 2, 512], BF16))
                            PT = _st6.enter_context(sb("PT", [128, 3, 2, 512], BF16))
                            osb = _st6.enter_context(sb("osb", [128, 9, 1024], BF16))
                            den = _st6.enter_context(sb("den", [128, 2, 8], F32))
                            b_ex = [[B(ex), B(ex)] for _ in range(2)]
                            b_PT = [[B(PT), B(PT)] for _ in range(3)]
                            b_den = [B(den), B(den)]
                            b_osb = [B(osb) for _ in range(9)]
                            woB = st_wob.enter_context(sb("woB", [128, 8, D], BF16, side="right"))
                            wob_state["tensor"] = woB
                        if m == 12:
                            attn = attention()
                        if m >= 26:
                            c = m - 26
                            xb_ = c % 2
                            fw.op(DVE, lambda e: e.tensor_copy(out=xl[:, xb_, 0:3], in_=convst[:, c, :]),
                                  rd=[b_convst[c]], wr=[b_xlp[xb_]])
                            fw.op(DVE, lambda e: e.tensor_copy(out=xl[:, xb_, 3 + NPP:3 + NPP + 3], in_=sconv[:, pas, c, :]),
                                  rd=[b_const], wr=[b_xlp[xb_]])
                        for gi in range(3):
                            if attn is not None:
                                inproj_group(slot, gi, evac, mid=lambda: next(attn, None))
                                next(attn, None)
                            else:
                                inproj_group(slot, gi, evac)
                            if chain is not None:
                                if next(chain, "done") == "done":
                                    chain = None
                        next_wload(pas)
                        if 12 <= m < 18:
                            for _ in range(2):
                                kq = wob_state["next"]
                                if kq < 8:
                                    b_woB[kq] = B(wob_state["tensor"])
                                    fw.dma(POOL, wob_state["tensor"][:, kq, :], wout_d[:, (8 + kq) * 2048:(9 + kq) * 2048],
                                           wr=[b_woB[kq]])
                                    wob_state["next"] = kq + 1
                        if m >= 26:
                            conv(m - 26)
                        elif 18 <= m < 26:
                            if chain is not None:
                                for _ in chain:
                                    pass
                            chain = lru_chain(m - 18)
                    wo3 = wout_d.rearrange("p (k n) -> p k n", k=16)
                    pre = {}
                    for b_ in b_xT:
                        fw._merge(pre, b_.w)
                        fw._merge(pre, b_.r)
                    allw = {}
                    for cg in range(4):
                        b_woA[cg].w = dict(pre)
                        b_woA[cg].r = {}
                        fw.dma(POOL, woA[:, :, cg * 512:(cg + 1) * 512], wo3[:, 0:8, cg * 512:(cg + 1) * 512],
                               wr=[b_woA[cg]])
                        fw._merge(allw, b_woA[cg].w)
                    for b_ in b_xT:
                        b_.w = dict(allw)
                        b_.r = {}
                    if chain is not None:
                        for _ in chain:
                            pass
                    for _ in attn:
                        pass
                    for t in range(9):
                        attn_tr(t)
                    if pas == 0:
                        fw.op(DVE, lambda e: e.tensor_copy(out=KTh[:], in_=KT[:, :, NPP - 128:NPP]),
                              rd=[b_KT[0][7], b_KT[1][7]], wr=[b_KTh])
                        fw.op(DVE, lambda e: e.tensor_copy(out=V1h[:], in_=V1[:, 7, :, :]), rd=[b_V1[7]], wr=[b_V1h])

            with ExitStack() as _st7:
                lng = _st7.enter_context(sb("lng", [128, D], F32))
                lnb = _st7.enter_context(sb("lnb", [128, D], F32))
                xr = _st7.enter_context(sb("xr", [128, 1, D], F32))
                yy = _st7.enter_context(sb("yy", [128, 3, D], F32))
                stat = _st7.enter_context(sb("stat", [128, 3, 4, 6], F32))
                mv = _st7.enter_context(sb("mv", [128, 3, 4], F32))
                b_ln = [B(lng), B(lnb)]
                b_xr = [B(xr)]
                b_yy, b_stat, b_mv = [B(yy) for _ in range(3)], [B(stat) for _ in range(3)], [B(mv) for _ in range(3)]
                p2tiles = list(range(8)) if pas == 0 else list(range(9))

                def xr_load(t):
                    nr = 128 if t < 8 else 2 * NS
                    src = xp[tok0 + t * 128: tok0 + (t + 1) * 128, :] if t < 8 else xs[0:2 * NS, :]
                    fw.dma(SP, xr[0:nr, 0, :], src, wr=[b_xr[0]])

                xr_load(0)
                fw.dma(SP, lng[:], lng_d, wr=[b_ln[0]])
                fw.dma(SP, lnb[:], lnb_d, wr=[b_ln[1]])
                if pas == 0:
                    for t_ in range(3):
                        xload(1, t_)
                    for _ in range(3):
                        next_wload(1, pre=True)

                def tail(t):
                    nr = 128 if t < 8 else 2 * NS
                    yb = t % 3
                    fw.op(DVE, lambda e: e.tensor_tensor(out=yy[0:nr, yb, :], in0=yy[0:nr, yb, :], in1=lng[0:nr, :],
                                                         op=ALU.mult), rd=[b_yy[yb], b_ln[0]], wr=[b_yy[yb]])
                    fw.op(DVE, lambda e: e.tensor_tensor(out=yy[0:nr, yb, :], in0=yy[0:nr, yb, :], in1=lnb[0:nr, :],
                                                         op=ALU.add), rd=[b_yy[yb], b_ln[1]], wr=[b_yy[yb]])
                    dst = yp[tok0 + t * 128: tok0 + (t + 1) * 128, :] if t < 8 else ys[0:2 * NS, :]
                    fw.dma(SP, dst, yy[0:nr, yb, :], rd=[b_yy[yb]])

                def mm_half(t, cg, first):
                    nr = 128 if t < 8 else 2 * NS
                    c0 = t * 128
                    bank = (t % 2) * 4 + cg
                    for ki, k in enumerate(range(8, 16) if first else range(8)):
                        if k < 8:
                            rhs, rb, nr_ = woA[:, k, cg * 512:(cg + 1) * 512], [b_woA[cg]], b_xT
                        else:
                            rhs, rb, nr_ = woB[:, k - 8, cg * 512:(cg + 1) * 512], [b_woB[k - 8]], ()
                        fw.op(PE, lambda e: e.matmul(py[0:nr, bank, :], lhsT=uT[:, k, c0:c0 + nr], rhs=rhs,
                                                     start=(first and ki == 0), stop=((not first) and ki == 7)),
                              rd=[b_uT[k][t]] + rb, wr=[b_py[bank]], inc=(ki == 7), note_r=nr_)

                for t_ in (0, 1):
                    for cg in range(4):
                        mm_half(t_, cg, True)

                for t in p2tiles:
                    nr = 128 if t < 8 else 2 * NS
                    c0 = t * 128
                    pb = t % 2
                    yb = t % 3
                    for cg in range(4):
                        bank = pb * 4 + cg
                        if t >= 2:
                            mm_half(t, cg, True)
                        mm_half(t, cg, False)
                        fw.op(DVE, lambda e: e.scalar_tensor_tensor(
                            out=yy[0:nr, yb, cg * 512:(cg + 1) * 512], in0=xr[0:nr, 0, cg * 512:(cg + 1) * 512],
                            scalar=ALPHA, in1=py[0:nr, bank, :], op0=ALU.mult, op1=ALU.add),
                            rd=[b_xr[0], b_py[bank]], wr=[b_yy[yb]])
                        fw.op(DVE, lambda e: e.bn_stats(out=stat[0:nr, yb, cg, :], in_=yy[0:nr, yb, cg * 512:(cg + 1) * 512]),
                              rd=[b_yy[yb]], wr=[b_stat[yb]])
                    if t + 1 in p2tiles:
                        xr_load(t + 1)
                    if t == 7 and pas == 0:
                        for t_ in range(3):
                            xT_tile(1, t_, pbanks=((0, 1) if t_ % 2 == 0 else (2, 3)), act_only=True)
                    fw.op(DVE, lambda e: e.bn_aggr(out=mv[0:nr, yb, 0:2],
                                                   in_=stat[0:nr, yb, :, :].rearrange("p a b -> p (a b)")),
                          rd=[b_stat[yb]], wr=[b_mv[yb]])
                    fw.op(ACT, lambda e: e.activation(out=mv[0:nr, yb, 2:3], in_=mv[0:nr, yb, 1:2], func=AF.Sqrt,
                                                      bias=epsb[0:nr, :]),
                          rd=[b_mv[yb], b_const], wr=[b_mv[yb]])
                    fw.op(DVE, lambda e: e.reciprocal(out=mv[0:nr, yb, 2:3], in_=mv[0:nr, yb, 2:3]),
                          rd=[b_mv[yb]], wr=[b_mv[yb]])
                    fw.op(DVE, lambda e: e.scalar_tensor_tensor(out=mv[0:nr, yb, 3:4], in0=mv[0:nr, yb, 0:1], scalar=-1.0,
                                                                in1=mv[0:nr, yb, 2:3], op0=ALU.mult, op1=ALU.mult),
                          rd=[b_mv[yb]], wr=[b_mv[yb]])
                    fw.op(ACT, lambda e: e.activation(out=yy[0:nr, yb, :], in_=yy[0:nr, yb, :], func=AF.Identity,
                                                      scale=mv[0:nr, yb, 2:3], bias=mv[0:nr, yb, 3:4]),
                          rd=[b_yy[yb], b_mv[yb]], wr=[b_yy[yb]])
                    if t >= 1:
                        tail(t - 1)
                tail(p2tiles[-1])
            st_wob.close()

        with ExitStack() as _st8:
            ost = _st8.enter_context(sb("ost", [12, 1024], F32))
            psof = PS[:, 0:2, :].rearrange("p a b -> p (a b)")
            b_ost = B(ost)
            for c in range(8):
                fw.op(PE, lambda e, c=c: e.transpose(psof[0:12, c * 128:(c + 1) * 128], outst[:, c, :], identf[:]),
                      rd=[b_outst[c], b_const], wr=b_ps[0:2], inc=(c == 7))
            fw.op(DVE, lambda e: e.tensor_copy(out=ost[:], in_=psof[0:12, :]),
                  rd=b_ps[0:2], wr=[b_ost])
            fw.dma(SP, hp, ost[0:1, :], rd=[b_ost])
            fw.dma(SP, cvp, ost[1:4, :], rd=[b_ost])
            for s in range(2):
                fw.dma(SP, hs[s:s + 1, :], ost[4 + 4 * s:5 + 4 * s, :], rd=[b_ost])
                fw.dma(SP, cvs[s], ost[5 + 4 * s:8 + 4 * s, :], rd=[b_ost])
            fw.barrier()
    return nc


def _alibi_tables():
    H = 16
    slopes = np.exp2(-8.0 * np.arange(1, H + 1, dtype=np.float64) / H)
    k = np.arange(128)[:, None]
    q = np.arange(128)[None, :]
    NEG = -200.0
    abp = np.zeros((128, 4, 2, 4, 128), np.float32)
    for kv in range(4):
        for g in range(4):
            s = slopes[kv * 4 + g]
            dA = (q + 128 - k).astype(np.float64)
            vA = (k // 64) >= (q // 64)
            abp[:, kv, 0, g, :] = np.where(vA, -s * dA, NEG)
            dB = np.abs(q - k).astype(np.float64)
            vB = (k // 64) <= (q // 64)
            abp[:, kv, 1, g, :] = np.where(vB, -s * dB, NEG)
    q2 = np.arange(32)[None, :]
    abs_ = np.zeros((128, 4, 2, 4, 32), np.float32)
    for kv in range(4):
        for g in range(4):
            s = slopes[kv * 4 + g]
            abs_[:, kv, 0, g, :] = -s * (q2 + 128 - k)
            abs_[:, kv, 1, g, :] = np.where(k < 32, -s * np.abs(q2 - k), NEG)
    return abp.reshape(128, 4096), abs_.reshape(128, 1024)


def _chunk_cols():
    cols = []
    for c in range(2):
        cols.append(np.arange(1024 + c * 128, 1024 + (c + 1) * 128))
    for j in range(2):
        for g in range(4):
            h0 = (2 * j) * 4 + g
            h1 = (2 * j + 1) * 4 + g
            cols.append(np.concatenate([np.arange(h0 * 64, h0 * 64 + 64), np.arange(h1 * 64, h1 * 64 + 64)]))
    for c in range(8):
        cols.append(np.arange(1536 + c * 128, 1536 + (c + 1) * 128))
    for c in range(8):
        cols.append(np.arange(3584 + c * 128, 3584 + (c + 1) * 128))
    for c in range(8):
        cols.append(np.arange(2560 + c * 128, 2560 + (c + 1) * 128))
    return cols


_NC_CACHE = {}


def kernel(x_prompt, x_sample, cache_k, cache_v, state_conv, state_h, w_in, b_in, conv_w, conv_b,
           w_gate_a, b_gate_a, w_gate_x, b_gate_x, lru_lambda, attn_sinks, w_out, ln_g, ln_b):
    f = lambda a: np.ascontiguousarray(np.asarray(a, dtype=np.float32))
    x_prompt, x_sample, cache_k, cache_v = f(x_prompt), f(x_sample), f(cache_k), f(cache_v)
    state_conv, state_h = f(state_conv), f(state_h)
    W = f(w_in)[0]; bi = f(b_in)[0]; Wo = f(w_out)[0]
    cols = _chunk_cols()
    win = np.empty((NCH, 128, 16, 128), np.float32)
    btab = np.empty((128, NCH), np.float32)
    for m, cc in enumerate(cols):
        win[m] = W[:, cc].reshape(16, 128, 128).transpose(1, 0, 2)
        btab[:, m] = bi[cc]
    win = win.reshape(NCH, 128, 2048)
    wkv = np.ascontiguousarray(W[:, 1024:1536].reshape(16, 128, 512).transpose(1, 0, 2)).reshape(128, 8192)
    wout = np.ascontiguousarray(Wo.reshape(16, 128, 2048).transpose(1, 0, 2)).reshape(128, 32768)
    bkv = np.ascontiguousarray(np.broadcast_to(bi[1024:1536], (128, 512)))
    cwt = np.ascontiguousarray(f(conv_w)[0].reshape(4, 8, 128).transpose(2, 1, 0)).reshape(128, 32)
    pc = lambda v: np.ascontiguousarray(f(v).reshape(8, 128).T)
    cbt, lamt = pc(f(conv_b)[0]), pc(f(lru_lambda)[0])
    bgat, bgxt = pc(f(b_gate_a)[0]), pc(f(b_gate_x)[0])
    wgt = np.zeros((128, 2, 8, 128), np.float32)
    for a, wsrc in enumerate((f(w_gate_a)[0], f(w_gate_x)[0])):
        for c in range(8):
            wgt[0:64, a, c, 0:64] = wsrc[2 * c]
            wgt[64:128, a, c, 64:128] = wsrc[2 * c + 1]
    wgt = wgt.reshape(128, 2048)
    lngt = np.ascontiguousarray(np.broadcast_to(f(ln_g)[0], (128, D)))
    lnbt = np.ascontiguousarray(np.broadcast_to(f(ln_b)[0], (128, D)))
    sinkt = np.ascontiguousarray(np.broadcast_to(f(attn_sinks)[0], (128, 16)))
    abp, abs_ = _alibi_tables()
    ident = np.eye(128, dtype=np.float32)

    if "nc" not in _NC_CACHE:
        _NC_CACHE["nc"] = build()
    nc = _NC_CACHE["nc"]

    in_maps = []
    for b in range(NCORES):
        sc = state_conv[0, 2 * b:2 * b + 2]
        sct = np.ascontiguousarray(sc.reshape(2, 3, 8, 128).transpose(3, 0, 2, 1)).reshape(128, 48)
        sht = np.ascontiguousarray(state_h[0, 2 * b:2 * b + 2].reshape(2, 8, 128).transpose(2, 0, 1)).reshape(128, 16)
        in_maps.append({
            "xp": x_prompt[b], "xs": np.ascontiguousarray(x_sample[2 * b:2 * b + 2].reshape(2 * NS, D)),
            "ck": np.ascontiguousarray(cache_k[0, 2 * b:2 * b + 2].reshape(2, 128, 256)),
            "cv": np.ascontiguousarray(cache_v[0, 2 * b:2 * b + 2].reshape(2, 128, 256)),
            "sconv": sct, "sh": sht, "win": win, "wkv": wkv, "wout": wout, "bt": btab, "bkv": bkv,
            "cw": cwt, "cb": cbt, "lam": lamt, "bga": bgat, "bgx": bgxt, "wg": wgt, "lng": lngt, "lnb": lnbt,
            "sinks": sinkt, "abp": abp, "abs": abs_, "ident": ident,
        })
    res = run_bass_kernel_spmd(nc, in_maps, core_ids=list(range(NCORES)))
    R = res.results
    y_p = np.stack([R[b]["yp"] for b in range(NCORES)])
    y_s = np.concatenate([R[b]["ys"].reshape(2, NS, D) for b in range(NCORES)])
    kwp = np.stack([R[b]["kwp"].reshape(128, 4, 64) for b in range(NCORES)])[None]
    vwp = np.stack([R[b]["vwp"].reshape(128, 4, 64) for b in range(NCORES)])[None]
    cvp = np.stack([R[b]["cvp"] for b in range(NCORES)])[None]
    hp = np.concatenate([R[b]["hp"] for b in range(NCORES)])[None]
    kws = np.concatenate([R[b]["kws"].reshape(2, 128, 4, 64) for b in range(NCORES)])[None]
    vws = np.concatenate([R[b]["vws"].reshape(2, 128, 4, 64) for b in range(NCORES)])[None]
    cvs = np.concatenate([R[b]["cvs"] for b in range(NCORES)])[None]
    hs = np.concatenate([R[b]["hs"] for b in range(NCORES)])[None]
    return (y_p, y_s, kwp, vwp, cvp, hp, kws, vws, cvs, hs)
```

```python
import numpy as np
from contextlib import ExitStack
import concourse.bass as bass
import concourse.mybir as mybir
from concourse.bass_utils import run_bass_kernel_spmd

F32 = mybir.dt.float32
BF16 = mybir.dt.bfloat16
AF = mybir.ActivationFunctionType
ALU = mybir.AluOpType

NCORES = 8
D = 2048
SEQ = 2048
NPP = 1024
NS = 32
NCOL = NPP + NS
NCH = 34
LW = NPP + 3 + NS
ALPHA = 2.0 ** 0.25
LN_EPS = 1e-5
GROUPS = [(0, 512), (512, 768), (768, 1056)]
LGROUPS = [(0, 512), (512, 1024), (1027, 1059)]


class Buf:
    __slots__ = ("w", "r", "name", "reg")

    def __init__(self, name=""):
        self.w = {}
        self.r = {}
        self.name = name
        self.reg = None


class Eng:
    def __init__(self, nc, h, name, in_order=False):
        self.h = h
        self.name = name
        self.sem = nc.alloc_semaphore(name="s_" + name)
        self.cnt = 0
        self.seen = {}
        self.in_order = in_order


class Fw:
    def __init__(self, nc, ndma=24):
        self.nc = nc
        self.pe = Eng(nc, nc.tensor, "pe", in_order=True)
        self.act = Eng(nc, nc.scalar, "act")
        self.dve = Eng(nc, nc.vector, "dve")
        self.pool = Eng(nc, nc.gpsimd, "pool")
        self.sp = Eng(nc, nc.sync, "sp")
        self.engs = [self.pe, self.act, self.dve, self.pool, self.sp]
        self.dsem = [nc.alloc_semaphore(name="d%d" % i) for i in range(ndma)]
        self.dcnt = [0] * ndma
        self.dnext = {"hw": 0, "sw": 0}
        self.half = ndma // 2
        self.limit = {}
        self.semobj = {}
        for e in self.engs:
            self.semobj[id(e.sem)] = e.sem
            self.limit[id(e.sem)] = 0
        for s in self.dsem:
            self.semobj[id(s)] = s
            self.limit[id(s)] = 0

    def _wait(self, eng, toks):
        for k, val in toks.items():
            if eng.in_order and k == id(eng.sem):
                continue
            if eng.seen.get(k, 0) >= val:
                continue
            assert val <= self.limit[k], "wait on a value never signalled (%s)" % eng.name
            eng.h.wait_ge(self.semobj[k], val)
            eng.seen[k] = val

    @staticmethod
    def _merge(dst, src):
        for k, v in src.items():
            if dst.get(k, 0) < v:
                dst[k] = v

    def _deps(self, rd, wr):
        toks = {}
        for b in rd:
            self._merge(toks, b.w)
        for b in wr:
            self._merge(toks, b.w)
            self._merge(toks, b.r)
        return toks

    def op(self, eng, fn, rd=(), wr=(), inc=True, note_r=()):
        self._wait(eng, self._deps(rd, wr))
        ins = fn(eng.h)
        k = id(eng.sem)
        if inc:
            eng.cnt += 1
            ins.then_inc(eng.sem, 1)
            self.limit[k] = eng.cnt
            tok = {k: eng.cnt}
        else:
            tok = {k: eng.cnt + 1}
        for b in rd:
            self._merge(b.r, tok)
        for b in note_r:
            self._merge(b.r, tok)
        for b in wr:
            b.w = dict(tok)
            b.r = {}
        return ins

    def dma(self, q, out, in_, rd=(), wr=()):
        self._wait(q, self._deps(rd, wr))
        kind = "sw" if q is self.pool else "hw"
        j = self.dnext[kind] + (self.half if kind == "sw" else 0)
        self.dnext[kind] = (self.dnext[kind] + 1) % self.half
        sem = self.dsem[j]
        k = id(sem)
        if self.dcnt[j] > 0:
            self._wait(q, {k: 16 * self.dcnt[j]})
        q.h.dma_start(out=out, in_=in_).then_inc(sem, 16)
        self.dcnt[j] += 1
        self.limit[k] = 16 * self.dcnt[j]
        tok = {k: 16 * self.dcnt[j]}
        for b in rd:
            self._merge(b.r, tok)
        for b in wr:
            b.w = dict(tok)
            b.r = {}

    def barrier(self):
        toks = {}
        for e in self.engs:
            if e.cnt > 0:
                toks[id(e.sem)] = e.cnt
        for j, s in enumerate(self.dsem):
            if self.dcnt[j] > 0:
                toks[id(s)] = 16 * self.dcnt[j]
        for e in self.engs:
            self._wait(e, toks)


def build():
    nc = bass.Bass("TRN2", target_bir_lowering=False)
    fw = Fw(nc)
    PE, ACT, DVE, POOL, SP = fw.pe, fw.act, fw.dve, fw.pool, fw.sp

    def din(name, shape):
        return nc.dram_tensor(name, shape, F32, kind="ExternalInput").ap()

    def dout(name, shape):
        return nc.dram_tensor(name, shape, F32, kind="ExternalOutput").ap()

    xp = din("xp", [SEQ, D]); xs = din("xs", [2 * NS, D])
    ck = din("ck", [2, 128, 256]); cv = din("cv", [2, 128, 256])
    sconv_d = din("sconv", [128, 48]); sh_d = din("sh", [128, 16])
    win = din("win", [NCH, 128, 2048]); wkv_d = din("wkv", [128, 8192]); wout_d = din("wout", [128, 32768])
    bt_d = din("bt", [128, NCH]); bkv_d = din("bkv", [128, 512])
    cw_d = din("cw", [128, 32]); cb_d = din("cb", [128, 8]); lam_d = din("lam", [128, 8])
    bga_d = din("bga", [128, 8]); bgx_d = din("bgx", [128, 8]); wg_d = din("wg", [128, 2048])
    lng_d = din("lng", [128, D]); lnb_d = din("lnb", [128, D]); sinks_d = din("sinks", [128, 16])
    abp_d = din("abp", [128, 4096]); abs_d = din("abs", [128, 1024]); ident_d = din("ident", [128, 128])

    yp = dout("yp", [SEQ, D]); ys = dout("ys", [2 * NS, D])
    kwp = dout("kwp", [128, 256]); vwp = dout("vwp", [128, 256])
    cvp = dout("cvp", [3, 1024]); hp = dout("hp", [1, 1024])
    kws = dout("kws", [2, 128, 256]); vws = dout("vws", [2, 128, 256])
    cvs = dout("cvs", [2, 3, 1024]); hs = dout("hs", [2, 1024])

    uniq = {"n": 0}

    def sb(name, shape, dt, side=None):
        uniq["n"] += 1
        return nc.sbuf_tensor("s%d_%s" % (uniq["n"], name), shape, dt, side=side)

    def ps(name, shape, dt):
        uniq["n"] += 1
        return nc.psum_tensor("p%d_%s" % (uniq["n"], name), shape, dt)

    with ExitStack() as _st1:
        uT = _st1.enter_context(sb("uT", [128, 16, NCOL + NS], BF16))
        identb = _st1.enter_context(sb("identb", [128, 128], BF16))
        identf = _st1.enter_context(sb("identf", [128, 128], F32))
        bt = _st1.enter_context(sb("bt", [128, NCH], F32))
        bkv = _st1.enter_context(sb("bkv", [128, 512], F32))
        cw = _st1.enter_context(sb("cw", [128, 8, 4], F32))
        cb = _st1.enter_context(sb("cb", [128, 8], F32))
        lamc = _st1.enter_context(sb("lamc", [128, 8], F32))
        bga = _st1.enter_context(sb("bga", [128, 8], F32))
        bgx = _st1.enter_context(sb("bgx", [128, 8], F32))
        wg = _st1.enter_context(sb("wg", [128, 2, 8, 128], BF16))
        Ep = _st1.enter_context(sb("Ep", [128, 4, 2, 4, 128], BF16))
        Es = _st1.enter_context(sb("Es", [128, 4, 2, 4, 32], BF16))
        esink = _st1.enter_context(sb("esink", [128, 16], F32))
        KTh = _st1.enter_context(sb("KTh", [128, 2, 128], BF16))
        V1h = _st1.enter_context(sb("V1h", [128, 4, 65], BF16))
        convst = _st1.enter_context(sb("convst", [128, 8, 3], F32))
        hst = _st1.enter_context(sb("hst", [128, 8], F32))
        outst = _st1.enter_context(sb("outst", [128, 8, 12], F32))
        sconv = _st1.enter_context(sb("sconv", [128, 2, 8, 3], F32))
        sh_sb = _st1.enter_context(sb("sh", [128, 2, 8], F32))
        epsb = _st1.enter_context(sb("epsb", [128, 1], F32))
        qtr = _st1.enter_context(sb("qtr", [128, 1], F32))
        scr = _st1.enter_context(sb("scr", [128, 2], F32))
        bth = _st1.enter_context(sb("bth", [128, NCH], F32))
        wb = _st1.enter_context(sb("wb", [128, 3, 2048], BF16))
        XW = _st1.enter_context(sb("XW", [128, 16 * NCOL], BF16))
        xT = XW[:, :].rearrange("p (k n) -> p k n", k=16)
        woA = XW[:, 0:8 * D].rearrange("p (k n) -> p k n", k=8)

        PS = _st1.enter_context(ps("PS", [128, 8, 512], F32))
        xst = _st1.enter_context(sb("xst", [128, 3, 2048], BF16))
        po = PS[:, 0:2, :]
        ptr = PS[:, 2, :].bitcast(BF16)
        pacc = PS[:, 3:6, :]
        pst = PS[:, 6:8, :]
        ptx = [PS[:, 6, :].bitcast(BF16), PS[:, 7, :].bitcast(BF16)]
        py = PS

        reg = {"all": []}

        def PB():
            return Buf()

        def B(t):
            b_ = Buf()
            ml = nc.lookup_mloc(t)
            b_.reg = (int(ml.addr), int(ml.addr) + int(ml.dims[1]))
            for o in reg["all"]:
                if o.reg is None or (o.reg[0] < b_.reg[1] and b_.reg[0] < o.reg[1]):
                    fw._merge(b_.w, o.w)
                    fw._merge(b_.w, o.r)
            reg["all"].append(b_)
            return b_

        b_ps = [PB() for _ in range(8)]
        b_po, b_ptr, b_pacc, b_ptx = b_ps[0:2], b_ps[2], b_ps[3:6], b_ps[6:8]
        b_pst2 = b_ps[6:8]
        b_py = b_ps
        b_xT = [PB() for _ in range(9)]
        b_xst = [PB(), PB(), PB()]
        b_wb = [PB() for _ in range(3)]
        b_uT = [[PB() for _ in range(9)] for _ in range(16)]
        b_const = PB()
        b_ident = PB()
        b_bias = PB()
        b_scr = PB()
        b_KTh, b_V1h = PB(), PB()
        b_convst = [PB() for _ in range(8)]
        b_hst = [PB() for _ in range(8)]
        b_outst = [PB() for _ in range(8)]

        init_bufs = []

        def IB():
            b_ = Buf()
            init_bufs.append(b_)
            return b_

        st_init = ExitStack()
        stg = st_init.enter_context(sb("stg", [128, 4096], F32, side="right"))
        stg2 = st_init.enter_context(sb("stg2", [128, 1024], F32, side="right"))
        stg3 = st_init.enter_context(sb("stg3", [128, 16], F32, side="right"))
        lamt = st_init.enter_context(sb("lamt", [128, 16], F32, side="right"))
        b_bga, b_bgx = IB(), IB()
        for dst, src in ((identf, ident_d), (bkv, bkv_d), (cb, cb_d)):
            fw.dma(SP, dst[:], src, wr=[IB()])
        fw.dma(SP, bt[:], bt_d, wr=[b_bias])
        fw.op(DVE, lambda e: e.tensor_scalar(out=bth[:], in0=bt[:], scalar1=0.5, scalar2=None, op0=ALU.mult),
              rd=[b_bias], wr=[IB()])
        fw.dma(SP, bga[:], bga_d, wr=[b_bga])
        fw.dma(SP, bgx[:], bgx_d, wr=[b_bgx])
        fw.dma(SP, cw[:].rearrange("p c t -> p (c t)"), cw_d, wr=[IB()])
        fw.dma(SP, sconv[:].rearrange("p s c t -> p (s c t)"), sconv_d, wr=[IB()])
        fw.dma(SP, sh_sb[:].rearrange("p s c -> p (s c)"), sh_d, wr=[IB()])
        fw.dma(POOL, identb[:], ident_d, wr=[b_ident])
        fw.op(DVE, lambda e: e.tensor_scalar(out=bga[:], in0=bga[:], scalar1=0.5, scalar2=None, op0=ALU.mult),
              rd=[b_bga], wr=[b_bga])
        fw.op(DVE, lambda e: e.tensor_scalar(out=bgx[:], in0=bgx[:], scalar1=0.5, scalar2=None, op0=ALU.mult),
              rd=[b_bgx], wr=[b_bgx])
        fw.op(DVE, lambda e: e.memset(qtr[:], 0.25), wr=[IB()])
        fw.op(DVE, lambda e: e.memset(scr[:], 0.0), wr=[IB()])
        fw.op(DVE, lambda e: e.memset(convst[:], 0.0), wr=b_convst)
        fw.op(DVE, lambda e: e.memset(epsb[:], LN_EPS), wr=[IB()])
        fw.op(DVE, lambda e: e.memset(hst[:], 0.0), wr=b_hst)
        fw.op(DVE, lambda e: e.memset(V1h[:], 1.0), wr=[b_V1h])
        fw.op(DVE, lambda e: e.memset(KTh[:], 0.0), wr=[b_KTh])

        def late_init():
            b_stg, b_stg2, b_stg3, b_lam = IB(), IB(), IB(), IB()
            fw.dma(POOL, wg[:].rearrange("p a c e -> p (a c e)"), wg_d, wr=[IB()])
            fw.dma(SP, stg[:], abp_d, wr=[b_stg])
            fw.dma(SP, stg2[:], abs_d, wr=[b_stg2])
            fw.dma(SP, stg3[:], sinks_d, wr=[b_stg3])
            fw.dma(SP, lamt[:, 0:8], lam_d, wr=[b_lam])
            fw.op(ACT, lambda e: e.activation(out=Ep[:].rearrange("p a b c d -> p (a b c d)"), in_=stg[:], func=AF.Exp),
                  rd=[b_stg], wr=[IB()])
            fw.op(ACT, lambda e: e.activation(out=Es[:].rearrange("p a b c d -> p (a b c d)"), in_=stg2[:], func=AF.Exp),
                  rd=[b_stg2], wr=[IB()])
            fw.op(ACT, lambda e: e.activation(out=esink[:], in_=stg3[:], func=AF.Exp), rd=[b_stg3], wr=[IB()])
            fw.op(ACT, lambda e: e.activation(out=lamt[:, 8:16], in_=lamt[:, 0:8], func=AF.Exp, scale=-1.0),
                  rd=[b_lam], wr=[b_lam])
            fw.op(ACT, lambda e: e.activation(out=lamt[:, 0:8], in_=lamt[:, 8:16], func=AF.Ln, bias=1.0),
                  rd=[b_lam], wr=[b_lam])
            fw.op(DVE, lambda e: e.tensor_scalar(out=lamc[:], in0=lamt[:, 0:8], scalar1=-4.0, scalar2=None,
                                                 op0=ALU.mult), rd=[b_lam], wr=[IB()])
            for b_ in init_bufs:
                fw._merge(b_const.w, b_.w)
                fw._merge(b_const.w, b_.r)
            reg["all"].extend(init_bufs)
            st_init.close()

        morder = list(range(10))
        for c in range(8):
            morder += [26 + c, 18 + c]
        morder += list(range(10, 18))
        wstate = [{"next": 0, "pre": 0}, {"next": 0, "pre": 0}]
        xdone = set()

        def next_wload(pas_, pre=False):
            st_ = wstate[pas_]
            if not pre and st_["pre"] > 0:
                st_["pre"] -= 1
                return
            i = st_["next"]
            if i < NCH:
                fw.dma(POOL, wb[:, i % 3, :], win[morder[i]], wr=[b_wb[i % 3]])
                st_["next"] = i + 1
                if pre:
                    st_["pre"] += 1

        def xload(pas_, t):
            if (pas_, "l", t) in xdone:
                return
            xdone.add((pas_, "l", t))
            nr = 128 if t < 8 else NS
            src = xp[pas_ * NPP + t * 128: pas_ * NPP + (t + 1) * 128, :] if t < 8 else xs[pas_ * NS:(pas_ + 1) * NS, :]
            fw.dma(POOL, xst[0:nr, t % 3, :], src, wr=[b_xst[t % 3]])

        def xT_tile(pas_, t, pbanks=(6, 7), act_only=False):
            if (pas_, "t", t) in xdone:
                return
            xdone.add((pas_, "t", t))
            nr = 128 if t < 8 else NS
            c0 = t * 128
            for half in range(2):
                bank = b_ps[pbanks[half]]
                pv = PS[:, pbanks[half], :].bitcast(BF16)
                for kk in range(8):
                    k = half * 8 + kk
                    fw.op(PE, lambda e, k=k, kk=kk, pv=pv: e.transpose(
                        pv[:, kk * 128: kk * 128 + nr], xst[0:nr, t % 3, k * 128:(k + 1) * 128],
                        identb[0:nr, 0:nr]),
                        rd=[b_xst[t % 3], b_ident], wr=[bank], inc=(kk == 7))
                src_ap = pv.rearrange("p (k c) -> p k c", k=8)[:, :, 0:nr]
                dst_ap = xT[:, half * 8:(half + 1) * 8, c0:c0 + nr]
                if half == 0 or act_only:
                    fw.op(ACT, lambda e, s=src_ap, d=dst_ap: e.copy(out=d, in_=s), rd=[bank], wr=[b_xT[t]])
                else:
                    fw.op(DVE, lambda e, s=src_ap, d=dst_ap: e.tensor_copy(out=d, in_=s), rd=[bank],
                          wr=[b_xT[t]])
            if t + 3 < 9:
                xload(pas_, t + 3)
            if pas_ == 0 and t < 3:
                next_wload(0)

        for pas in range(2):
            tok0 = pas * NPP
            st_wob = ExitStack()
            b_woB = [None] * 8
            wob_state = {"next": 0}
            b_woA = [PB() for _ in range(4)]
            with ExitStack() as _st3:
                QT = _st3.enter_context(sb("QT", [128, 8, NCOL], BF16))
                KT = _st3.enter_context(sb("KT", [128, 2, NPP + 128 + NS], BF16))
                V1 = _st3.enter_context(sb("V1", [128, 10, 4, 65], BF16))
                b_QT = [[B(QT) for _ in range(3)] for _ in range(8)]
                b_KT = [[B(KT) for _ in range(10)] for _ in range(2)]
                b_V1 = [B(V1) for _ in range(10)]
                fw.op(DVE, lambda e: e.memset(V1[:], 1.0), wr=b_V1)

                if True:
                    def kv_tokmajor(t):
                        nr = 128 if t < 8 else NS
                        c0 = t * 128
                        need_k = (t == 8) or (pas == 1 and t == 7)
                        lo = 0 if need_k else 256
                        bank = t % 2
                        for k in range(16):
                            fw.op(PE, lambda e, k=k: e.matmul(po[0:nr, bank, lo:512], lhsT=xT[:, k, c0:c0 + nr],
                                                               rhs=wkv[:, k, lo:512], start=(k == 0), stop=(k == 15)),
                                  rd=[b_xT[t], b_wkv], wr=[b_po[bank]], inc=(k == 15))
                        vt = t if t < 8 else 9
                        if need_k:
                            fw.op(DVE, lambda e: e.tensor_tensor(out=kvst[0:nr, bank, :], in0=po[0:nr, bank, :],
                                                                 in1=bkv[0:nr, :], op=ALU.add),
                                  rd=[b_po[bank], b_const], wr=[b_kvst[bank]])
                            fw.op(ACT, lambda e: e.copy(out=V1[0:nr, vt, :, 0:64],
                                                        in_=kvst[0:nr, bank, 256:512].rearrange("p (h d) -> p h d", h=4)),
                                  rd=[b_kvst[bank]], wr=[b_V1[vt]])
                            if t == 8:
                                fw.dma(SP, kws[pas, 96:128, :], kvst[0:NS, bank, 0:256], rd=[b_kvst[bank]])
                                fw.dma(SP, vws[pas, 96:128, :], kvst[0:NS, bank, 256:512], rd=[b_kvst[bank]])
                            else:
                                fw.dma(SP, kwp, kvst[:, bank, 0:256], rd=[b_kvst[bank]])
                                fw.dma(SP, vwp, kvst[:, bank, 256:512], rd=[b_kvst[bank]])
                        else:
                            fw.op(DVE, lambda e: e.tensor_tensor(
                                out=V1[0:nr, vt, :, 0:64],
                                in0=po[0:nr, bank, 256:512].rearrange("p (h d) -> p h d", h=4),
                                in1=bkv[0:nr, 256:512].rearrange("p (h d) -> p h d", h=4), op=ALU.add),
                                rd=[b_po[bank], b_const], wr=[b_V1[vt]])

                    for t_ in range(3):
                        xload(pas, t_)
                    if pas == 1:
                        for _ in range(3):
                            next_wload(pas)

                    def kv_phase():
                        for t in range(9):
                            kv_tokmajor(t)
                        for c in range(2):
                            fw.op(PE, lambda e, c=c: e.transpose(ptr[:, c * 128:(c + 1) * 128],
                                                                 ckst[:, c * 128:(c + 1) * 128], identb[:]),
                                  rd=[b_ck, b_const], wr=[b_ptr], inc=(c == 1))
                        fw.op(DVE, lambda e: e.tensor_copy(out=KT[:, :, NPP:NPP + 128],
                                                           in_=ptr[:, 0:256].rearrange("p (c t) -> p c t", c=2)),
                              rd=[b_ptr], wr=[b_KT[0][8], b_KT[1][8]])


                with ExitStack() as _st6:
                    tA = _st6.enter_context(sb("tA", [128, 2, 512], F32))
                    b_tA = [B(tA), B(tA)]
                    zA = _st6.enter_context(sb("zA", [128, 2, 512], F32))
                    b_zA = [B(zA), B(zA)]
                    st_1a = ExitStack()
                    wkv = st_1a.enter_context(sb("wkv", [128, 16, 512], BF16))
                    kvst = st_1a.enter_context(sb("kvst", [128, 2, 512], F32))
                    ckst = st_1a.enter_context(sb("ckst", [128, 256], BF16))
                    b_wkv, b_kvst, b_ck = B(wkv), [B(kvst), B(kvst)], B(ckst)
                    st_lru = ExitStack()
                    accn = {"i": 0, "e": 0, "banks": list(range(8)), "last": [0] * 8}

                    def inproj_group(slot, gi, evac, mid=None):
                        lo, hi = GROUPS[gi]
                        bi = accn["banks"][accn["i"] % len(accn["banks"])]
                        accn["i"] += 1
                        accn["last"][bi] = accn["i"]
                        t_lo, t_hi = lo // 128, (hi + 127) // 128
                        for k in range(16):
                            fw.op(PE, lambda e, k=k: e.matmul(PS[:, bi, 0:hi - lo], lhsT=wb[:, slot, k * 128:(k + 1) * 128],
                                                               rhs=xT[:, k, lo:hi], start=(k == 0), stop=(k == 15)),
                                  rd=[b_wb[slot]] + b_xT[t_lo:t_hi], wr=[b_ps[bi]], inc=(k == 15))
                            if k == 7 and mid is not None:
                                mid()
                        evac(PS[:, bi, 0:hi - lo], b_ps[bi], lo, hi, gi)

                    def tiles_of(lo, hi):
                        return range(lo // 128, (hi + 127) // 128)

                    def attn_unit(t, kv):
                        nq = 128 if t < 8 else NS
                        q0 = t * 128
                        j, half = kv // 2, kv % 2
                        pb = slice(half * 64, half * 64 + 64)
                        xb = kv % 2
                        p3 = (t * 4 + kv) % 3
                        pk = kv % 2
                        tiles = []
                        if t < 8:
                            if t == 0:
                                if pas == 1:
                                    tiles.append((KTh[pb, j, :], b_KTh, V1h[:, kv, :], b_V1h, 128, Ep[:, kv, 0]))
                            else:
                                tiles.append((KT[pb, j, (t - 1) * 128:t * 128], b_KT[j][t - 1],
                                              V1[:, t - 1, kv, :], b_V1[t - 1], 128, Ep[:, kv, 0]))
                            tiles.append((KT[pb, j, t * 128:(t + 1) * 128], b_KT[j][t], V1[:, t, kv, :], b_V1[t],
                                          128, Ep[:, kv, 1]))
                        else:
                            tiles.append((KT[pb, j, NPP:NPP + 128], b_KT[j][8], V1[:, 8, kv, :], b_V1[8], 128,
                                          Es[:, kv, 0]))
                            tiles.append((KT[pb, j, NPP + 128:NPP + 128 + NS], b_KT[j][9], V1[0:NS, 9, kv, :],
                                          b_V1[9], NS, Es[0:NS, kv, 1]))
                        nt = len(tiles)

                        def st():
                            for ti, (kt_ap, kt_buf, v_ap, v_buf, nk, e_ap) in enumerate(tiles):
                                fw.op(PE, lambda e: e.matmul(
                                    pst[0:nk, ti, 0:4 * nq].rearrange("p (g q) -> p g q", g=4),
                                    lhsT=kt_ap, rhs=QT[pb, j * 4:(j + 1) * 4, q0:q0 + nq], start=True, stop=True),
                                    rd=[kt_buf] + [b_QT[j * 4 + g][0 if t < 4 else (1 if t < 6 else 2)] for g in range(4)],
                                    wr=[b_pst2[ti]], inc=(ti == nt - 1))
                            for ti, (kt_ap, kt_buf, v_ap, v_buf, nk, e_ap) in enumerate(tiles):
                                fw.op(ACT, lambda e: e.activation(
                                    out=ex[0:nk, xb, ti, 0:4 * nq], in_=pst[0:nk, ti, 0:4 * nq], func=AF.Exp, scale=0.125),
                                    rd=[b_pst2[ti]], wr=[b_ex[xb][ti]])
                                fw.op(DVE, lambda e: e.tensor_tensor(
                                    out=PT[0:nk, p3, ti, 0:4 * nq], in0=ex[0:nk, xb, ti, 0:4 * nq],
                                    in1=e_ap.rearrange("p g q -> p (g q)"), op=ALU.mult),
                                    rd=[b_ex[xb][ti], b_const], wr=[b_PT[p3][ti]])

                        def pvn():
                            for g in range(4):
                                for ti, (kt_ap, kt_buf, v_ap, v_buf, nk, e_ap) in enumerate(tiles):
                                    last = (g == 3 and ti == nt - 1)
                                    fw.op(PE, lambda e: e.matmul(
                                        po[0:nq, pk, g * 65:(g + 1) * 65], lhsT=PT[0:nk, p3, ti, g * nq:(g + 1) * nq],
                                        rhs=v_ap, start=(ti == 0), stop=(ti == nt - 1)),
                                        rd=[b_PT[p3][ti], v_buf], wr=[b_po[pk]], inc=last)
                            pov = po[0:nq, pk, 0:260].rearrange("p (g d) -> p g d", g=4)
                            fw.op(DVE, lambda e: e.tensor_tensor(
                                out=den[0:nq, pk, 0:4], in0=pov[:, :, 64], in1=esink[0:nq, kv * 4:(kv + 1) * 4], op=ALU.add),
                                rd=[b_po[pk], b_const], wr=[b_den[pk]])
                            fw.op(DVE, lambda e: e.reciprocal(out=den[0:nq, pk, 4:8], in_=den[0:nq, pk, 0:4]),
                                  rd=[b_den[pk]], wr=[b_den[pk]])
                            fw.op(DVE, lambda e: e.tensor_tensor(
                                out=osb[0:nq, t, kv * 256:(kv + 1) * 256].rearrange("p (g d) -> p g d", g=4),
                                in0=pov[:, :, 0:64], in1=den[0:nq, pk, 4:8].unsqueeze(2).to_broadcast([nq, 4, 64]),
                                op=ALU.mult), rd=[b_po[pk], b_den[pk]], wr=[b_osb[t]])
                        return st, pvn

                    def attn_tr(t):
                        nq = 128 if t < 8 else NS
                        q0 = t * 128 if t < 8 else NPP + NS * pas
                        bk = 2 + (t % 2)
                        ptb = PS[:, bk, :].bitcast(BF16)
                        for c in range(8):
                            fw.op(PE, lambda e, c=c: e.transpose(ptb[:, c * 128:c * 128 + nq],
                                                                 osb[0:nq, t, c * 128:(c + 1) * 128],
                                                                 identb[0:nq, 0:nq]),
                                  rd=[b_osb[t], b_const], wr=[b_ps[bk]], inc=(c == 7))
                        fw.op(DVE, lambda e: e.tensor_tensor(
                            out=uT[:, 0:8, q0:q0 + nq], in0=ptb.rearrange("p (c q) -> p c q", c=8)[:, :, 0:nq],
                            in1=uT[:, 0:8, q0:q0 + nq], op=ALU.mult),
                            rd=[b_ps[bk]] + [b_uT[c][t] for c in range(8)], wr=[b_uT[c][t] for c in range(8)])

                    def attention():
                        units = [(t, kv) for t in range(9) for kv in range(4)]
                        fns = [attn_unit(t, kv) for (t, kv) in units]
                        for i in range(len(units) + 2):
                            if i < len(units):
                                fns[i][0]()
                            if i >= 2:
                                fns[i - 2][1]()
                            yield

                    def lru_chain(c):
                        xb_ = c % 2
                        so = 4 + 4 * pas
                        rrc, b_rrc = rr2[:, c % 2, :], b_rr2[c % 2]
                        for gate in range(2):
                            dst, dbuf, bia = (rrc, b_rrc, bga) if gate == 0 else (ii, b_ii, bgx)
                            assert len(accn["banks"]) == 8
                            if accn["i"] % 8 > 5:
                                accn["i"] += 8 - accn["i"] % 8
                            b0 = accn["i"] % 8
                            for gi_, (lo, hi) in enumerate(LGROUPS):
                                bi = b0 + gi_
                                accn["i"] += 1
                                accn["last"][bi] = accn["i"]
                                off = lo - 512 * gi_
                                fw.op(PE, lambda e: e.matmul(PS[:, bi, off:off + hi - lo], lhsT=wg[:, gate, c, :],
                                                             rhs=xcb[:, xb_, lo:hi], start=True, stop=True),
                                      rd=[b_xcb[xb_], b_const], wr=[b_ps[bi]])
                            fw.op(ACT, lambda e: e.activation(
                                out=dst[:, 0:LW], in_=PS[:, b0:b0 + 3, :].rearrange("p a b -> p (a b)")[:, 0:LW],
                                func=AF.Tanh, bias=bia[:, c:c + 1], scale=0.5),
                                rd=b_ps[b0:b0 + 3] + [b_const], wr=[dbuf])
                            if gate == 0:
                                fw.op(ACT, lambda e: e.activation(out=aa[:], in_=rrc, func=AF.Exp, scale=lamc[:, c:c + 1],
                                                                  bias=lamc[:, c:c + 1]),
                                      rd=[b_rrc, b_const], wr=[b_aa])
                            yield
                        fw.op(ACT, lambda e: e.activation(out=rrc[:, 0:NCOL], in_=gbuf[:, xb_, :], func=AF.Tanh, scale=0.5),
                              rd=[b_gbuf[xb_], b_rrc], wr=[b_rrc])
                        fw.op(ACT, lambda e: e.activation(out=mm[:], in_=aa[:], func=AF.Square), rd=[b_aa], wr=[b_mm])
                        fw.op(ACT, lambda e: e.activation(out=mm[:], in_=mm[:], func=AF.Sqrt, scale=-0.25, bias=qtr[:, 0:1]),
                              rd=[b_mm, b_const], wr=[b_mm])
                        fw.op(ACT, lambda e: e.activation(out=scr[:, 0:1], in_=scr[:, 1:2], func=AF.Tanh), rd=[b_const],
                              wr=[b_scr])
                        yield
                        fw.op(DVE, lambda e: e.scalar_tensor_tensor(out=ii[:], in0=ii[:], scalar=1.0, in1=xc[:, xb_, :],
                                                                    op0=ALU.add, op1=ALU.mult),
                              rd=[b_ii, b_xc[xb_]], wr=[b_ii])
                        fw.op(DVE, lambda e: e.tensor_tensor(out=ii[:], in0=ii[:], in1=mm[:], op=ALU.mult),
                              rd=[b_ii, b_mm], wr=[b_ii])
                        fw.op(DVE, lambda e: e.scalar_tensor_tensor(out=rrc[:, 0:NCOL], in0=rrc[:, 0:NCOL], scalar=1.0,
                                                                    in1=gbuf[:, xb_, :], op0=ALU.add, op1=ALU.mult),
                              rd=[b_rrc, b_gbuf[xb_]], wr=[b_rrc])
                        yield
                        fw.op(DVE, lambda e: e.tensor_tensor_scan(out=mm[:, 0:NPP], data0=aa[:, 0:NPP], data1=ii[:, 0:NPP],
                                                                  initial=hst[:, c:c + 1], op0=ALU.mult, op1=ALU.add),
                              rd=[b_aa, b_ii, b_hst[c]], wr=[b_mm])
                        fw.op(DVE, lambda e: e.tensor_tensor_scan(out=mm[:, NPP + 3:LW], data0=aa[:, NPP + 3:LW],
                                                                  data1=ii[:, NPP + 3:LW], initial=sh_sb[:, pas, c:c + 1],
                                                                  op0=ALU.mult, op1=ALU.add),
                              rd=[b_aa, b_ii, b_const], wr=[b_mm])
                        yield
                        fw.op(DVE, lambda e: e.tensor_copy(out=hst[:, c:c + 1], in_=mm[:, NPP - 1:NPP]), rd=[b_mm],
                              wr=[b_hst[c]])
                        if pas == 1:
                            fw.op(DVE, lambda e: e.tensor_copy(out=outst[:, c, 0:1], in_=mm[:, NPP - 1:NPP]),
                                  rd=[b_mm], wr=[b_outst[c]])
                        fw.op(DVE, lambda e: e.tensor_copy(out=outst[:, c, so:so + 1], in_=mm[:, LW - 1:LW]),
                              rd=[b_mm], wr=[b_outst[c]])
                        fw.op(DVE, lambda e: e.scalar_tensor_tensor(out=uT[:, 8 + c, 0:NPP], in0=mm[:, 0:NPP], scalar=0.5,
                                                                    in1=rrc[:, 0:NPP], op0=ALU.mult, op1=ALU.mult),
                              rd=[b_mm, b_rrc], wr=b_uT[8 + c][0:8])
                        fw.op(DVE, lambda e: e.scalar_tensor_tensor(out=uT[:, 8 + c, NPP + NS * pas:NCOL + NS * pas], in0=mm[:, NPP + 3:LW],
                                                                    scalar=0.5, in1=rrc[:, NPP:NCOL], op0=ALU.mult,
                                                                    op1=ALU.mult),
                              rd=[b_mm, b_rrc], wr=[b_uT[8 + c][8]])
                        yield

                    def conv(c):
                        xb_ = c % 2
                        so = 4 + 4 * pas
                        xlv = xl[:, xb_, :]
                        fw.op(DVE, lambda e: e.tensor_copy(out=convst[:, c, :], in_=xlv[:, NPP:NPP + 3]),
                              rd=[b_xl[xb_], b_xlp[xb_]], wr=[b_convst[c]])
                        if pas == 1:
                            fw.op(DVE, lambda e: e.tensor_copy(out=outst[:, c, 1:4], in_=xlv[:, NPP:NPP + 3]),
                                  rd=[b_xl[xb_], b_xlp[xb_]], wr=[b_outst[c]])
                        fw.op(DVE, lambda e: e.tensor_copy(out=outst[:, c, so + 1:so + 4], in_=xlv[:, LW:LW + 3]),
                              rd=[b_xl[xb_], b_xlp[xb_]], wr=[b_outst[c]])
                        fw.op(DVE, lambda e: e.tensor_scalar(out=xc[:, xb_, :], in0=xlv[:, 0:LW], scalar1=cw[:, c, 0:1],
                                                             scalar2=cb[:, c:c + 1], op0=ALU.mult, op1=ALU.add),
                              rd=[b_xl[xb_], b_xlp[xb_], b_const], wr=[b_xc[xb_]])
                        for tap in range(1, 4):
                            fw.op(DVE, lambda e, tap=tap: e.scalar_tensor_tensor(
                                out=xc[:, xb_, :], in0=xlv[:, tap:tap + LW], scalar=cw[:, c, tap:tap + 1], in1=xc[:, xb_, :],
                                op0=ALU.mult, op1=ALU.add), rd=[b_xl[xb_], b_xlp[xb_], b_xc[xb_], b_const], wr=[b_xc[xb_]])
                        fw.op(ACT, lambda e: e.copy(out=xcb[:, xb_, :], in_=xc[:, xb_, :]), rd=[b_xc[xb_]],
                              wr=[b_xcb[xb_]])

                    def pieces(lo, hi):
                        out = []
                        if lo < NPP:
                            out.append((0, min(hi, NPP) - lo, False))
                        if hi > NPP:
                            out.append((max(lo, NPP) - lo, hi - max(lo, NPP), True))
                        return out

                    def make_evac(m):
                        bias = bt[:, m:m + 1]
                        if m < 2:
                            def evac(pa, pb_, lo, hi, gi):
                                for (po_, w_, smp) in pieces(lo, hi):
                                    dlo = (NPP + 128) if smp else lo
                                    bufs = [b_KT[m][9]] if smp else [b_KT[m][t] for t in tiles_of(lo, min(hi, NPP))]
                                    fw.op(ACT, lambda e: e.activation(out=KT[:, m, dlo:dlo + w_], in_=pa[:, po_:po_ + w_],
                                                                      func=AF.Identity, bias=bias),
                                          rd=[pb_, b_bias], wr=bufs)
                        elif m < 10:
                            def evac(pa, pb_, lo, hi, gi):
                                fw.op(ACT, lambda e: e.activation(out=QT[:, m - 2, lo:hi], in_=pa, func=AF.Identity,
                                                                  bias=bias), rd=[pb_, b_bias], wr=[b_QT[m - 2][gi]])
                        elif m < 18:
                            def evac(pa, pb_, lo, hi, gi):
                                c = m - 10
                                w_all = hi - lo
                                tb = accn["e"] % 2
                                accn["e"] += 1
                                fw.op(ACT, lambda e: e.activation(out=tA[:, tb, 0:w_all], in_=pa, func=AF.Tanh,
                                                                  bias=bth[:, m:m + 1], scale=0.5),
                                      rd=[pb_, b_const], wr=[b_tA[tb]])
                                fw.op(ACT, lambda e: e.activation(out=zA[:, tb, 0:w_all], in_=pa, func=AF.Identity,
                                                                  bias=bth[:, m:m + 1], scale=0.5),
                                      rd=[pb_, b_const], wr=[b_zA[tb]])
                                for (po_, w_, smp) in pieces(lo, hi):
                                    ulo = (NPP + NS * pas) if smp else lo
                                    bufs = [b_uT[c][8]] if smp else [b_uT[c][t] for t in tiles_of(lo, min(hi, NPP))]
                                    fw.op(DVE, lambda e: e.scalar_tensor_tensor(
                                        out=uT[:, c, ulo:ulo + w_], in0=tA[:, tb, po_:po_ + w_], scalar=1.0,
                                        in1=zA[:, tb, po_:po_ + w_], op0=ALU.add, op1=ALU.mult),
                                        rd=[b_tA[tb], b_zA[tb]], wr=bufs)
                        elif m < 26:
                            def evac(pa, pb_, lo, hi, gi):
                                c = m - 18
                                fw.op(ACT, lambda e: e.activation(out=gbuf[:, c % 2, lo:hi], in_=pa, func=AF.Identity,
                                                                  bias=bias), rd=[pb_, b_bias], wr=[b_gbuf[c % 2]])
                        else:
                            def evac(pa, pb_, lo, hi, gi):
                                c = m - 26
                                for (po_, w_, smp) in pieces(lo, hi):
                                    dlo = (3 + NPP + 3) if smp else 3 + lo
                                    fw.op(ACT, lambda e: e.activation(out=xl[:, c % 2, dlo:dlo + w_], in_=pa[:, po_:po_ + w_],
                                                                      func=AF.Identity, bias=bias),
                                          rd=[pb_, b_bias], wr=[b_xl[c % 2]])
                        return evac

                    attn = None
                    chain = None
                    for t in range(4):
                        if pas == 1:
                            xT_tile(pas, t, pbanks=((0, 1) if t % 2 == 0 else (2, 3)), act_only=True)
                        else:
                            xT_tile(pas, t)
                    inproj_group(0, 0, make_evac(morder[0]))
                    xT_tile(pas, 4)
                    inproj_group(1, 0, make_evac(morder[1]))
                    xT_tile(pas, 5)
                    inproj_group(2, 0, make_evac(morder[2]))
                    inproj_group(0, 1, make_evac(morder[0]))
                    xT_tile(pas, 6)
                    inproj_group(1, 1, make_evac(morder[1]))
                    xT_tile(pas, 7)
                    inproj_group(2, 1, make_evac(morder[2]))
                    xT_tile(pas, 8)
                    for i3 in range(3):
                        inproj_group(i3, 2, make_evac(morder[i3]))
                        next_wload(pas)
                    for idx, m in enumerate(morder):
                        if idx < 3:
                            continue
                        slot = idx % 3
                        evac = make_evac(m)
                        if idx == 4 and pas == 0:
                            late_init()
                        if 5 <= idx <= 8:
                            q = idx - 5
                            b_wkvq = B(wkv)
                            fw.dma(POOL, wkv[:, q * 4:(q + 1) * 4, :].rearrange("p k n -> p (k n)"),
                                   wkv_d[:, q * 2048:(q + 1) * 2048], wr=[b_wkvq])
                            fw._merge(b_wkv.w, b_wkvq.w)
                        if idx == 3:
                            fw.dma(POOL, ckst[:], ck[pas], wr=[b_ck])
                            fw.dma(POOL, V1[:, 8, :, 0:64], cv[pas].rearrange("t (h d) -> t h d", h=4), wr=[b_V1[8]])
                            fw.dma(SP, kws[pas, 0:96, :], ck[pas, 32:128, :])
                            fw.dma(SP, vws[pas, 0:96, :], cv[pas, 32:128, :])
                        if idx == 10:
                            kv_phase()
                            st_1a.close()
                            xl = st_lru.enter_context(sb("xl", [128, 2, LW + 3], F32))
                            gbuf = st_lru.enter_context(sb("gbuf", [128, 2, NCOL], F32))
                            xc = st_lru.enter_context(sb("xc", [128, 2, LW], F32))
                            xcb = st_lru.enter_context(sb("xcb", [128, 2, LW], BF16))
                            rr2 = st_lru.enter_context(sb("rr", [128, 2, LW], F32))
                            ii = st_lru.enter_context(sb("ii", [128, LW], F32))
                            aa = st_lru.enter_context(sb("aa", [128, LW], F32))
                            mm = st_lru.enter_context(sb("mm", [128, LW], F32))
                            b_xl, b_gbuf, b_xc, b_xcb = [B(xl), B(xl)], [B(gbuf), B(gbuf)], [B(xc), B(xc)], [B(xcb), B(xcb)]
                            b_xlp = [B(xl), B(xl)]
                            b_rr2, b_ii, b_aa, b_mm = [B(rr2), B(rr2)], B(ii), B(aa), B(mm)
                            fw.op(DVE, lambda e: e.memset(rr2[:], 0.0), wr=b_rr2)
                            fw.op(DVE, lambda e: e.memset(ii[:], 0.0), wr=[b_ii])
                        if m == 12:
                            if chain is not None:
                                for _ in chain:
                                    pass
                                chain = None
                            st_lru.close()
                            accn["banks"] = sorted([2, 3, 4, 5], key=lambda b_: accn["last"][b_])
                            accn["i"] = 0
                            ex = _st6.enter_context(sb("ex", [128, 2, 2, 512], BF16))
                            PT = _st6.enter_context(sb("PT", [128, 3, 2, 512], BF16))
                            osb = _st6.enter_context(sb("osb", [128, 9, 1024], BF16))
                            den = _st6.enter_context(sb("den", [128, 2, 8], F32))
                            b_ex = [[B(ex), B(ex)] for _ in range(2)]
                            b_PT = [[B(PT), B(PT)] for _ in range(3)]
                            b_den = [B(den), B(den)]
                            b_osb = [B(osb) for _ in range(9)]
                            woB = st_wob.enter_context(sb("woB", [128, 8, D], BF16, side="right"))
                            wob_state["tensor"] = woB
                        if m == 12:
                            attn = attention()
                        if m >= 26:
                            c = m - 26
                            xb_ = c % 2
                            fw.op(DVE, lambda e: e.tensor_copy(out=xl[:, xb_, 0:3], in_=convst[:, c, :]),
                                  rd=[b_convst[c]], wr=[b_xlp[xb_]])
                            fw.op(DVE, lambda e: e.tensor_copy(out=xl[:, xb_, 3 + NPP:3 + NPP + 3], in_=sconv[:, pas, c, :]),
                                  rd=[b_const], wr=[b_xlp[xb_]])
                        for gi in range(3):
                            if attn is not None:
                                inproj_group(slot, gi, evac, mid=lambda: next(attn, None))
                                next(attn, None)
                            else:
                                inproj_group(slot, gi, evac)
                            if chain is not None:
                                if next(chain, "done") == "done":
                                    chain = None
                        next_wload(pas)
                        if 12 <= m < 18:
                            for _ in range(2):
                                kq = wob_state["next"]
                                if kq < 8:
                                    b_woB[kq] = B(wob_state["tensor"])
                                    fw.dma(POOL, wob_state["tensor"][:, kq, :], wout_d[:, (8 + kq) * 2048:(9 + kq) * 2048],
                                           wr=[b_woB[kq]])
                                    wob_state["next"] = kq + 1
                        if m >= 26:
                            conv(m - 26)
                        elif 18 <= m < 26:
                            if chain is not None:
                                for _ in chain:
                                    pass
                            chain = lru_chain(m - 18)
                    wo3 = wout_d.rearrange("p (k n) -> p k n", k=16)
                    pre = {}
                    for b_ in b_xT:
                        fw._merge(pre, b_.w)
                        fw._merge(pre, b_.r)
                    allw = {}
                    for cg in range(4):
                        b_woA[cg].w = dict(pre)
                        b_woA[cg].r = {}
                        fw.dma(POOL, woA[:, :, cg * 512:(cg + 1) * 512], wo3[:, 0:8, cg * 512:(cg + 1) * 512],
                               wr=[b_woA[cg]])
                        fw._merge(allw, b_woA[cg].w)
                    for b_ in b_xT:
                        b_.w = dict(allw)
                        b_.r = {}
                    if chain is not None:
                        for _ in chain:
                            pass
                    for _ in attn:
                        pass
                    for t in range(9):
                        attn_tr(t)
                    if pas == 0:
                        fw.op(DVE, lambda e: e.tensor_copy(out=KTh[:], in_=KT[:, :, NPP - 128:NPP]),
                              rd=[b_KT[0][7], b_KT[1][7]], wr=[b_KTh])
                        fw.op(DVE, lambda e: e.tensor_copy(out=V1h[:], in_=V1[:, 7, :, :]), rd=[b_V1[7]], wr=[b_V1h])

            with ExitStack() as _st7:
                lng = _st7.enter_context(sb("lng", [128, D], F32))
                lnb = _st7.enter_context(sb("lnb", [128, D], F32))
                xr = _st7.enter_context(sb("xr", [128, 1, D], F32))
                yy = _st7.enter_context(sb("yy", [128, 3, D], F32))
                stat = _st7.enter_context(sb("stat", [128, 3, 4, 6], F32))
                mv = _st7.enter_context(sb("mv", [128, 3, 4], F32))
                b_ln = [B(lng), B(lnb)]
                b_xr = [B(xr)]
                b_yy, b_stat, b_mv = [B(yy) for _ in range(3)], [B(stat) for _ in range(3)], [B(mv) for _ in range(3)]
                p2tiles = list(range(8)) if pas == 0 else list(range(9))

                def xr_load(t):
                    nr = 128 if t < 8 else 2 * NS
                    src = xp[tok0 + t * 128: tok0 + (t + 1) * 128, :] if t < 8 else xs[0:2 * NS, :]
                    fw.dma(SP, xr[0:nr, 0, :], src, wr=[b_xr[0]])

                xr_load(0)
                fw.dma(SP, lng[:], lng_d, wr=[b_ln[0]])
                fw.dma(SP, lnb[:], lnb_d, wr=[b_ln[1]])
                if pas == 0:
                    for t_ in range(3):
                        xload(1, t_)
                    for _ in range(3):
                        next_wload(1, pre=True)

                def tail(t):
                    nr = 128 if t < 8 else 2 * NS
                    yb = t % 3
                    fw.op(DVE, lambda e: e.tensor_tensor(out=yy[0:nr, yb, :], in0=yy[0:nr, yb, :], in1=lng[0:nr, :],
                                                         op=ALU.mult), rd=[b_yy[yb], b_ln[0]], wr=[b_yy[yb]])
                    fw.op(DVE, lambda e: e.tensor_tensor(out=yy[0:nr, yb, :], in0=yy[0:nr, yb, :], in1=lnb[0:nr, :],
                                                         op=ALU.add), rd=[b_yy[yb], b_ln[1]], wr=[b_yy[yb]])
                    dst = yp[tok0 + t * 128: tok0 + (t + 1) * 128, :] if t < 8 else ys[0:2 * NS, :]
                    fw.dma(SP, dst, yy[0:nr, yb, :], rd=[b_yy[yb]])

                def mm_half(t, cg, first):
                    nr = 128 if t < 8 else 2 * NS
                    c0 = t * 128
                    bank = (t % 2) * 4 + cg
                    for ki, k in enumerate(range(8, 16) if first else range(8)):
                        if k < 8:
                            rhs, rb, nr_ = woA[:, k, cg * 512:(cg + 1) * 512], [b_woA[cg]], b_xT
                        else:
                            rhs, rb, nr_ = woB[:, k - 8, cg * 512:(cg + 1) * 512], [b_woB[k - 8]], ()
                        fw.op(PE, lambda e: e.matmul(py[0:nr, bank, :], lhsT=uT[:, k, c0:c0 + nr], rhs=rhs,
                                                     start=(first and ki == 0), stop=((not first) and ki == 7)),
                              rd=[b_uT[k][t]] + rb, wr=[b_py[bank]], inc=(ki == 7), note_r=nr_)

                for t_ in (0, 1):
                    for cg in range(4):
                        mm_half(t_, cg, True)

                for t in p2tiles:
                    nr = 128 if t < 8 else 2 * NS
                    c0 = t * 128
                    pb = t % 2
                    yb = t % 3
                    for cg in range(4):
                        bank = pb * 4 + cg
                        if t >= 2:
                            mm_half(t, cg, True)
                        mm_half(t, cg, False)
                        fw.op(DVE, lambda e: e.scalar_tensor_tensor(
                            out=yy[0:nr, yb, cg * 512:(cg + 1) * 512], in0=xr[0:nr, 0, cg * 512:(cg + 1) * 512],
                            scalar=ALPHA, in1=py[0:nr, bank, :], op0=ALU.mult, op1=ALU.add),
                            rd=[b_xr[0], b_py[bank]], wr=[b_yy[yb]])
                        fw.op(DVE, lambda e: e.bn_stats(out=stat[0:nr, yb, cg, :], in_=yy[0:nr, yb, cg * 512:(cg + 1) * 512]),
                              rd=[b_yy[yb]], wr=[b_stat[yb]])
                    if t + 1 in p2tiles:
                        xr_load(t + 1)
                    if t == 7 and pas == 0:
                        for t_ in range(3):
                            xT_tile(1, t_, pbanks=((0, 1) if t_ % 2 == 0 else (2, 3)), act_only=True)
                    fw.op(DVE, lambda e: e.bn_aggr(out=mv[0:nr, yb, 0:2],
                                                   in_=stat[0:nr, yb, :, :].rearrange("p a b -> p (a b)")),
                          rd=[b_stat[yb]], wr=[b_mv[yb]])
                    fw.op(ACT, lambda e: e.activation(out=mv[0:nr, yb, 2:3], in_=mv[0:nr, yb, 1:2], func=AF.Sqrt,
                                                      bias=epsb[0:nr, :]),
                          rd=[b_mv[yb], b_const], wr=[b_mv[yb]])
                    fw.op(DVE, lambda e: e.reciprocal(out=mv[0:nr, yb, 2:3], in_=mv[0:nr, yb, 2:3]),
                          rd=[b_mv[yb]], wr=[b_mv[yb]])
                    fw.op(DVE, lambda e: e.scalar_tensor_tensor(out=mv[0:nr, yb, 3:4], in0=mv[0:nr, yb, 0:1], scalar=-1.0,
                                                                in1=mv[0:nr, yb, 2:3], op0=ALU.mult, op1=ALU.mult),
                          rd=[b_mv[yb]], wr=[b_mv[yb]])
                    fw.op(ACT, lambda e: e.activation(out=yy[0:nr, yb, :], in_=yy[0:nr, yb, :], func=AF.Identity,
                                                      scale=mv[0:nr, yb, 2:3], bias=mv[0:nr, yb, 3:4]),
                          rd=[b_yy[yb], b_mv[yb]], wr=[b_yy[yb]])
                    if t >= 1:
                        tail(t - 1)
                tail(p2tiles[-1])
            st_wob.close()

        with ExitStack() as _st8:
            ost = _st8.enter_context(sb("ost", [12, 1024], F32))
            psof = PS[:, 0:2, :].rearrange("p a b -> p (a b)")
            b_ost = B(ost)
            for c in range(8):
                fw.op(PE, lambda e, c=c: e.transpose(psof[0:12, c * 128:(c + 1) * 128], outst[:, c, :], identf[:]),
                      rd=[b_outst[c], b_const], wr=b_ps[0:2], inc=(c == 7))
            fw.op(DVE, lambda e: e.tensor_copy(out=ost[:], in_=psof[0:12, :]),
                  rd=b_ps[0:2], wr=[b_ost])
            fw.dma(SP, hp, ost[0:1, :], rd=[b_ost])
            fw.dma(SP, cvp, ost[1:4, :], rd=[b_ost])
            for s in range(2):
                fw.dma(SP, hs[s:s + 1, :], ost[4 + 4 * s:5 + 4 * s, :], rd=[b_ost])
                fw.dma(SP, cvs[s], ost[5 + 4 * s:8 + 4 * s, :], rd=[b_ost])
            fw.barrier()
    return nc


def _alibi_tables():
    H = 16
    slopes = np.exp2(-8.0 * np.arange(1, H + 1, dtype=np.float64) / H)
    k = np.arange(128)[:, None]
    q = np.arange(128)[None, :]
    NEG = -200.0
    abp = np.zeros((128, 4, 2, 4, 128), np.float32)
    for kv in range(4):
        for g in range(4):
            s = slopes[kv * 4 + g]
            dA = (q + 128 - k).astype(np.float64)
            vA = (k // 64) >= (q // 64)
            abp[:, kv, 0, g, :] = np.where(vA, -s * dA, NEG)
            dB = np.abs(q - k).astype(np.float64)
            vB = (k // 64) <= (q // 64)
            abp[:, kv, 1, g, :] = np.where(vB, -s * dB, NEG)
    q2 = np.arange(32)[None, :]
    abs_ = np.zeros((128, 4, 2, 4, 32), np.float32)
    for kv in range(4):
        for g in range(4):
            s = slopes[kv * 4 + g]
            abs_[:, kv, 0, g, :] = -s * (q2 + 128 - k)
            abs_[:, kv, 1, g, :] = np.where(k < 32, -s * np.abs(q2 - k), NEG)
    return abp.reshape(128, 4096), abs_.reshape(128, 1024)


def _chunk_cols():
    cols = []
    for c in range(2):
        cols.append(np.arange(1024 + c * 128, 1024 + (c + 1) * 128))
    for j in range(2):
        for g in range(4):
            h0 = (2 * j) * 4 + g
            h1 = (2 * j + 1) * 4 + g
            cols.append(np.concatenate([np.arange(h0 * 64, h0 * 64 + 64), np.arange(h1 * 64, h1 * 64 + 64)]))
    for c in range(8):
        cols.append(np.arange(1536 + c * 128, 1536 + (c + 1) * 128))
    for c in range(8):
        cols.append(np.arange(3584 + c * 128, 3584 + (c + 1) * 128))
    for c in range(8):
        cols.append(np.arange(2560 + c * 128, 2560 + (c + 1) * 128))
    return cols


_NC_CACHE = {}


def kernel(x_prompt, x_sample, cache_k, cache_v, state_conv, state_h, w_in, b_in, conv_w, conv_b,
           w_gate_a, b_gate_a, w_gate_x, b_gate_x, lru_lambda, attn_sinks, w_out, ln_g, ln_b):
    f = lambda a: np.ascontiguousarray(np.asarray(a, dtype=np.float32))
    x_prompt, x_sample, cache_k, cache_v = f(x_prompt), f(x_sample), f(cache_k), f(cache_v)
    state_conv, state_h = f(state_conv), f(state_h)
    W = f(w_in)[0]; bi = f(b_in)[0]; Wo = f(w_out)[0]
    cols = _chunk_cols()
    win = np.empty((NCH, 128, 16, 128), np.float32)
    btab = np.empty((128, NCH), np.float32)
    for m, cc in enumerate(cols):
        win[m] = W[:, cc].reshape(16, 128, 128).transpose(1, 0, 2)
        btab[:, m] = bi[cc]
    win = win.reshape(NCH, 128, 2048)
    wkv = np.ascontiguousarray(W[:, 1024:1536].reshape(16, 128, 512).transpose(1, 0, 2)).reshape(128, 8192)
    wout = np.ascontiguousarray(Wo.reshape(16, 128, 2048).transpose(1, 0, 2)).reshape(128, 32768)
    bkv = np.ascontiguousarray(np.broadcast_to(bi[1024:1536], (128, 512)))
    cwt = np.ascontiguousarray(f(conv_w)[0].reshape(4, 8, 128).transpose(2, 1, 0)).reshape(128, 32)
    pc = lambda v: np.ascontiguousarray(f(v).reshape(8, 128).T)
    cbt, lamt = pc(f(conv_b)[0]), pc(f(lru_lambda)[0])
    bgat, bgxt = pc(f(b_gate_a)[0]), pc(f(b_gate_x)[0])
    wgt = np.zeros((128, 2, 8, 128), np.float32)
    for a, wsrc in enumerate((f(w_gate_a)[0], f(w_gate_x)[0])):
        for c in range(8):
            wgt[0:64, a, c, 0:64] = wsrc[2 * c]
            wgt[64:128, a, c, 64:128] = wsrc[2 * c + 1]
    wgt = wgt.reshape(128, 2048)
    lngt = np.ascontiguousarray(np.broadcast_to(f(ln_g)[0], (128, D)))
    lnbt = np.ascontiguousarray(np.broadcast_to(f(ln_b)[0], (128, D)))
    sinkt = np.ascontiguousarray(np.broadcast_to(f(attn_sinks)[0], (128, 16)))
    abp, abs_ = _alibi_tables()
    ident = np.eye(128, dtype=np.float32)

    if "nc" not in _NC_CACHE:
        _NC_CACHE["nc"] = build()
    nc = _NC_CACHE["nc"]

    in_maps = []
    for b in range(NCORES):
        sc = state_conv[0, 2 * b:2 * b + 2]
        sct = np.ascontiguousarray(sc.reshape(2, 3, 8, 128).transpose(3, 0, 2, 1)).reshape(128, 48)
        sht = np.ascontiguousarray(state_h[0, 2 * b:2 * b + 2].reshape(2, 8, 128).transpose(2, 0, 1)).reshape(128, 16)
        in_maps.append({
            "xp": x_prompt[b], "xs": np.ascontiguousarray(x_sample[2 * b:2 * b + 2].reshape(2 * NS, D)),
            "ck": np.ascontiguousarray(cache_k[0, 2 * b:2 * b + 2].reshape(2, 128, 256)),
            "cv": np.ascontiguousarray(cache_v[0, 2 * b:2 * b + 2].reshape(2, 128, 256)),
            "sconv": sct, "sh": sht, "win": win, "wkv": wkv, "wout": wout, "bt": btab, "bkv": bkv,
            "cw": cwt, "cb": cbt, "lam": lamt, "bga": bgat, "bgx": bgxt, "wg": wgt, "lng": lngt, "lnb": lnbt,
            "sinks": sinkt, "abp": abp, "abs": abs_, "ident": ident,
        })
    res = run_bass_kernel_spmd(nc, in_maps, core_ids=list(range(NCORES)))
    R = res.results
    y_p = np.stack([R[b]["yp"] for b in range(NCORES)])
    y_s = np.concatenate([R[b]["ys"].reshape(2, NS, D) for b in range(NCORES)])
    kwp = np.stack([R[b]["kwp"].reshape(128, 4, 64) for b in range(NCORES)])[None]
    vwp = np.stack([R[b]["vwp"].reshape(128, 4, 64) for b in range(NCORES)])[None]
    cvp = np.stack([R[b]["cvp"] for b in range(NCORES)])[None]
    hp = np.concatenate([R[b]["hp"] for b in range(NCORES)])[None]
    kws = np.concatenate([R[b]["kws"].reshape(2, 128, 4, 64) for b in range(NCORES)])[None]
    vws = np.concatenate([R[b]["vws"].reshape(2, 128, 4, 64) for b in range(NCORES)])[None]
    cvs = np.concatenate([R[b]["cvs"] for b in range(NCORES)])[None]
    hs = np.concatenate([R[b]["hs"] for b in range(NCORES)])[None]
    return (y_p, y_s, kwp, vwp, cvp, hp, kws, vws, cvs, hs)
```
